# Optimizing a Trainium2 kernel written in Bass

```python
import functools
import jax, jax.numpy as jnp
from jax import lax
import numpy as np

D_MODEL = 1024
BATCH = 16
SEQ = 2048
DEPTH = 1
DEC_BATCH = 32
DEC_SEQ = 4
PAST_LEN = 16384
PAGE_SIZE = 128

D_MIX = D_MODEL
D_CONV = D_MIX // 2
CONV_W = 3
N_HEADS = 8
HEAD_DIM = (D_MIX - D_CONV) // N_HEADS
N_KV = 2
N_REP = N_HEADS // N_KV
ROT_DIM = HEAD_DIM // 4
ROPE_THETA = 500000.0
N_IDX_HEADS = 8
D_IDX = 64
IDX_ROT = D_IDX // 4
K_TOP = 256
Q_BLOCK = 128
D_FF = 128 * ((8 * D_MODEL // 3 + 127) // 128)
EPS = 1e-6
IN_WIDTHS = (D_CONV, D_CONV, D_CONV, N_HEADS * HEAD_DIM, N_KV * HEAD_DIM, N_KV * HEAD_DIM,
             N_IDX_HEADS * D_IDX, D_IDX, N_IDX_HEADS)
N_IN = sum(IN_WIDTHS)

kernel_name = "hymba_conv_dsa_macaron_step"


def rmsnorm(x, g):
    xf = x.astype(jnp.float32)
    y = xf * lax.rsqrt(jnp.mean(xf * xf, axis=-1, keepdims=True) + EPS)
    return (y * g.astype(jnp.float32)).astype(x.dtype)


def swiglu(x, w_gate, w_up, w_down):
    return (jax.nn.silu(x @ w_gate) * (x @ w_up)) @ w_down


def partial_rope(x, pos, rot_dim):
    half = rot_dim // 2
    inv = jnp.power(jnp.float32(ROPE_THETA), -jnp.arange(half, dtype=jnp.float32) * (2.0 / rot_dim))
    ang = pos.astype(jnp.float32)[:, None] * inv[None, :]
    cos = jnp.cos(ang)[:, None, :]
    sin = jnp.sin(ang)[:, None, :]
    x1 = x[..., :half].astype(jnp.float32)
    x2 = x[..., half:rot_dim].astype(jnp.float32)
    rot = jnp.concatenate([x1 * cos - x2 * sin, x2 * cos + x1 * sin], axis=-1).astype(x.dtype)
    return jnp.concatenate([rot, x[..., rot_dim:]], axis=-1)


def short_conv(u_ext, w, b):
    T = u_ext.shape[1] - (CONV_W - 1)
    out = b
    for j in range(CONV_W):
        out = out + w[j] * u_ext[:, j:j + T]
    return out


def take_rows(a, idx):
    return jax.vmap(lambda ab, ib: ab[ib])(a, idx)


def indexer_scores(qi, wi, ki):
    s = jax.nn.relu(jnp.einsum('bqhd,bsd->bqhs', qi, ki).astype(jnp.float32))
    return jnp.einsum('bqhs,bqh->bqs', s, wi.astype(jnp.float32)) * (D_IDX ** -0.5 * N_IDX_HEADS ** -0.5)


def sparse_attention(q, scores, q_pos, gather_kv, k_sel):
    n_keys = scores.shape[-1]
    key_pos = jnp.arange(n_keys, dtype=jnp.int32)
    scores = jnp.where(key_pos[None, None, :] <= q_pos[None, :, None], scores, -jnp.inf)
    _, idx = lax.top_k(scores, k_sel)
    valid = idx <= q_pos[None, :, None]
    kg, vg = gather_kv(idx)
    B, Q = q.shape[:2]
    qg = q.reshape(B, Q, N_KV, N_REP, HEAD_DIM)
    logits = jnp.einsum('bqgrd,bqkgd->bqgrk', qg, kg).astype(jnp.float32) * (HEAD_DIM ** -0.5)
    logits = jnp.where(valid[:, :, None, None, :], logits, -jnp.inf)
    p = jax.nn.softmax(logits, axis=-1)
    out = jnp.einsum('bqgrk,bqkgd->bqgrd', p.astype(vg.dtype), vg)
    return out.reshape(B, Q, N_HEADS, HEAD_DIM)


def attend_prompt(q, k, v, qi, ki, wi, k_sel):
    B, T = q.shape[:2]
    n_blk = T // Q_BLOCK
    gather_kv = lambda idx: (take_rows(k, idx), take_rows(v, idx))

    def block(j):
        t0 = j * Q_BLOCK
        sl = lambda a: lax.dynamic_slice_in_dim(a, t0, Q_BLOCK, axis=1)
        q_pos = t0 + jnp.arange(Q_BLOCK, dtype=jnp.int32)
        sc = indexer_scores(sl(qi), sl(wi), ki)
        return sparse_attention(sl(q), sc, q_pos, gather_kv, k_sel)

    out = lax.map(block, jnp.arange(n_blk, dtype=jnp.int32))
    return jnp.moveaxis(out, 0, 1).reshape(B, T, N_HEADS, HEAD_DIM)


def attend_paged(q, k, v, qi, ki, wi, pool_k, pool_v, pool_idx_k, page_table, k_sel):
    Bd, T = q.shape[:2]
    past_len = page_table.shape[1] * PAGE_SIZE
    past_ki = pool_idx_k[page_table].reshape(Bd, past_len, D_IDX).astype(ki.dtype)
    ki_all = jnp.concatenate([past_ki, ki], axis=1)
    q_pos = past_len + jnp.arange(T, dtype=jnp.int32)
    sc = indexer_scores(qi, wi, ki_all)

    def gather_kv(idx):
        is_past = (idx < past_len)[..., None, None]
        ic = jnp.minimum(idx, past_len - 1)
        page = jax.vmap(lambda pt, i: pt[i])(page_table, ic // PAGE_SIZE)
        off = ic % PAGE_SIZE
        inew = jnp.clip(idx - past_len, 0, T - 1)
        kg = jnp.where(is_past, pool_k[page, off].astype(k.dtype), take_rows(k, inew))
        vg = jnp.where(is_past, pool_v[page, off].astype(v.dtype), take_rows(v, inew))
        return kg, vg

    return sparse_attention(q, sc, q_pos, gather_kv, k_sel)


def decoder_layer(x, pos, conv_prev, attend, ffn1_norm, ffn1_w_gate, ffn1_w_up, ffn1_w_down,
                  mix_norm, w_in, conv_w, conv_b, q_norm, k_norm, w_out,
                  ffn2_norm, ffn2_w_gate, ffn2_w_up, ffn2_w_down):
    B, T, _ = x.shape
    x = x + 0.5 * swiglu(rmsnorm(x, ffn1_norm), ffn1_w_gate, ffn1_w_up, ffn1_w_down)
    h = rmsnorm(x, mix_norm)
    split_at = [int(s) for s in np.cumsum(IN_WIDTHS)[:-1]]
    gb, gc, hc, q, k, v, qi, ki, wi = jnp.split(h @ w_in, split_at, axis=-1)
    u = gc * hc
    u_ext = jnp.concatenate([conv_prev.astype(u.dtype), u], axis=1)
    y_conv = gb * short_conv(u_ext, conv_w, conv_b)
    conv_state = u_ext[:, -(CONV_W - 1):]
    q = partial_rope(rmsnorm(q.reshape(B, T, N_HEADS, HEAD_DIM), q_norm), pos, ROT_DIM)
    k = partial_rope(rmsnorm(k.reshape(B, T, N_KV, HEAD_DIM), k_norm), pos, ROT_DIM)
    v = v.reshape(B, T, N_KV, HEAD_DIM)
    qi = partial_rope(qi.reshape(B, T, N_IDX_HEADS, D_IDX), pos, IDX_ROT)
    ki = partial_rope(ki[:, :, None, :], pos, IDX_ROT)[:, :, 0]
    attn = attend(q, k, v, qi, ki, wi)
    x = x + jnp.concatenate([y_conv, attn.reshape(B, T, N_HEADS * HEAD_DIM)], axis=-1) @ w_out
    x = x + 0.5 * swiglu(rmsnorm(x, ffn2_norm), ffn2_w_gate, ffn2_w_up, ffn2_w_down)
    return x, k, v, ki, conv_state


def setup_inputs(seed: int = 0) -> dict:
    key = jax.random.key(seed)
    ks = jax.random.split(key, 24)
    n_pages = PAST_LEN // PAGE_SIZE
    n_pool = (DEC_BATCH * n_pages * 5) // 4
    nrm = lambda k, shape, s=1.0: jax.random.normal(k, shape, jnp.float32) * s
    gain = lambda k, shape: 1.0 + 0.01 * jax.random.normal(k, shape, jnp.float32)
    page_table = jax.random.permutation(ks[6], n_pool)[:DEC_BATCH * n_pages].reshape(DEC_BATCH, n_pages).astype(jnp.int32)
    return {
        "x_prompt": nrm(ks[0], (BATCH, SEQ, D_MODEL)),
        "x_sample": nrm(ks[1], (DEC_BATCH, DEC_SEQ, D_MODEL)),
        "cache_k": nrm(ks[2], (DEPTH, n_pool, PAGE_SIZE, N_KV, HEAD_DIM)),
        "cache_v": nrm(ks[3], (DEPTH, n_pool, PAGE_SIZE, N_KV, HEAD_DIM)),
        "cache_idx_k": nrm(ks[4], (DEPTH, n_pool, PAGE_SIZE, D_IDX)),
        "cache_conv": nrm(ks[5], (DEPTH, DEC_BATCH, CONV_W - 1, D_CONV)),
        "page_table": page_table,
        "ffn1_norm": gain(ks[7], (DEPTH, D_MODEL)),
        "ffn1_w_gate": nrm(ks[8], (DEPTH, D_MODEL, D_FF), D_MODEL ** -0.5),
        "ffn1_w_up": nrm(ks[9], (DEPTH, D_MODEL, D_FF), D_MODEL ** -0.5),
        "ffn1_w_down": nrm(ks[10], (DEPTH, D_FF, D_MODEL), D_FF ** -0.5),
        "mix_norm": gain(ks[11], (DEPTH, D_MODEL)),
        "w_in": nrm(ks[12], (DEPTH, D_MODEL, N_IN), D_MODEL ** -0.5),
        "conv_w": nrm(ks[13], (DEPTH, CONV_W, D_CONV), CONV_W ** -0.5),
        "conv_b": nrm(ks[14], (DEPTH, D_CONV), 0.01),
        "q_norm": gain(ks[15], (DEPTH, HEAD_DIM)),
        "k_norm": gain(ks[16], (DEPTH, HEAD_DIM)),
        "w_out": nrm(ks[17], (DEPTH, D_MIX, D_MODEL), D_MIX ** -0.5),
        "ffn2_norm": gain(ks[18], (DEPTH, D_MODEL)),
        "ffn2_w_gate": nrm(ks[19], (DEPTH, D_MODEL, D_FF), D_MODEL ** -0.5),
        "ffn2_w_up": nrm(ks[20], (DEPTH, D_MODEL, D_FF), D_MODEL ** -0.5),
        "ffn2_w_down": nrm(ks[21], (DEPTH, D_FF, D_MODEL), D_FF ** -0.5),
    }


def reference(x_prompt, x_sample, cache_k, cache_v, cache_idx_k, cache_conv, page_table,
              ffn1_norm, ffn1_w_gate, ffn1_w_up, ffn1_w_down, mix_norm, w_in, conv_w, conv_b,
              q_norm, k_norm, w_out, ffn2_norm, ffn2_w_gate, ffn2_w_up, ffn2_w_down):
    past_len = page_table.shape[1] * PAGE_SIZE
    pos_p = jnp.arange(SEQ, dtype=jnp.int32)
    pos_s = past_len + jnp.arange(DEC_SEQ, dtype=jnp.int32)
    k_sel_p = min(K_TOP, SEQ // 4)
    k_sel_s = min(K_TOP, (past_len + DEC_SEQ) // 4)
    yp, ys = x_prompt, x_sample
    kp_l, vp_l, ip_l, cp_l, ks_l, vs_l, is_l, cs_l = [], [], [], [], [], [], [], []
    for l in range(DEPTH):
        w = (ffn1_norm[l], ffn1_w_gate[l], ffn1_w_up[l], ffn1_w_down[l], mix_norm[l], w_in[l],
             conv_w[l], conv_b[l], q_norm[l], k_norm[l], w_out[l],
             ffn2_norm[l], ffn2_w_gate[l], ffn2_w_up[l], ffn2_w_down[l])
        conv_zero = jnp.zeros((yp.shape[0], CONV_W - 1, D_CONV), yp.dtype)
        att_p = functools.partial(attend_prompt, k_sel=k_sel_p)
        att_s = functools.partial(attend_paged, pool_k=cache_k[l], pool_v=cache_v[l],
                                  pool_idx_k=cache_idx_k[l], page_table=page_table, k_sel=k_sel_s)
        yp, kp, vp, ip, cp = decoder_layer(yp, pos_p, conv_zero, att_p, *w)
        ys, kss, vss, iss, css = decoder_layer(ys, pos_s, cache_conv[l], att_s, *w)
        kp_l.append(kp); vp_l.append(vp); ip_l.append(ip); cp_l.append(cp)
        ks_l.append(kss); vs_l.append(vss); is_l.append(iss); cs_l.append(css)
    k_prompt = jnp.stack(kp_l); v_prompt = jnp.stack(vp_l)
    idxk_prompt = jnp.stack(ip_l); conv_prompt = jnp.stack(cp_l)
    k_sample = jnp.stack(ks_l); v_sample = jnp.stack(vs_l)
    idxk_sample = jnp.stack(is_l); conv_sample = jnp.stack(cs_l)
    return (yp, ys, k_prompt, v_prompt, idxk_prompt, conv_prompt, k_sample, v_sample, idxk_sample, conv_sample)
```

```python
import os
import numpy as np
import ml_dtypes
from contextlib import ExitStack
import concourse.bass as bass
import concourse.mybir as mybir
from concourse.bass_utils import run_bass_kernel_spmd

F32 = mybir.dt.float32
BF16 = mybir.dt.bfloat16
I32 = mybir.dt.int32
AF = mybir.ActivationFunctionType
ALU = mybir.AluOpType
AX = mybir.AxisListType

NCORES = 8
D = 1024
DFF = 2816
SEQ = 2048
NPOOL = int(os.environ.get('K_NPOOL', '5120'))
PARTS = int(os.environ.get('K_PARTS', '99'))
SUB = int(os.environ.get('K_SUB', '99'))
EPS = 1e-6
NEG = -1.0e30
K_TOP = 256
N_BISECT = 16


class Buf:
    __slots__ = ("name", "w", "r")

    def __init__(self, name):
        self.name = name
        self.w = None
        self.r = []


class Op:
    __slots__ = ("eng", "idx", "fn", "dma", "waits", "lane", "lane_val", "signal", "rank")

    def __init__(self, eng, idx, fn, dma):
        self.eng = eng
        self.idx = idx
        self.fn = fn
        self.dma = dma
        self.waits = []
        self.lane = None
        self.lane_val = None
        self.signal = False
        self.rank = None


class Sched:
    ENGS = ("pe", "act", "dve", "pool", "sp")
    NLANES = {"sp": 12, "act": 8, "pool": 6}

    def __init__(self):
        self.ops = {e: [] for e in self.ENGS}
        self.maxw = {e: {f: -1 for f in self.ENGS} for e in self.ENGS}
        self.lane_last = {e: [None] * n for e, n in self.NLANES.items()}
        self.lane_cnt = {e: [0] * n for e, n in self.NLANES.items()}
        self.lane_rr = {e: 0 for e in self.NLANES}
        self.lane_waited = {e: {} for e in self.ENGS}

    def _add_wait(self, op, d):
        eng = op.eng
        if d.dma:
            key = (d.eng, d.lane)
            if self.lane_waited[eng].get(key, 0) >= d.lane_val:
                return
            self.lane_waited[eng][key] = d.lane_val
            op.waits.append(d)
        else:
            if self.maxw[eng][d.eng] >= d.idx:
                return
            self.maxw[eng][d.eng] = d.idx
            d.signal = True
            op.waits.append(d)

    def add(self, eng, fn, reads=(), writes=(), dma=False):
        op = Op(eng, len(self.ops[eng]), fn, dma)
        raw, other = [], []
        for b in reads:
            if b.w is not None:
                raw.append(b.w)
        for b in writes:
            if b.w is not None:
                other.append(b.w)
            other.extend(b.r)
        need = []
        for d in raw:
            if d.dma or dma or d.eng != eng or eng != "pe":
                need.append(d)
        for d in other:
            if d is op:
                continue
            if d.dma or dma or d.eng != eng:
                need.append(d)
        best = {}
        for d in need:
            key = (d.eng, d.lane) if d.dma else (d.eng, None)
            cur = best.get(key)
            val = d.lane_val if d.dma else d.idx
            if cur is None or val > (cur.lane_val if cur.dma else cur.idx):
                best[key] = d
        for d in best.values():
            self._add_wait(op, d)
        if dma:
            n = self.NLANES[eng]
            l = self.lane_rr[eng]
            self.lane_rr[eng] = (l + 1) % n
            prev = self.lane_last[eng][l]
            if prev is not None:
                self._add_wait(op, prev)
            self.lane_cnt[eng][l] += 16
            op.lane = l
            op.lane_val = self.lane_cnt[eng][l]
            self.lane_last[eng][l] = op
        for b in reads:
            b.r.append(op)
        for b in writes:
            b.w = op
            b.r = []
        self.ops[eng].append(op)
        return op

    def emit(self, nc, es):
        sems = {}
        for e in ("pe", "act", "dve", "pool"):
            sems[e] = es.enter_context(nc.semaphore("c_" + e))
        lsems = {}
        for e, n in self.NLANES.items():
            for l in range(n):
                lsems[(e, l)] = es.enter_context(nc.semaphore("l_%s%d" % (e, l)))
        for e in ("pe", "act", "dve", "pool"):
            k = 0
            for op in self.ops[e]:
                if (not op.dma) and op.signal:
                    k += 1
                    op.rank = k
        all_dma = [op for e in self.ENGS for op in self.ops[e] if op.dma]

        def run(engname, h):
            for op in self.ops[engname]:
                for d in op.waits:
                    if d.dma:
                        h.wait_ge(lsems[(d.eng, d.lane)], d.lane_val)
                    else:
                        h.wait_ge(sems[d.eng], d.rank)
                ins = op.fn(h)
                if op.dma:
                    ins.then_inc(lsems[(engname, op.lane)], 16)
                elif op.signal:
                    ins.then_inc(sems[engname], 1)
            if engname == "sp":
                for (e, l), s in lsems.items():
                    v = self.lane_cnt[e][l]
                    if v > 0:
                        h.wait_ge(s, v)

        with nc.Block() as block:
            @block.tensor
            def _(h):
                run("pe", h)

            @block.scalar
            def _(h):
                run("act", h)

            @block.vector
            def _(h):
                run("dve", h)

            @block.gpsimd
            def _(h):
                run("pool", h)

            @block.sync
            def _(h):
                run("sp", h)


class PsumAlloc:
    def __init__(self, banks):
        self.banks = banks
        self.busy = [False] * len(banks)
        self.rr = 0

    def get(self):
        n = len(self.banks)
        for k in range(n):
            i = (self.rr + k) % n
            if not self.busy[i]:
                self.busy[i] = True
                self.rr = (i + 1) % n
                return i
        raise RuntimeError("PSUM banks exhausted")

    def rel(self, i):
        self.busy[i] = False


def build_program(stage=3):
    nc = bass.Bass("TRN2", target_bir_lowering=False)
    S = Sched()
    es = ExitStack()

    def din(name, shape, dt=F32):
        return nc.dram_tensor(name, list(shape), dt, kind="ExternalInput").ap()

    def dout(name, shape, dt=F32):
        return nc.dram_tensor(name, list(shape), dt, kind="ExternalOutput").ap()

    xp = din("xp", [2 * SEQ, D])
    xs = din("xs", [16, D])
    npool = NPOOL if stage >= 3 else 2
    ck = din("ck", [npool, 128 * 128])
    cv = din("cv", [npool, 128 * 128])
    cik = din("cik", [npool, 128 * 64])
    cconv = din("cconv", [128, 4, 4, 2])
    ptT = din("ptT", [128, 4], I32)
    wf1 = din("wf1", [33, 128, 2048])
    wf2 = din("wf2", [33, 128, 2048])
    win = din("win", [12, 128, 2048])
    wout = din("wout", [4, 128, 2048])
    gains = din("gains", [128, 24])
    convp = din("convp", [128, 16])
    qkg = din("qkg", [128, 128])
    identb_d = din("identb", [128, 128], BF16)
    identf_d = din("identf", [128, 128])
    cm_d = din("cm", [128, 128])
    e_d = din("emat", [128, 1024], BF16)
    ropep = din("ropep", [128, 2, 16, 8])
    ropes = din("ropes", [16, 2, 8])
    cms_d = din("cms", [128, 4, 4])
    iota_d = din("iotap", [128, 1])
    cpow_d = din("cpow", [128, 32])

    yp = dout("yp", [2 * SEQ, D])
    ys = dout("ys", [16, D])
    kp = dout("kp", [2 * SEQ, 128])
    vp = dout("vp", [2 * SEQ, 128])
    ip = dout("ip", [2 * SEQ, 64])
    cpo = dout("cpo", [2, 128, 4, 2])
    kso = dout("kso", [16, 128])
    vso = dout("vso", [16, 128])
    iso = dout("iso", [16, 64])
    cso = dout("cso", [128, 4, 4, 2])

    wscr = {}
    for nm, npieces in (("wf1", 33), ("win", 12), ("wf2", 33)):
        t_ = nc.dram_tensor(nm + "_bf", [npieces, 128, 2048], BF16, kind="Internal").ap()
        wscr[nm] = (t_, [Buf("%s_bf%d" % (nm, i)) for i in range(npieces)])

    def sb(name, shape, dt=F32):
        t = es.enter_context(nc.sbuf_tensor(name, list(shape), dt))
        return t, Buf(name)

    X, bX = sb("X", [128, 4, 1024])
    bXs = [Buf("Xs%d" % i) for i in range(4)]
    XNT, bXNT = sb("XNT", [128, 8, 512], BF16)
    XSC, bXSC = sb("XSC", [128, 1024], BF16)
    SQJ, bSQJ = sb("SQJ", [128, 1024], BF16)
    SS, bSS = sb("SS", [128, 8])
    RS, bRS = sb("RS", [128, 8])
    NSTG = 2
    STG = [sb("STG%d" % i, [128, 2048]) for i in range(NSTG)]
    STGG = [Buf("STGG%d" % i) for i in range(NSTG)]
    STGH = [Buf("STGH%d" % i) for i in range(NSTG)]
    NWR = 4
    WR = [sb("WR%d" % i, [128, 2048], BF16) for i in range(NWR)]
    NWD = 4
    WD = [sb("WD%d" % i, [128, 2048], BF16) for i in range(NWD)]
    HT = [sb("HT%d" % i, [128, 512], BF16) for i in range(8)]
    ST = [sb("ST%d" % i, [128, 512]) for i in range(2)]
    WOUT, bWOUT = sb("WOUT", [128, 8, 1024], BF16)
    KT, bKT = sb("KT", [128, 2048], BF16)
    VE, bVE = sb("VE", [128, 16, 2, 65], BF16)
    KIT, bKIT = sb("KIT", [128, 2048], BF16)
    QT, bQT = sb("QT", [128, 4, 4, 128], BF16)
    QIT, bQIT = sb("QIT", [128, 4, 512], BF16)
    WI, bWI = sb("WI", [128, 4, 8])
    KVF, bKVF = sb("KVF", [128, 256])
    HC, bHC = sb("HC", [128, 512])
    UEXT, bUEXT = sb("UEXT", [128, 4, 520])
    ACC, bACC = sb("ACC", [128, 512])
    YCT, bYCT = sb("YCT", [128, 4, 512], BF16)
    GN, bGN = sb("GN", [128, 24])
    CVP, bCVP = sb("CVP", [128, 16])
    QKG, bQKG = sb("QKG", [128, 128])
    IDB, bIDB = sb("IDB", [128, 128], BF16)
    IDF, bIDF = sb("IDF", [128, 128])
    CM, bCM = sb("CM", [128, 128])
    EM, bEM = sb("EM", [128, 8, 128], BF16)
    ROPE, bROPE = sb("ROPE", [128, 2, 16, 8])
    ROPS, bROPS = sb("ROPS", [16, 2, 8])
    QIBD2 = [sb("QIBD%d" % i, [128, 8, 128], BF16) for i in range(2)]
    WIREP, bWIREP = sb("WIREP", [128, 128])
    WSEL2 = [sb("WSEL%d" % i, [128, 8, 128], BF16) for i in range(2)]
    DEN, bDEN = sb("DEN", [128, 8])
    MBIAS, bMB = sb("MBIAS", [128, 1])
    NEG1, bNEG1 = sb("NEG1", [128, 8])
    RR = [sb("RR%d" % i, [128, 512], BF16) for i in range(3)]
    ISB, bISB = sb("ISB", [128, 2048])
    JUNK, bJUNK = sb("JUNK", [128, 2048], BF16)
    MASKB, bMASKB = sb("MASKB", [128, 2048], BF16)
    MASKT, bMASKT = sb("MASKT", [128, 16, 128], BF16)
    PEXP = [sb("PEXP%d" % i, [128, 512], BF16) for i in range(2)]
    PTB = [sb("PTB%d" % i, [128, 512], BF16) for i in range(2)]
    OB, bOB = sb("OB", [128, 4, 2, 64], BF16)
    OT, bOT = sb("OT", [128, 4, 128], BF16)
    BS, bBS = sb("BS", [128, 16])
    HW, bHW = sb("HW", [128, 32])
    CPOW, bCPOW = sb("CPOW", [128, 32])
    MIDP, bMIDP = sb("MIDP", [128, 2])
    RDEN, bRDEN = sb("RDEN", [128, 8])

    banks = []
    for i in range(8):
        t = es.enter_context(nc.psum_tensor("PS%d" % i, [128, 512], F32))
        banks.append((t, Buf("PS%d" % i)))
    PA = PsumAlloc(banks)

    def pf(i):
        return banks[i][0]

    def pb(i):
        return banks[i][1]

    def pbf(i):
        return banks[i][0][:, :].bitcast(BF16)

    def dma(out, in_, reads, writes, eng="sp", **kw):
        return S.add(eng, lambda h: h.dma_start(out=out, in_=in_, **kw), reads, writes, dma=True)

    def act(out, in_, func, reads, writes, **kw):
        return S.add("act", lambda h: h.activation(out=out, in_=in_, func=func, **kw), reads, writes)

    def mm(out, lhsT, rhs, start, stop, reads, writes):
        return S.add("pe", lambda h: h.matmul(out, lhsT, rhs, start=start, stop=stop,
                                               skip_group_check=True), reads, writes)

    def tr(out, in_, ident, reads, writes):
        return S.add("pe", lambda h: h.transpose(out, in_, ident), reads, writes)

    def V(eng, name, reads, writes, *a, **kw):
        return S.add(eng, lambda h: getattr(h, name)(*a, **kw), reads, writes)

    dma(GN[:, :], gains, [], [bGN])
    dma(CVP[:, :], convp, [], [bCVP])
    dma(QKG[:, :], qkg, [], [bQKG])
    dma(IDB[:, :], identb_d, [], [bIDB])
    dma(IDF[:, :], identf_d, [], [bIDF])
    dma(CM[:, :], cm_d, [], [bCM])
    dma(EM[:, :, :], e_d.rearrange("p (g t) -> p g t", g=8), [], [bEM])
    dma(ROPE[:, :, :, :], ropep, [], [bROPE])
    dma(ROPS[:, :, :], ropes, [], [bROPS])
    dma(CPOW[:, :], cpow_d, [], [bCPOW])
    V("pool", "memset", [], [bVE], VE[:, :, :, :], 1.0)
    for i_ in range(2):
        V("pool", "memset", [], [QIBD2[i_][1]], QIBD2[i_][0][:, :, :], 0.0)
    V("pool", "memset", [], [bNEG1], NEG1[:, :], -1.0)
    V("pool", "memset", [], [bMB], MBIAS[:, :], -60000.0)
    V("pool", "memset", [], [bUEXT], UEXT[:, :, :], 0.0)

    stg_rr = [0]
    cast_rr = [0]

    cur_blk = [0]

    def load_piece(dram_piece, dst, dstbuf, gain_col=None, scr=None):
        if scr is not None and cur_blk[0] > 0:
            nm, pi = scr
            dma(dst, wscr[nm][0][pi], [wscr[nm][1][pi]], [dstbuf])
            return
        i = stg_rr[0]
        stg_rr[0] = (i + 1) % NSTG
        st, bst = STG[i]
        dma(st[:, :], dram_piece, [], [bst, STGG[i], STGH[i]])
        cast_rr[0] += 1
        if gain_col is None:
            V("dve", "tensor_copy", [bst, STGG[i], STGH[i]], [dstbuf], dst[:, :], st[:, :])
        else:
            g = GN[:, gain_col:gain_col + 8]
            V("dve" if cast_rr[0] % 3 != 0 else "pool", "tensor_tensor", [bst, STGG[i], STGH[i], bGN], [dstbuf],
              dst[:, :].rearrange("p (k n) -> p k n", k=8),
              st[:, :].rearrange("p (k n) -> p k n", k=8),
              g.unsqueeze(2).to_broadcast([128, 8, 256]), ALU.mult)
        if scr is not None:
            nm, pi = scr
            dma(wscr[nm][0][pi], dst, [dstbuf], [wscr[nm][1][pi]], eng="act")

    for i in range(4):
        load_piece(wout[i], WOUT[:, 2 * i:2 * i + 2, :].rearrange("p a n -> p (a n)"), bWOUT)

    wr_rr = [0]

    def next_wr():
        i = wr_rr[0]
        wr_rr[0] = (i + 1) % NWR
        return WR[i]

    SSN, bSSN = sb("SSN", [128, 4])
    RSN, bRSN = sb("RSN", [128, 4])

    def norm_block(nsub, pr):
        junk = ST[0][0][:, :].bitcast(BF16)
        bjunk = ST[0][1]
        for s in range(nsub):
            act(junk[:pr, :], X[:pr, s, :], AF.Square, [bXs[s]], [bjunk, bSSN], accum_out=SSN[:pr, s:s + 1])
        act(RSN[:pr, 0:nsub], SSN[:pr, 0:nsub], AF.Sqrt, [bSSN, bEPSB], [bRSN], scale=1.0 / D, bias=EPSB[:pr, 0:1])
        V("dve", "reciprocal", [bRSN], [bRSN], RSN[:pr, 0:nsub], RSN[:pr, 0:nsub])
        for s in range(nsub):
            xs_, bxs = (XSC, bXSC) if s % 2 == 0 else (SQJ, bSQJ)
            act(xs_[:pr, :], X[:pr, s, :], AF.Copy, [bXs[s], bRSN], [bxs], scale=RSN[:pr, s:s + 1])
            b = PA.get()
            for kc in range(8):
                tr(pbf(b)[:, kc * 128:kc * 128 + pr], xs_[:pr, kc * 128:(kc + 1) * 128], IDB[:pr, :pr],
                   [bxs, bIDB], [pb(b)])
            V("dve", "tensor_copy", [pb(b)], [bXNT], XNT[:, :, s * 128:s * 128 + pr],
              pbf(b).rearrange("p (k t) -> p k t", k=8)[:, :, 0:pr])
            PA.rel(b)

    EPSB, bEPSB = sb("EPSB", [128, 1])
    V("pool", "memset", [], [bEPSB], EPSB[:, :], EPS)

    THIRDS = [(0, 4), (4, 8), (8, 11)]

    def ffn_block(wf, gcol, nsub, pr, nb, wname, after_sub=None):
        for (g0, g1) in THIRDS:
            nch = (g1 - g0) * 2
            wds = []
            for gi in range(g0, g1):
                wg, bwg = next_wr()
                load_piece(wf[3 * gi], wg[:, :], bwg, gcol, (wname, 3 * gi))
                wu, bwu = next_wr()
                load_piece(wf[3 * gi + 1], wu[:, :], bwu, gcol, (wname, 3 * gi + 1))
                wd, bwd = WD[gi - g0]
                load_piece(wf[3 * gi + 2], wd[:, :], bwd, None, (wname, 3 * gi + 2))
                wds.append((wd, bwd))
                wg3 = wg[:, :].rearrange("p (k n) -> p k n", k=8)
                wu3 = wu[:, :].rearrange("p (k n) -> p k n", k=8)
                for c2 in range(2):
                    ci = (gi - g0) * 2 + c2
                    pg = PA.get()
                    pu = PA.get()
                    for kc in range(8):
                        mm(pf(pg)[:, :nb], wg3[:, kc, c2 * 128:(c2 + 1) * 128], XNT[:, kc, :nb],
                           kc == 0, kc == 7, [bwg, bXNT], [pb(pg)])
                    for kc in range(8):
                        mm(pf(pu)[:, :nb], wu3[:, kc, c2 * 128:(c2 + 1) * 128], XNT[:, kc, :nb],
                           kc == 0, kc == 7, [bwu, bXNT], [pb(pu)])
                    st, bst = ST[ci % 2]
                    act(st[:, :nb], pf(pg)[:, :nb], AF.Silu, [pb(pg)], [bst])
                    ht, bht = HT[ci]
                    V("dve", "tensor_tensor", [bst, pb(pu)], [bht], ht[:, :nb], st[:, :nb], pf(pu)[:, :nb],
                      ALU.mult)
                    PA.rel(pg)
                    PA.rel(pu)
            for s in range(nsub):
                for hh in range(2):
                    pd = PA.get()
                    for ci in range(nch):
                        wd, bwd = wds[ci // 2]
                        wd3 = wd[:, :].rearrange("p (c n) -> p c n", c=2)
                        ht, bht = HT[ci]
                        mm(pf(pd)[:pr, :], ht[:, s * 128:s * 128 + pr], wd3[:, ci % 2, hh * 512:(hh + 1) * 512],
                           ci == 0, ci == nch - 1, [bht, bwd], [pb(pd)])
                    V("dve", "scalar_tensor_tensor", [pb(pd), bXs[s]], [bXs[s]],
                      X[:pr, s, hh * 512:(hh + 1) * 512], pf(pd)[:pr, :], 0.5,
                      X[:pr, s, hh * 512:(hh + 1) * 512], ALU.mult, ALU.add)
                    PA.rel(pd)
                if after_sub is not None and (g0, g1) == THIRDS[-1]:
                    after_sub(s)

    class TS:
        pass

    TSETS = []
    for par in range(3):
        t_ = TS()
        t_.SSS16, t_.bSSS16 = sb("SSS16_%d" % par, [128, 16])
        t_.RSS16, t_.bRSS16 = sb("RSS16_%d" % par, [128, 16])
        TSETS.append(t_)

    def mix_in_block(blk, nsub, pr, nb, is_sample):
        q = blk // 4
        bi = blk % 4
        t0 = blk * 512
        s0 = bi * 512
        norm_block(nsub, pr)
        chunk_ps = {}
        slots = {}
        for j in range(6):
            w, bw = next_wr()
            load_piece(win[j], w[:, :], bw, 8, ("win", j))
            slots[j] = (w, bw)
            w3 = w[:, :].rearrange("p (k n) -> p k n", k=8)
            for c2 in range(2):
                cc = 2 * j + c2
                i, typ = cc // 3, cc % 3
                b = PA.get()
                for kc in range(8):
                    mm(pf(b)[:, :nb], w3[:, kc, c2 * 128:(c2 + 1) * 128], XNT[:, kc, :nb], kc == 0, kc == 7,
                       [bw, bXNT], [pb(b)])
                chunk_ps[(i, typ)] = b
                if typ == 2:
                    bgb, bgc, bhc = chunk_ps[(i, 0)], chunk_ps[(i, 1)], chunk_ps[(i, 2)]
                    act(HC[:, :nb], pf(bhc)[:, :nb], AF.Copy, [pb(bhc)], [bHC])
                    if not is_sample:
                        ue = UEXT[:, i, 2:2 + nb]
                        V("dve", "tensor_tensor", [pb(bgc), bHC], [bUEXT], ue, pf(bgc)[:, :nb], HC[:, :nb], ALU.mult)
                        u0 = UEXT[:, i, 0:nb]
                        u1 = UEXT[:, i, 1:1 + nb]
                        u2 = UEXT[:, i, 2:2 + nb]
                        acc = ACC[:, :nb]
                        gbp = pf(bgb)[:, :nb]
                        yo = YCT[:, i, :nb]
                    else:
                        ue4 = UEXT[:, i, 0:24].rearrange("p (b t) -> p b t", t=6)
                        V("dve", "tensor_tensor", [pb(bgc), bHC], [bUEXT], ue4[:, :, 2:6],
                          pf(bgc)[:, :nb].rearrange("p (b t) -> p b t", t=4),
                          HC[:, :nb].rearrange("p (b t) -> p b t", t=4), ALU.mult)
                        u0, u1, u2 = ue4[:, :, 0:4], ue4[:, :, 1:5], ue4[:, :, 2:6]
                        acc = ACC[:, :nb].rearrange("p (b t) -> p b t", t=4)
                        gbp = pf(bgb)[:, :nb].rearrange("p (b t) -> p b t", t=4)
                        yo = YCT[:, i, :nb].rearrange("p (b t) -> p b t", t=4)
                    V("dve", "tensor_scalar", [bUEXT, bCVP], [bACC], acc, u2, CVP[:, i * 4 + 2:i * 4 + 3],
                      CVP[:, i * 4 + 3:i * 4 + 4], ALU.mult, ALU.add)
                    V("dve", "scalar_tensor_tensor", [bUEXT, bACC, bCVP], [bACC], acc, u1,
                      CVP[:, i * 4 + 1:i * 4 + 2], acc, ALU.mult, ALU.add)
                    V("dve", "scalar_tensor_tensor", [bUEXT, bACC, bCVP], [bACC], acc, u0,
                      CVP[:, i * 4 + 0:i * 4 + 1], acc, ALU.mult, ALU.add)
                    V("dve", "tensor_tensor", [pb(bgb), bACC], [bYCT], yo, gbp, acc, ALU.mult)
                    for bb in (bgb, bgc, bhc):
                        PA.rel(bb)
                    if not is_sample:
                        if bi == 3:
                            dma(cpo[q, :, i, :], UEXT[:, i, 512:514], [bUEXT], [], eng="act")
                            V("pool", "memset", [bUEXT], [bUEXT], UEXT[:, i, 0:2], 0.0)
                        else:
                            V("pool", "tensor_copy", [bUEXT], [bUEXT], UEXT[:, i, 0:2], UEXT[:, i, 512:514])
                    else:
                        dma(cso[:, i, :, :], ue4[:, :, 4:6], [bUEXT], [])
        if not is_sample:
            cosb = lambda nh: ROPE[:pr, 0, bi * 4:bi * 4 + nsub, :].unsqueeze(2).to_broadcast([pr, nsub, nh, 8])
            sinb = lambda nh: ROPE[:pr, 1, bi * 4:bi * 4 + nsub, :].unsqueeze(2).to_broadcast([pr, nsub, nh, 8])
            brope = bROPE
        else:
            cosb = lambda nh: ROPS[:pr, 0, :].unsqueeze(1).unsqueeze(1).to_broadcast([pr, 1, nh, 8])
            sinb = lambda nh: ROPS[:pr, 1, :].unsqueeze(1).unsqueeze(1).to_broadcast([pr, 1, nh, 8])
            brope = bROPS
        nbank = (nsub + 1) // 2

        def rope_b(F, bF, G, bG, nh, width, off, bH):
            xv = F[:pr, 0:nsub * width].rearrange("p (s c) -> p s c", c=width)[:, :, off:off + nh * 64].rearrange(
                "p s (h d) -> p s h d", d=64)
            x1 = xv[:, :, :, 0:8]
            x2 = xv[:, :, :, 8:16]
            tt = [G[:pr, k * 128:k * 128 + nsub * nh * 8].rearrange("p (s h e) -> p s h e", s=nsub, e=8)
                  for k in range(4)]
            V("dve", "tensor_tensor", [bF, brope], [bG], tt[0], x1, cosb(nh), ALU.mult)
            V("dve", "tensor_tensor", [bF, brope], [bG], tt[1], x2, sinb(nh), ALU.mult)
            V("pool", "tensor_tensor", [bF, brope], [bH], tt[2], x2, cosb(nh), ALU.mult)
            V("pool", "tensor_tensor", [bF, brope], [bH], tt[3], x1, sinb(nh), ALU.mult)
            V("dve", "tensor_tensor", [bG, bH], [bF], x1, tt[0], tt[1], ALU.subtract)
            V("dve", "tensor_tensor", [bH, bG], [bF], x2, tt[2], tt[3], ALU.add)

        def tmA(j):
            w, bw = next_wr()
            load_piece(win[j], w[:, :], bw, 8, ("win", j))
            w3 = w[:, :].rearrange("p (k n) -> p k n", k=8)
            bk = [PA.get() for _ in range(nbank)]
            for s in range(nsub):
                b = bk[s // 2]
                off = (s % 2) * 256
                for kc in range(8):
                    mm(pf(b)[:pr, off:off + 256], XNT[:, kc, s * 128:s * 128 + pr], w3[:, kc, :], kc == 0, kc == 7,
                       [bw, bXNT], [pb(b)])
            return bk

        def tmB(j, bk):
            par = j % 3
            if par < 2:
                AR, bAR = STG[par]
                bGG, bHH = STGG[par], STGH[par]
            else:
                AR, bAR, bGG, bHH = ISB, bISB, bISB, bISB
            F, G = AR[:, 0:1024], AR[:, 1024:2048]
            BB, bBB = ((XSC, bXSC), (SQJ, bSQJ), (JUNK, bJUNK))[par]
            SSB, bSSB = TSETS[par].SSS16, TSETS[par].bSSS16
            RSB, bRSB = TSETS[par].RSS16, TSETS[par].bRSS16
            ns2 = [min(2, nsub - 2 * q_) for q_ in range(nbank)]
            P3 = [pf(bk[q_])[:pr, 0:ns2[q_] * 256].rearrange("p (s c) -> p s c", c=256) for q_ in range(nbank)]
            F3 = F[:pr, 0:nsub * 256].rearrange("p (s c) -> p s c", c=256)
            F3b = [F[:pr, q_ * 512:q_ * 512 + ns2[q_] * 256].rearrange("p (s c) -> p s c", c=256) for q_ in range(nbank)]
            r0 = t0
            c0 = s0
            if j in (6, 7, 8):
                kw = 256 if j != 8 else 128
                nh = kw // 64
                gcol = 0 if j != 8 else 64
                for q_ in range(nbank):
                    act(G[:pr, q_ * 2 * kw:q_ * 2 * kw + ns2[q_] * kw].rearrange("p (s c) -> p s c", c=kw),
                        P3[q_][:, :, 0:kw], AF.Square, [pb(bk[q_])], [bGG])
                V("dve", "tensor_reduce", [bGG], [bSSB], SSB[:pr, 0:nsub * nh],
                  G[:pr, 0:nsub * kw].rearrange("p (h d) -> p h d", d=64), AX.X, ALU.add)
                act(RSB[:pr, 0:nsub * nh], SSB[:pr, 0:nsub * nh], AF.Sqrt, [bSSB, bEPSB], [bRSB], scale=1.0 / 64,
                    bias=EPSB[:pr, 0:1])
                V("dve", "reciprocal", [bRSB], [bRSB], RSB[:pr, 0:nsub * nh], RSB[:pr, 0:nsub * nh])
                for q_ in range(nbank):
                    V("dve", "tensor_tensor", [pb(bk[q_]), bRSB], [bAR],
                      F3b[q_][:, :, 0:kw].rearrange("p s (h d) -> p s h d", d=64),
                      P3[q_][:, :, 0:kw].rearrange("p s (h d) -> p s h d", d=64),
                      RSB[:pr, q_ * 2 * nh:q_ * 2 * nh + ns2[q_] * nh].rearrange("p (s h) -> p s h", h=nh).unsqueeze(
                          3).to_broadcast([pr, ns2[q_], nh, 64]), ALU.mult)
                fk = F3[:, :, 0:kw].rearrange("p s (h d) -> p s h d", d=64)
                V("dve", "tensor_tensor", [bAR, bQKG], [bAR], fk, fk,
                  QKG[:pr, gcol:gcol + 64].unsqueeze(1).unsqueeze(1).to_broadcast([pr, nsub, nh, 64]), ALU.mult)
                rope_b(F, bAR, G, bGG, nh, 256, 0, bHH)
            if j in (6, 7):
                act(BB[:pr, 0:nsub * 256], F[:pr, 0:nsub * 256], AF.Copy, [bAR], [bBB])
                b2 = PA.get()
                for s in range(nsub):
                    for rr in range(2):
                        tr(pbf(b2)[:, (s * 2 + rr) * 128:(s * 2 + rr) * 128 + pr],
                           BB[:pr, s * 256 + rr * 128:s * 256 + (rr + 1) * 128], IDB[:pr, :pr], [bBB, bIDB], [pb(b2)])
                r_ = (j - 6) * 2
                V("dve", "tensor_copy", [pb(b2)], [bQT], QT[:, 0:nsub, r_:r_ + 2, 0:pr],
                  pbf(b2)[:, 0:nsub * 256].rearrange("p (s r t) -> p s r t", r=2, t=128)[:, :, :, 0:pr])
                PA.rel(b2)
            elif j == 8:
                for q_ in range(nbank):
                    act(F3b[q_][:, :, 128:256], P3[q_][:, :, 128:256], AF.Copy, [pb(bk[q_])], [bAR])
                if not is_sample:
                    dma(kp[r0:r0 + nsub * 128, :].rearrange("(s p) c -> p s c", p=128), F3[:, :, 0:128], [bAR], [], eng="act")
                    dma(vp[r0:r0 + nsub * 128, :].rearrange("(s p) c -> p s c", p=128), F3[:, :, 128:256], [bAR], [], eng="act")
                    kt_dst, bkt = KT[:, c0:c0 + nsub * 128], bKT
                    V("dve", "tensor_copy", [bAR], [bVE], VE[:pr, c0 // 128:c0 // 128 + nsub, :, 0:64],
                      F3[:, :, 128:256].rearrange("p s (g d) -> p s g d", d=64))
                else:
                    dma(kso[0:pr, :], F[:pr, 0:128], [bAR], [], eng="act")
                    dma(vso[0:pr, :], F[:pr, 128:256], [bAR], [], eng="act")
                    kt_dst, bkt = KTS[:, 0:pr], bKTS
                    V("dve", "tensor_copy", [bAR], [bVES], VES[:pr, :, 0:64],
                      F[:pr, 128:256].rearrange("p (g d) -> p g d", d=64))
                    V("dve", "tensor_copy", [bAR], [bKVF], KVF[:pr, 128:256], F[:pr, 128:256])
                act(BB[:pr, 0:nsub * 128].rearrange("p (s c) -> p s c", c=128), F3[:, :, 0:128], AF.Copy, [bAR], [bBB])
                b2 = PA.get()
                for s in range(nsub):
                    tr(pbf(b2)[:, s * 128:s * 128 + pr], BB[:pr, s * 128:(s + 1) * 128], IDB[:pr, :pr],
                       [bBB, bIDB], [pb(b2)])
                if not is_sample:
                    V("dve", "tensor_copy", [pb(b2)], [bkt], kt_dst, pbf(b2)[:, 0:nsub * 128])
                else:
                    V("dve", "tensor_copy", [pb(b2)], [bkt], kt_dst, pbf(b2)[:, 0:pr])
                PA.rel(b2)
            elif j in (9, 10):
                for q_ in range(nbank):
                    act(F3b[q_][:, :, :], P3[q_][:, :, :], AF.Copy, [pb(bk[q_])], [bAR])
                rope_b(F, bAR, G, bGG, 4, 256, 0, bHH)
                act(BB[:pr, 0:nsub * 256], F[:pr, 0:nsub * 256], AF.Copy, [bAR], [bBB])
                if is_sample and stage >= 3:
                    for jl in range(2):
                        jj = (j - 9) * 2 + jl
                        for half in range(2):
                            hh_ = half * 4 + jj
                            src_ = F[:pr, jl * 128 + half * 64:jl * 128 + half * 64 + 64]
                            V("dve", "tensor_copy", [bAR], [bQID], QID[:pr, hh_, :, :],
                              src_.unsqueeze(1).to_broadcast([pr, 2, 64]))
                b2 = PA.get()
                for s in range(nsub):
                    for rr in range(2):
                        tr(pbf(b2)[:, (s * 2 + rr) * 128:(s * 2 + rr) * 128 + pr],
                           BB[:pr, s * 256 + rr * 128:s * 256 + (rr + 1) * 128], IDB[:pr, :pr], [bBB, bIDB], [pb(b2)])
                r_ = (j - 9) * 2
                V("dve", "tensor_copy", [pb(b2)], [bQIT],
                  QIT[:, r_:r_ + 2, 0:nsub * 128].rearrange("p r (s t) -> p s r t", t=128)[:, :, :, 0:pr],
                  pbf(b2)[:, 0:nsub * 256].rearrange("p (s r t) -> p s r t", r=2, t=128)[:, :, :, 0:pr])
                PA.rel(b2)
            else:
                for q_ in range(nbank):
                    act(F3b[q_][:, :, 0:64], P3[q_][:, :, 0:64], AF.Copy, [pb(bk[q_])], [bAR])
                    V("dve", "tensor_copy", [pb(bk[q_])], [bWI], WI[:pr, q_ * 2:q_ * 2 + ns2[q_], :], P3[q_][:, :, 64:72])
                rope_b(F, bAR, G, bGG, 1, 256, 0, bHH)
                if not is_sample:
                    dma(ip[r0:r0 + nsub * 128, :].rearrange("(s p) c -> p s c", p=128), F3[:, :, 0:64], [bAR], [], eng="act")
                    kit_dst, bkit = KIT[:, c0:c0 + nsub * 128], bKIT
                else:
                    dma(iso[0:pr, :], F[:pr, 0:64], [bAR], [], eng="act")
                    kit_dst, bkit = KITS[:, 0:pr], bKITS
                bb3 = BB[:pr, 0:nsub * 128].rearrange("p (s c) -> p s c", c=128)
                act(bb3[:, :, 0:64], F3[:, :, 0:64], AF.Copy, [bAR], [bBB])
                act(bb3[:, :, 64:128], F3[:, :, 0:64], AF.Copy, [bAR], [bBB])
                b2 = PA.get()
                for s in range(nsub):
                    tr(pbf(b2)[:, s * 128:s * 128 + pr], BB[:pr, s * 128:(s + 1) * 128], IDB[:pr, :pr],
                       [bBB, bIDB], [pb(b2)])
                if not is_sample:
                    V("dve", "tensor_copy", [pb(b2)], [bkit], kit_dst, pbf(b2)[:, 0:nsub * 128])
                else:
                    V("dve", "tensor_copy", [pb(b2)], [bkit], kit_dst, pbf(b2)[:, 0:pr])
                PA.rel(b2)
            for b in bk:
                PA.rel(b)

        ctxs = {6: tmA(6), 7: tmA(7)}
        for j in range(6, 12):
            if j + 2 <= 11:
                ctxs[j + 2] = tmA(j + 2)
            tmB(j, ctxs.pop(j))


    KTS, bKTS = sb("KTS", [128, 16], BF16)
    KITS, bKITS = sb("KITS", [128, 16], BF16)
    VES, bVES = sb("VES", [16, 2, 65], BF16)
    V("pool", "memset", [], [bVES], VES[:, :, :], 1.0)

    def attn_prompt_block(blk):
        bi = blk % 4

        def geom(s):
            i = bi * 4 + s
            nk = i + 1
            return i, nk, nk * 128, (nk * 128 + 511) // 512

        def idx_prep(s):
            QIBD, bQIBD = QIBD2[s % 2]
            WSEL, bWSEL = WSEL2[s % 2]
            for half in range(2):
                srcv = QIT[half * 64:(half + 1) * 64, :, s * 128:(s + 1) * 128].rearrange(
                    "p j (g t) -> p g j t", t=16)
                dst = QIBD[half * 64:(half + 1) * 64, :, half * 64:(half + 1) * 64].rearrange(
                    "p g (j t) -> p g j t", t=16)
                V("pool", "tensor_copy", [bQIT], [bQIBD], dst, srcv)
            V("dve", "tensor_copy", [bWI], [bWIREP], WIREP[:, :].rearrange("p (h t) -> p h t", t=16),
              WI[:, s, :].unsqueeze(2).to_broadcast([128, 8, 16]))
            b = PA.get()
            tr(pf(b)[:, 0:128], WIREP[:, :], IDF[:, :], [bWIREP, bIDF], [pb(b)])
            V("dve", "tensor_tensor", [bEM, pb(b)], [bWSEL], WSEL[:, :, :], EM[:, :, :],
              pf(b)[:, 0:128].unsqueeze(1).to_broadcast([128, 8, 128]), ALU.mult)
            PA.rel(b)

        def idx_compute(s):
            i, nk, Sk, npc = geom(s)
            QIBD, bQIBD = QIBD2[s % 2]
            WSEL, bWSEL = WSEL2[s % 2]
            held = []
            items = [(p, grp) for p in range(npc) for grp in range(8)]
            bIs = {}

            def score(k):
                p, grp = items[k]
                w = min(512, Sk - p * 512)
                bs = PA.get()
                mm(pf(bs)[:, :w], QIBD[:, grp, :], KIT[:, p * 512:p * 512 + w], True, True,
                   [bQIBD, bKIT], [pb(bs)])
                return bs

            nxt_bs = score(0)
            for k, (p, grp) in enumerate(items):
                w = min(512, Sk - p * 512)
                bs = nxt_bs
                if k + 1 < len(items):
                    nxt_bs = score(k + 1)
                if grp == 0:
                    bIs[p] = PA.get()
                bI = bIs[p]
                rr, brr = RR[k % 3]
                act(rr[:, :w], pf(bs)[:, :w], AF.Relu, [pb(bs)], [brr])
                PA.rel(bs)
                mm(pf(bI)[:, :w], WSEL[:, grp, :], rr[:, :w], grp == 0, grp == 7, [bWSEL, brr], [pb(bI)])
                if grp == 7:
                    held.append((bI, p, w))
            return held

        def idx_evac(s, held):
            i, nk, Sk, npc = geom(s)
            for (bI, p, w) in held:
                act(ISB[:, p * 512:p * 512 + w], pf(bI)[:, :w], AF.Copy, [pb(bI)], [bISB])
                PA.rel(bI)

        def bisect(s):
            i, nk, Sk, npc = geom(s)
            if i >= 2:
                V("dve", "tensor_reduce", [bISB], [bBS], BS[:, 0:1], ISB[:, :Sk], AX.X, ALU.max,
                  apply_absolute_value=True)
                V("dve", "tensor_scalar", [bBS], [bBS], BS[:, 1:2], BS[:, 0:1], 2.0, 2.0, ALU.mult, ALU.add)
                V("dve", "tensor_scalar", [bBS, bCPOW], [bHW], HW[:, :], CPOW[:, :], BS[:, 1:2], None, ALU.mult)
                V("dve", "memset", [], [bMIDP], MIDP[:, :], 0.0)
            V("dve", "tensor_tensor", [bISB, bCM], [bISB], ISB[:, i * 128:(i + 1) * 128],
              ISB[:, i * 128:(i + 1) * 128], CM[:, :], ALU.add)
            if i >= 2:
                for n in range(N_BISECT):
                    V("dve", "tensor_scalar", [bISB, bMIDP], [bJUNK, bBS], JUNK[:, :Sk], ISB[:, :Sk], MIDP[:, 0:1],
                      None, ALU.is_ge, ALU.add, accum_out=BS[:, 4:5])
                    V("dve", "scalar_tensor_tensor", [bBS, bHW], [bBS], BS[:, 5:6], BS[:, 4:5], K_TOP - 0.5,
                      HW[:, n:n + 1], ALU.is_ge, ALU.mult)
                    V("dve", "scalar_tensor_tensor", [bMIDP, bHW, bBS], [bMIDP], MIDP[:, 0:1], MIDP[:, 0:1],
                      HW[:, n + 1:n + 2], BS[:, 5:6], ALU.subtract, ALU.add)
                V("dve", "tensor_scalar", [bMIDP, bHW], [bBS], BS[:, 2:3], MIDP[:, 0:1], HW[:, N_BISECT:N_BISECT + 1],
                  None, ALU.subtract)
                V("dve", "tensor_scalar", [bISB, bBS], [bMASKB], MASKB[:, :Sk], ISB[:, :Sk], BS[:, 2:3], None,
                  ALU.is_ge)
            else:
                V("dve", "tensor_scalar", [bISB], [bMASKB], MASKB[:, :Sk], ISB[:, :Sk], -1.0e29, None, ALU.is_ge)

        def masktrans(s):
            i, nk, Sk, npc = geom(s)
            for c0 in range(0, nk, 8):
                n = min(8, nk - c0)
                b = PA.get()
                for c in range(c0, c0 + n):
                    tr(pbf(b)[:, (c - c0) * 128:(c - c0 + 1) * 128], MASKB[:, c * 128:(c + 1) * 128], IDB[:, :],
                       [bMASKB, bIDB], [pb(b)])
                act(MASKT[:, c0:c0 + n, :], pbf(b)[:, 0:n * 128].rearrange("p (c t) -> p c t", t=128), AF.Identity,
                    [pb(b), bMB], [bMASKT], scale=60000.0, bias=MBIAS[:, 0:1])
                PA.rel(b)

        def attend(s):
            i, nk, Sk, npc = geom(s)
            bO = [PA.get(), PA.get()]
            aitems = [(c, g) for c in range(nk) for g in range(2)]

            def qk(k):
                c, g = aitems[k]
                bs = PA.get()
                mm(pf(bs)[:, :], KT[g * 64:(g + 1) * 64, c * 128:(c + 1) * 128],
                   QT[g * 64:(g + 1) * 64, s, :, :].rearrange("p r t -> p (r t)"), True, False,
                   [bKT, bQT], [pb(bs)])
                for r in range(4):
                    mm(pf(bs)[:, r * 128:(r + 1) * 128], IDB[:, :], MASKT[:, c, :], False, r == 3,
                       [bIDB, bMASKT], [pb(bs)])
                return bs

            nxt_bs = qk(0)
            for k, (c, g) in enumerate(aitems):
                bs = nxt_bs
                if k + 1 < len(aitems):
                    nxt_bs = qk(k + 1)
                pt_, bpt = (PEXP + PTB)[k % 4]
                act(pt_[:, :], pf(bs)[:, :], AF.Exp, [pb(bs)], [bpt], scale=0.125)
                PA.rel(bs)
                for r in range(4):
                    mm(pf(bO[g])[:, r * 65:(r + 1) * 65], pt_[:, r * 128:(r + 1) * 128], VE[:, c, g, :],
                       c == 0 and r == 0, c == nk - 1 and r == 3, [bpt, bVE], [pb(bO[g])])
            for g in range(2):
                o3 = pf(bO[g])[:, 0:260].rearrange("p (r e) -> p r e", e=65)
                act(DEN[:, g * 4:(g + 1) * 4].unsqueeze(2), o3[:, :, 64:65], AF.Copy, [pb(bO[g])], [bDEN])
            V("pool", "tensor_tensor", [bDEN, bNEG1], [bRDEN], RDEN[:, :], DEN[:, :], NEG1[:, :], ALU.pow)
            for g in range(2):
                o3 = pf(bO[g])[:, 0:260].rearrange("p (r e) -> p r e", e=65)
                for r in range(4):
                    act(OB[:, r, g, :], o3[:, r, 0:64], AF.Copy, [pb(bO[g]), bRDEN], [bOB],
                        scale=RDEN[:, g * 4 + r:g * 4 + r + 1])
                PA.rel(bO[g])
            b = PA.get()
            for r in range(4):
                tr(pbf(b)[:, r * 128:(r + 1) * 128], OB[:, r, :, :].rearrange("p g d -> p (g d)"), IDB[:, :],
                   [bOB, bIDB], [pb(b)])
            act(OT[:, :, :], pbf(b)[:, 0:512].rearrange("p (r t) -> p r t", r=4), AF.Copy, [pb(b)], [bOT])
            PA.rel(b)
            w_out_add(s, 128, True)

        idx_prep(0)
        held = idx_compute(0)
        idx_evac(0, held)
        idx_prep(1)
        bisect(0)
        held = idx_compute(1)
        masktrans(0)
        idx_evac(1, held)
        idx_prep(2)
        for s in range(4):
            held2 = idx_compute(s + 2) if s + 2 <= 3 else None
            if s + 1 <= 3:
                bisect(s + 1)
            attend(s)
            if held2 is not None:
                idx_evac(s + 2, held2)
                if s + 3 <= 3:
                    idx_prep(s + 3)
            if s + 1 <= 3:
                masktrans(s + 1)

    def w_out_add(s, pr, with_attn, ot_cols=None, otbuf=None):
        for hh in range(2):
            b = PA.get()
            nk8 = 8 if with_attn else 4
            for k8 in range(nk8):
                if k8 < 4:
                    lhs, rb = YCT[:, k8, s * 128:s * 128 + pr], bYCT
                else:
                    lhs, rb = (OT[:, k8 - 4, 0:pr] if ot_cols is None else ot_cols(k8 - 4)), (bOT if otbuf is None else otbuf)
                mm(pf(b)[:pr, :], lhs, WOUT[:, k8, hh * 512:(hh + 1) * 512], k8 == 0, k8 == nk8 - 1,
                   [rb, bWOUT], [pb(b)])
            if with_attn and pr == 128:
                tx, btx = ST[1]
                act(tx[:pr, :], pf(b)[:pr, :], AF.Copy, [pb(b)], [btx])
                V("pool", "tensor_tensor", [btx, bXs[s]], [bXs[s]], X[:pr, s, hh * 512:(hh + 1) * 512], tx[:pr, :],
                  X[:pr, s, hh * 512:(hh + 1) * 512], ALU.add)
            else:
                V("dve", "tensor_tensor", [pb(b), bXs[s]], [bXs[s]], X[:pr, s, hh * 512:(hh + 1) * 512], pf(b)[:pr, :],
                  X[:pr, s, hh * 512:(hh + 1) * 512], ALU.add)
            PA.rel(b)

    N_BIS_S = 18
    if stage >= 3:
        IT = X[:, 1:4, :].rearrange("p a n -> p (a n)")[:, 0:2064].rearrange("p (b k t) -> p b k t", b=4, k=129)
        bIT = bX
        CMPB = XNT[:, :, :].rearrange("p k n -> p (k n)")[:, 0:2064].rearrange("p (b k t) -> p b k t", b=4, k=129)
        bCMPB = bXNT
        MASKS, bMASKS = sb("MASKS", [128, 4, 129, 4], BF16)
        QID, bQID = sb("QID", [16, 8, 2, 64], BF16)
        QITD, bQITD = sb("QITD", [128, 4, 4, 8], BF16)
        QTSB, bQTSB = sb("QTSB", [128, 4, 4, 4], BF16)
        WM, bWM = sb("WM", [16, 16, 8])
        WB, bWB = sb("WB", [128, 128])
        ONESF, bONESF = sb("ONESF", [128, 128])
        PTSB, bPTSB = sb("PTSB", [128, 4], I32)
        PTF, bPTF = sb("PTF", [128, 4])
        IDX, bIDX = sb("IDX", [128, 4, 8], I32)
        IDX4, bIDX4 = sb("IDX4", [128, 4, 4], I32)
        CMS, bCMS = sb("CMS", [128, 4, 4])
        SELM, bSELM = sb("SELM", [16, 8])
        VMK, bVMK = sb("VMK", [16, 4, 128])
        VESB, bVESB = sb("VESB", [4, 4, 2, 65], BF16)
        LO, bLO = sb("LO", [128, 16])
        MID, bMID = sb("MID", [128, 16])
        PC, bPC = sb("PC", [128, 16])
        T1, bT1 = sb("T1", [128, 16])
        AM, bAM = sb("AM", [128, 4])
        OBS, bOBS = sb("OBS", [16, 2, 64], BF16)
        OTS, bOTS = sb("OTS", [128, 4, 16], BF16)
        RDS, bRDS = sb("RDS", [16, 2])
        KTP2, bKTP2 = sb("KTP2", [128, 2048], BF16)
        VE2, bVE2 = sb("VE2", [128, 16, 2, 65], BF16)
        V("pool", "memset", [], [bVE2], VE2[:, :, :, :], 1.0)
        ck8 = ck.rearrange("n (e x) -> (n e) x", e=8)
        cv8 = cv.rearrange("n (e x) -> (n e) x", e=8)
        cik4 = cik.rearrange("n (e x) -> (n e) x", e=4)
        selm_d = din("selm", [16, 8])
        dma(PTSB[:, :], ptT, [], [bPTSB])
        dma(CMS[:, :, :], cms_d, [], [bCMS])
        dma(SELM[:, :], selm_d, [], [bSELM])
        V("pool", "memset", [], [bONESF], ONESF[:, :], 1.0)
        V("pool", "memset", [], [bVESB], VESB[:, :, :, :], 1.0)

    _bc = {}

    def bc_reg(h, val):
        if val not in _bc:
            _bc[val] = h.to_reg(val)
        return _bc[val]

    def attn_sample_block():
        V("pool", "memset", [], [bIT, bXs[1], bXs[2], bXs[3]], IT[:, :, :, :], NEG)
        V("dve", "tensor_copy", [bPTSB], [bPTF], PTF[:, :], PTSB[:, :])
        for e in range(8):
            V("dve", "tensor_scalar", [bPTF], [bIDX], IDX[:, :, e], PTF[:, :], 8.0, float(e), ALU.mult, ALU.add)
        for e in range(4):
            V("dve", "tensor_scalar", [bPTF], [bIDX4], IDX4[:, :, e], PTF[:, :], 4.0, float(e), ALU.mult, ALU.add)
        b0 = PA.get()
        for h in range(8):
            tr(pbf(b0)[:, h * 16:(h + 1) * 16], QID[:, h, :, :].rearrange("p a d -> p (a d)"), IDB[:16, :16],
               [bQID, bIDB], [pb(b0)])
        V("dve", "tensor_copy", [pb(b0)], [bQITD], QITD[:, :, :, :].rearrange("p b t h -> p h b t"),
          pbf(b0)[:, 0:128].rearrange("p (h b t) -> p h b t", h=8, b=4))
        PA.rel(b0)
        V("dve", "tensor_copy", [bQT], [bQTSB], QTSB[:, :, :, :].rearrange("p b r t -> p r b t"),
          QT[:, 0, :, 0:16].rearrange("p r (b t) -> p r b t", t=4))
        V("dve", "tensor_tensor", [bWI, bIDF], [bWM], WM[:, :, :], WI[:16, 0, :].unsqueeze(1).to_broadcast([16, 16, 8]),
          IDF[:16, :16].unsqueeze(2).to_broadcast([16, 16, 8]), ALU.mult)
        b0 = PA.get()
        mm(pf(b0)[:, 0:128], ONESF[:16, :], WM[:, :, :].rearrange("p a h -> p (a h)"), True, True, [bONESF, bWM], [pb(b0)])
        act(WB[:, :], pf(b0)[:, 0:128], AF.Copy, [pb(b0)], [bWB])
        PA.rel(b0)
        V("dve", "tensor_tensor", [bKVF, bSELM], [bVMK], VMK[:, :, :], KVF[:16, 128:256].unsqueeze(1).to_broadcast([16, 4, 128]),
          SELM[:, 4:8].unsqueeze(2).to_broadcast([16, 4, 128]), ALU.mult)
        b0 = PA.get()
        mm(pf(b0)[0:4, :], SELM[:, 0:4], VMK[:, :, :].rearrange("p b x -> p (b x)"), True, True, [bSELM, bVMK], [pb(b0)])
        V("dve", "tensor_copy", [pb(b0)], [bVESB], VESB[:, :, :, 0:64],
          pf(b0)[0:4, :].rearrange("p (b g d) -> p b g d", b=4, g=2))
        PA.rel(b0)
        if PARTS < 1:
            return w_out_add(0, 16, False)
        IKP, bIKP = ISB, bISB
        IKB, bIKB = JUNK, bJUNK
        KITP, bKITP = MASKB, bMASKB
        for b in range(4):
            for q4 in range(4):
                S.add("pool", lambda h, b=b, q4=q4: h.indirect_dma_start(
                    out=IKP[:, :], out_offset=None, in_=cik4,
                    in_offset=bass.IndirectOffsetOnAxis(ap=IDX4[:, b, q4:q4 + 1], axis=0),
                    bounds_check=bc_reg(h, npool * 4 - 1), oob_is_err=False),
                    [bIDX4], [bIKP], dma=True)
                act(IKB[:, :], IKP[:, :], AF.Copy, [bIKP], [bIKB])
                if SUB < 1:
                    continue
                for c0 in range(0, 16, 8):
                    bb = PA.get()
                    for pp in range(8):
                        tr(pbf(bb)[:, pp * 128:(pp + 1) * 128], IKB[:, (c0 + pp) * 128:(c0 + pp + 1) * 128], IDB[:, :],
                           [bIKB, bIDB], [pb(bb)])
                    V("dve", "tensor_copy", [pb(bb)], [bKITP], KITP[:, c0 * 128:(c0 + 8) * 128], pbf(bb)[:, :])
                    PA.rel(bb)
                if SUB < 2:
                    continue
                bsh = [PA.get(), PA.get()]
                for kk in range(32):
                    half, m = kk % 2, kk // 2
                    mm(pf(bsh[half])[:, m * 32:(m + 1) * 32], KITP[half * 64:(half + 1) * 64, m * 128:(m + 1) * 128],
                       QITD[half * 64:(half + 1) * 64, b, :, :].rearrange("p t h -> p (t h)"), True, True,
                       [bKITP, bQITD], [pb(bsh[half])])
                for half in range(2):
                    rr, brr = RR[half]
                    act(rr[:, :], pf(bsh[half])[:, :], AF.Relu, [pb(bsh[half])], [brr])
                    PA.rel(bsh[half])
                    if SUB < 3:
                        continue
                    st, bst = ST[half]
                    V("dve", "tensor_tensor", [brr, bWB], [bst], st[:, :].rearrange("p (k x) -> p k x", k=16),
                      rr[:, :].rearrange("p (k x) -> p k x", k=16),
                      WB[:, b * 32:(b + 1) * 32].unsqueeze(1).to_broadcast([128, 16, 32]), ALU.mult)
                    V("dve", "tensor_reduce", [bst], [bIT],
                      IT[:, b, q4 * 32:(q4 + 1) * 32, :].rearrange("p (m two) t -> p m two t", two=2)[:, :, half, :],
                      st[:, :].rearrange("p (k t h) -> p k t h", k=16, t=4), AX.X, ALU.add)
            if SUB < 4:
                continue
            bs = PA.get()
            mm(pf(bs)[0:4, 0:32], KITS[0:64, b * 4:(b + 1) * 4], QITD[0:64, b, :, :].rearrange("p t h -> p (t h)"),
               True, True, [bKITS, bQITD], [pb(bs)])
            rr, brr = RR[2]
            act(rr[0:4, 0:32], pf(bs)[0:4, 0:32], AF.Relu, [pb(bs)], [brr])
            PA.rel(bs)
            st, bst = ST[0]
            V("dve", "tensor_tensor", [brr, bWB], [bst], st[0:4, 0:32], rr[0:4, 0:32], WB[0:4, b * 32:(b + 1) * 32], ALU.mult)
            V("dve", "tensor_reduce", [bst], [bAM], AM[0:4, 0:4], st[0:4, 0:32].rearrange("p (t h) -> p t h", t=4),
              AX.X, ALU.add)
            V("dve", "tensor_tensor", [bAM, bCMS], [bIT], IT[0:4, b, 128, :], AM[0:4, 0:4], CMS[0:4, b, :], ALU.add)
        if PARTS < 2:
            return w_out_add(0, 16, False)
        V("dve", "tensor_reduce", [bIT], [bAM], AM[:, 0:1], IT[:, :, 0:128, :], AX.XYZ, ALU.max, apply_absolute_value=True)
        b0 = PA.get()
        tr(pf(b0)[0:1, 0:128], AM[:, 0:1], IDF[:, :], [bAM, bIDF], [pb(b0)])
        V("dve", "tensor_reduce", [pb(b0)], [bAM], AM[0:1, 1:2], pf(b0)[0:1, 0:128], AX.X, ALU.max)
        PA.rel(b0)
        b0 = PA.get()
        mm(pf(b0)[:, 0:2], ONESF[0:1, :], AM[0:1, 1:3], True, True, [bONESF, bAM], [pb(b0)])
        V("dve", "tensor_scalar", [pb(b0)], [bAM], AM[:, 2:3], pf(b0)[:, 0:1], 2.0, 2.0, ALU.mult, ALU.add)
        V("dve", "tensor_scalar", [pb(b0)], [bAM], AM[:, 3:4], pf(b0)[:, 0:1], -1.0, -1.0, ALU.mult, ALU.add)
        PA.rel(b0)
        V("dve", "tensor_copy", [bAM], [bLO], LO[:, :], AM[:, 3:4].to_broadcast([128, 16]))
        if PARTS < 3:
            return w_out_add(0, 16, False)
        itv = IT[:, :, :, :].rearrange("p b k t -> p b t k")
        cmv = CMPB[:, :, :, :].rearrange("p b k t -> p b t k")
        for n in range(N_BIS_S):
            c = 2.0 ** -(n + 1)
            V("dve", "scalar_tensor_tensor", [bAM, bLO], [bMID], MID[:, :], AM[:, 2:3].to_broadcast([128, 16]), c, LO[:, :],
              ALU.mult, ALU.add)
            V("dve", "tensor_tensor", [bIT, bMID], [bCMPB], cmv, itv,
              MID[:, :].rearrange("p (b t) -> p b t", t=4).unsqueeze(3).to_broadcast([128, 4, 4, 129]), ALU.is_ge)
            V("dve", "tensor_reduce", [bCMPB], [bPC], PC[:, :].rearrange("p (b t) -> p b t", t=4), cmv, AX.X, ALU.add)
            b0 = PA.get()
            mm(pf(b0)[:, 0:16], ONESF[:, :], PC[:, :], True, True, [bONESF, bPC], [pb(b0)])
            V("dve", "tensor_scalar", [pb(b0)], [bT1], T1[:, :], pf(b0)[:, 0:16], K_TOP - 0.5, c, ALU.is_ge, ALU.mult)
            PA.rel(b0)
            V("dve", "scalar_tensor_tensor", [bT1, bAM, bLO], [bLO], LO[:, :], T1[:, :], AM[:, 2:3], LO[:, :], ALU.mult, ALU.add)
        V("dve", "tensor_tensor", [bIT, bLO], [bMASKS], MASKS[:, :, :, :].rearrange("p b k t -> p b t k"), itv,
          LO[:, :].rearrange("p (b t) -> p b t", t=4).unsqueeze(3).to_broadcast([128, 4, 4, 129]), ALU.is_ge)
        if PARTS < 4:
            return w_out_add(0, 16, False)
        KPs = [(ISB, [bISB]), (STG[1][0], [STG[1][1], STGG[1], STGH[1]])]
        VPs = [(STG[0][0], [STG[0][1], STGG[0], STGH[0]]), (XNT[:, :, :].rearrange("p k n -> p (k n)").bitcast(F32), [bXNT])]
        KPBs = [(JUNK, bJUNK), (MASKT[:, :, :].rearrange("p c t -> p (c t)"), bMASKT)]
        KTPs = [(MASKB, bMASKB), (KTP2, bKTP2)]
        VEs = [(VE, bVE), (VE2, bVE2)]
        steps = [(b, e) for b in range(4) for e in range(8)]

        def gather(it):
            b, e = steps[it]
            KP, bKP = KPs[it % 2]
            VP, bVP = VPs[it % 2]
            S.add("pool", lambda h, b=b, e=e, KP=KP: h.indirect_dma_start(
                out=KP[:, :], out_offset=None, in_=ck8,
                in_offset=bass.IndirectOffsetOnAxis(ap=IDX[:, b, e:e + 1], axis=0),
                bounds_check=bc_reg(h, npool * 8 - 1), oob_is_err=False),
                [bIDX], bKP, dma=True)
            S.add("pool", lambda h, b=b, e=e, VP=VP: h.indirect_dma_start(
                out=VP[:, :], out_offset=None, in_=cv8,
                in_offset=bass.IndirectOffsetOnAxis(ap=IDX[:, b, e:e + 1], axis=0),
                bounds_check=bc_reg(h, npool * 8 - 1), oob_is_err=False),
                [bIDX], bVP, dma=True)

        def cast(it):
            KP, bKP = KPs[it % 2]
            VP, bVP = VPs[it % 2]
            KPB, bKPB = KPBs[it % 2]
            VEx, bVEx = VEs[it % 2]
            act(KPB[:, :], KP[:, :], AF.Copy, bKP, [bKPB])
            act(VEx[:, :, :, 0:64], VP[:, :].rearrange("p (k g d) -> p k g d", k=16, g=2), AF.Copy, bVP, [bVEx])

        state = {}

        def compute(it):
            b, e = steps[it]
            KPB, bKPB = KPBs[it % 2]
            KTP, bKTP = KTPs[it % 2]
            VEx, bVEx = VEs[it % 2]
            if e == 0:
                state["bO"] = PA.get()
                state["first"] = True
            bO = state["bO"]
            for c0 in range(0, 16, 8):
                bb = PA.get()
                for pp in range(8):
                    tr(pbf(bb)[:, pp * 128:(pp + 1) * 128], KPB[:, (c0 + pp) * 128:(c0 + pp + 1) * 128], IDB[:, :],
                       [bKPB, bIDB], [pb(bb)])
                V("dve", "tensor_copy", [pb(bb)], [bKTP], KTP[:, c0 * 128:(c0 + 8) * 128], pbf(bb)[:, :])
                PA.rel(bb)
            bsg = [PA.get(), PA.get()]
            for kl in range(16):
                for g in range(2):
                    mm(pf(bsg[g])[:, kl * 16:(kl + 1) * 16],
                       KTP[g * 64:(g + 1) * 64, kl * 128:(kl + 1) * 128],
                       QTSB[g * 64:(g + 1) * 64, b, :, :].rearrange("p r t -> p (r t)"), True, True,
                       [bKTP, bQTSB], [pb(bsg[g])])
            for g in range(2):
                pe_, bpe = PEXP[g]
                act(pe_[:, 0:256], pf(bsg[g])[:, 0:256], AF.Exp, [pb(bsg[g])], [bpe], scale=0.125)
                PA.rel(bsg[g])
                pt_, bpt = PTB[g]
                V("pool", "tensor_tensor", [bpe, bMASKS], [bpt],
                  pt_[:, 0:256].rearrange("p (k r t) -> p k r t", k=16, t=4),
                  pe_[:, 0:256].rearrange("p (k r t) -> p k r t", k=16, t=4),
                  MASKS[:, b, e * 16:(e + 1) * 16, :].unsqueeze(2).to_broadcast([128, 16, 4, 4]), ALU.mult)
            for kl in range(16):
                for g in range(2):
                    pt_, bpt = PTB[g]
                    mm(pf(bO)[0:16, g * 65:(g + 1) * 65], pt_[:, kl * 16:(kl + 1) * 16],
                       VEx[:, kl, g, :], state["first"], False, [bpt, bVEx], [pb(bO)])
                    state["first"] = False
            if e == 7:
                finish_seq(b, bO)

        def finish_seq(b, bO):
                for g in range(2):
                    bs = PA.get()
                    mm(pf(bs)[0:4, 0:16], KTS[g * 64:(g + 1) * 64, b * 4:(b + 1) * 4],
                       QTSB[g * 64:(g + 1) * 64, b, :, :].rearrange("p r t -> p (r t)"), True, True, [bKTS, bQTSB], [pb(bs)])
                    pe_, bpe = PEXP[g]
                    act(pe_[0:4, 0:16], pf(bs)[0:4, 0:16], AF.Exp, [pb(bs)], [bpe], scale=0.125)
                    PA.rel(bs)
                    pt_, bpt = PTB[g]
                    V("pool", "tensor_tensor", [bpe, bMASKS], [bpt], pt_[0:4, 0:16].rearrange("p (r t) -> p r t", t=4),
                      pe_[0:4, 0:16].rearrange("p (r t) -> p r t", t=4),
                      MASKS[0:4, b, 128, :].unsqueeze(1).to_broadcast([4, 4, 4]), ALU.mult)
                    mm(pf(bO)[0:16, g * 65:(g + 1) * 65], pt_[0:4, 0:16], VESB[0:4, b, g, :], False, g == 1,
                       [bpt, bVESB], [pb(bO)])
                o3 = pf(bO)[0:16, 0:130].rearrange("p (g e) -> p g e", e=65)
                V("dve", "reciprocal", [pb(bO)], [bRDS], RDS[:, :].unsqueeze(2), o3[:, :, 64:65])
                V("dve", "tensor_tensor", [pb(bO), bRDS], [bOBS], OBS[:, :, :], o3[:, :, 0:64],
                  RDS[:, :].unsqueeze(2).to_broadcast([16, 2, 64]), ALU.mult)
                PA.rel(bO)
                b0 = PA.get()
                tr(pbf(b0)[:, 0:16], OBS[:, :, :].rearrange("p g d -> p (g d)"), IDB[:16, :16], [bOBS, bIDB], [pb(b0)])
                V("dve", "tensor_copy", [pb(b0)], [bOTS], OTS[:, :, b * 4:(b + 1) * 4],
                  pbf(b0)[:, 0:16].rearrange("p (r t) -> p r t", t=4))
                PA.rel(b0)

        gather(0)
        cast(0)
        for it in range(len(steps)):
            if it + 1 < len(steps):
                gather(it + 1)
            compute(it)
            if it + 1 < len(steps):
                cast(it + 1)
        w_out_add(0, 16, True, ot_cols=lambda r: OTS[:, r, :], otbuf=bOTS)

    nblk_prompt = 8
    for blk in range(nblk_prompt + 1):
        is_sample = blk == nblk_prompt
        cur_blk[0] = blk
        if not is_sample:
            nsub, pr, nb = 4, 128, 512
            for s_ in range(4):
                dma(X[:, s_, :], xp[blk * 512 + s_ * 128:blk * 512 + (s_ + 1) * 128, :], [], [bXs[s_]], eng="act")
        else:
            nsub, pr, nb = 1, 16, 16
            dma(X[:16, 0, :], xs, [], [bXs[0]])
            for i4 in range(4):
                dma(UEXT[:, i4, 0:24].rearrange("p (b t) -> p b t", t=6)[:, :, 0:2], cconv[:, i4, :, :],
                    [bUEXT], [bUEXT])
        norm_block(nsub, pr)
        ffn_block(wf1, 0, nsub, pr, nb, "wf1")
        mix_in_block(blk, nsub, pr, nb, is_sample)
        if stage >= 2:
            if not is_sample:
                attn_prompt_block(blk)
            elif stage >= 3:
                attn_sample_block()
            else:
                w_out_add(0, 16, False)
            norm_block(nsub, pr)
            if not is_sample:
                def store_sub(s_, blk=blk):
                    dma(yp[blk * 512 + s_ * 128:blk * 512 + (s_ + 1) * 128, :], X[:, s_, :], [bXs[s_]], [], eng="act")
                ffn_block(wf2, 16, nsub, pr, nb, "wf2", after_sub=store_sub)
            else:
                ffn_block(wf2, 16, nsub, pr, nb, "wf2")
        if not is_sample:
            if stage < 2:
                for s_ in range(4):
                    dma(yp[blk * 512 + s_ * 128:blk * 512 + (s_ + 1) * 128, :], X[:, s_, :], [bXs[s_]], [], eng="act")
        else:
            dma(ys, X[:16, 0, :], [bXs[0]], [])

    S.emit(nc, es)
    es.close()
    return nc


def _pieces_kn(W, ncol_piece=256):
    K, N = W.shape
    nj = N // ncol_piece
    a = W.reshape(8, 128, nj, ncol_piece).transpose(2, 1, 0, 3)
    return np.ascontiguousarray(a.reshape(nj, 128, 8 * ncol_piece))


def _pieces_rows(W, rows_piece=256):
    R, N = W.shape
    ng = R // rows_piece
    a = W.reshape(ng, 2, 128, N).transpose(0, 2, 1, 3)
    return np.ascontiguousarray(a.reshape(ng, 128, 2 * N))


def _ffn_pieces(wg, wu, wd):
    pg = _pieces_kn(wg)
    pu = _pieces_kn(wu)
    pd = _pieces_rows(wd)
    out = np.empty((33, 128, 2048), np.float32)
    out[0::3] = pg
    out[1::3] = pu
    out[2::3] = pd
    return out


def _perm_heads():
    idx = np.empty(512, np.int64)
    for r in range(4):
        for g in range(2):
            for d in range(64):
                idx[r * 128 + g * 64 + d] = (g * 4 + r) * 64 + d
    return idx


def _prep_shared(inp):
    f = lambda k: np.asarray(inp[k], np.float32)[0]
    sh = {}
    sh["wf1"] = _ffn_pieces(f("ffn1_w_gate"), f("ffn1_w_up"), f("ffn1_w_down"))
    sh["wf2"] = _ffn_pieces(f("ffn2_w_gate"), f("ffn2_w_up"), f("ffn2_w_down"))
    w_in = f("w_in")
    ph = _perm_heads()
    gb, gc, hc = w_in[:, 0:512], w_in[:, 512:1024], w_in[:, 1024:1536]
    q = w_in[:, 1536:2048][:, ph]
    k = w_in[:, 2048:2176]
    v = w_in[:, 2176:2304]
    qi = w_in[:, 2304:2816][:, ph]
    ki = w_in[:, 2816:2880]
    wi = w_in[:, 2880:2888]
    conv_cols = []
    for i in range(4):
        for t in (gb, gc, hc):
            conv_cols.append(t[:, i * 128:(i + 1) * 128])
    pad = np.zeros((1024, 256 - 72), np.float32)
    wperm = np.concatenate(conv_cols + [q, k, v, qi, ki, wi, pad], axis=1)
    assert wperm.shape[1] == 3072
    sh["win"] = _pieces_kn(wperm)
    w_out = f("w_out")
    wo = np.empty((128, 8, 1024), np.float32)
    for i in range(4):
        wo[:, i, :] = w_out[i * 128:(i + 1) * 128, :]
    for r in range(4):
        for g in range(2):
            h = g * 4 + r
            wo[g * 64:(g + 1) * 64, 4 + r, :] = w_out[512 + h * 64:512 + (h + 1) * 64, :]
    sh["wout"] = np.ascontiguousarray(wo.reshape(128, 4, 2048).transpose(1, 0, 2))
    gains = np.empty((128, 24), np.float32)
    for n, key in enumerate(("ffn1_norm", "mix_norm", "ffn2_norm")):
        gains[:, n * 8:(n + 1) * 8] = f(key).reshape(8, 128).T
    sh["gains"] = gains
    cw = f("conv_w")
    cb = f("conv_b")
    convp = np.empty((128, 16), np.float32)
    for i in range(4):
        for j in range(3):
            convp[:, i * 4 + j] = cw[j, i * 128:(i + 1) * 128]
        convp[:, i * 4 + 3] = cb[i * 128:(i + 1) * 128]
    sh["convp"] = convp
    qkg = np.empty((128, 128), np.float32)
    qkg[:, 0:64] = f("q_norm")[None, :]
    qkg[:, 64:128] = f("k_norm")[None, :]
    sh["qkg"] = qkg
    sh["identb"] = np.eye(128, dtype=np.float32).astype(ml_dtypes.bfloat16)
    sh["identf"] = np.eye(128, dtype=np.float32)
    t = np.arange(128)
    sh["cm"] = np.where(t[None, :] <= t[:, None], 0.0, NEG).astype(np.float32)
    em = np.zeros((128, 8, 128), np.float32)
    for half in range(2):
        for j in range(4):
            for tt in range(16):
                for grp in range(8):
                    em[half * 64 + j * 16 + tt, grp, grp * 16 + tt] = 1.0
    sh["emat"] = em.reshape(128, 1024).astype(ml_dtypes.bfloat16)
    inv = np.power(np.float32(500000.0), -np.arange(8, dtype=np.float32) * np.float32(2.0 / 16)).astype(np.float32)
    pos = np.arange(SEQ, dtype=np.float32)
    ang = (pos[:, None] * inv[None, :]).astype(np.float32)
    rp = np.empty((128, 2, 16, 8), np.float32)
    rp[:, 0] = np.cos(ang).reshape(16, 128, 8).transpose(1, 0, 2)
    rp[:, 1] = np.sin(ang).reshape(16, 128, 8).transpose(1, 0, 2)
    sh["ropep"] = rp
    poss = (16384 + np.arange(4)).astype(np.float32)
    angs = (poss[:, None] * inv[None, :]).astype(np.float32)
    rs = np.empty((16, 2, 8), np.float32)
    rs[:, 0] = np.tile(np.cos(angs), (4, 1))
    rs[:, 1] = np.tile(np.sin(angs), (4, 1))
    sh["ropes"] = rs
    cms = np.full((128, 4, 4), NEG, np.float32)
    for tp in range(4):
        for tt in range(4):
            if tp <= tt:
                cms[tp, :, tt] = 0.0
    sh["cms"] = cms
    sh["iotap"] = np.arange(128, dtype=np.float32).reshape(128, 1)
    sh["cpow"] = np.tile((2.0 ** -(np.arange(32, dtype=np.float64) + 1)).astype(np.float32)[None, :], (128, 1))
    selm = np.zeros((16, 8), np.float32)
    for b in range(4):
        for t in range(4):
            selm[b * 4 + t, t] = 1.0
            selm[b * 4 + t, 4 + b] = 1.0
    sh["selm"] = selm
    sh["ck"] = np.ascontiguousarray(np.asarray(inp["cache_k"], np.float32)[0].reshape(NPOOL, 128 * 128))
    sh["cv"] = np.ascontiguousarray(np.asarray(inp["cache_v"], np.float32)[0].reshape(NPOOL, 128 * 128))
    sh["cik"] = np.ascontiguousarray(np.asarray(inp["cache_idx_k"], np.float32)[0].reshape(NPOOL, 128 * 64))
    return sh


_PROG = {}


def kernel(**inp):
    stage = inp.pop("_stage", 3)
    if stage not in _PROG:
        _PROG[stage] = build_program(stage)
    nc = _PROG[stage]
    sh = _prep_shared(inp)
    if stage < 3:
        for k_ in ("ck", "cv", "cik"):
            sh[k_] = np.ascontiguousarray(sh[k_][:2])
        sh.pop("selm")
    x_prompt = np.asarray(inp["x_prompt"], np.float32)
    x_sample = np.asarray(inp["x_sample"], np.float32)
    cache_conv = np.asarray(inp["cache_conv"], np.float32)[0]
    page_table = np.asarray(inp["page_table"], np.int32)
    in_maps = []
    for c in range(NCORES):
        m = dict(sh)
        m["xp"] = np.ascontiguousarray(x_prompt[2 * c:2 * c + 2].reshape(2 * SEQ, D))
        m["xs"] = np.ascontiguousarray(x_sample[4 * c:4 * c + 4].reshape(16, D))
        cc = cache_conv[4 * c:4 * c + 4]
        m["cconv"] = np.ascontiguousarray(cc.reshape(4, 2, 4, 128).transpose(3, 2, 0, 1))
        m["ptT"] = np.ascontiguousarray(page_table[4 * c:4 * c + 4].T)
        in_maps.append(m)
    res = run_bass_kernel_spmd(nc, in_maps, core_ids=list(range(NCORES)))
    R = res.results
    cat = lambda k: np.concatenate([np.asarray(r[k]) for r in R], axis=0)
    yp = cat("yp").reshape(16, SEQ, D)
    ys = cat("ys").reshape(32, 4, D)
    kp = cat("kp").reshape(1, 16, SEQ, 2, 64)
    vp = cat("vp").reshape(1, 16, SEQ, 2, 64)
    ip = cat("ip").reshape(1, 16, SEQ, 64)
    cpo = cat("cpo").reshape(16, 128, 4, 2)
    cp = np.ascontiguousarray(cpo.transpose(0, 3, 2, 1).reshape(1, 16, 2, 512))
    ks = cat("kso").reshape(1, 32, 4, 2, 64)
    vs = cat("vso").reshape(1, 32, 4, 2, 64)
    iss = cat("iso").reshape(1, 32, 4, 64)
    cso = np.stack([np.asarray(r["cso"]) for r in R], axis=0)
    cs = np.ascontiguousarray(cso.transpose(0, 3, 4, 2, 1).reshape(1, 32, 2, 512))
    f32 = lambda a: np.ascontiguousarray(a, dtype=np.float32)
    return tuple(f32(a) for a in (yp, ys, kp, vp, ip, cp, ks, vs, iss, cs))
```

```python
import os
import numpy as np
import ml_dtypes
from contextlib import ExitStack
import concourse.bass as bass
import concourse.mybir as mybir
from concourse.bass_utils import run_bass_kernel_spmd

F32 = mybir.dt.float32
BF16 = mybir.dt.bfloat16
I32 = mybir.dt.int32
AF = mybir.ActivationFunctionType
ALU = mybir.AluOpType
AX = mybir.AxisListType

NCORES = 8
D = 1024
DFF = 2816
SEQ = 2048
NPOOL = int(os.environ.get('K_NPOOL', '5120'))
PARTS = int(os.environ.get('K_PARTS', '99'))
SUB = int(os.environ.get('K_SUB', '99'))
EPS = 1e-6
NEG = -1.0e30
K_TOP = 256
N_BISECT = 16


class Buf:
    __slots__ = ("name", "w", "r")

    def __init__(self, name):
        self.name = name
        self.w = None
        self.r = []


class Op:
    __slots__ = ("eng", "idx", "fn", "dma", "waits", "lane", "lane_val", "signal", "rank")

    def __init__(self, eng, idx, fn, dma):
        self.eng = eng
        self.idx = idx
        self.fn = fn
        self.dma = dma
        self.waits = []
        self.lane = None
        self.lane_val = None
        self.signal = False
        self.rank = None


class Sched:
    ENGS = ("pe", "act", "dve", "pool", "sp")
    NLANES = {"sp": 12, "act": 8, "pool": 6}

    def __init__(self):
        self.ops = {e: [] for e in self.ENGS}
        self.maxw = {e: {f: -1 for f in self.ENGS} for e in self.ENGS}
        self.lane_last = {e: [None] * n for e, n in self.NLANES.items()}
        self.lane_cnt = {e: [0] * n for e, n in self.NLANES.items()}
        self.lane_rr = {e: 0 for e in self.NLANES}
        self.lane_waited = {e: {} for e in self.ENGS}

    def _add_wait(self, op, d):
        eng = op.eng
        if d.dma:
            key = (d.eng, d.lane)
            if self.lane_waited[eng].get(key, 0) >= d.lane_val:
                return
            self.lane_waited[eng][key] = d.lane_val
            op.waits.append(d)
        else:
            if self.maxw[eng][d.eng] >= d.idx:
                return
            self.maxw[eng][d.eng] = d.idx
            d.signal = True
            op.waits.append(d)

    def add(self, eng, fn, reads=(), writes=(), dma=False):
        op = Op(eng, len(self.ops[eng]), fn, dma)
        raw, other = [], []
        for b in reads:
            if b.w is not None:
                raw.append(b.w)
        for b in writes:
            if b.w is not None:
                other.append(b.w)
            other.extend(b.r)
        need = []
        for d in raw:
            if d.dma or dma or d.eng != eng or eng != "pe":
                need.append(d)
        for d in other:
            if d is op:
                continue
            if d.dma or dma or d.eng != eng:
                need.append(d)
        best = {}
        for d in need:
            key = (d.eng, d.lane) if d.dma else (d.eng, None)
            cur = best.get(key)
            val = d.lane_val if d.dma else d.idx
            if cur is None or val > (cur.lane_val if cur.dma else cur.idx):
                best[key] = d
        for d in best.values():
            self._add_wait(op, d)
        if dma:
            n = self.NLANES[eng]
            l = self.lane_rr[eng]
            self.lane_rr[eng] = (l + 1) % n
            prev = self.lane_last[eng][l]
            if prev is not None:
                self._add_wait(op, prev)
            self.lane_cnt[eng][l] += 16
            op.lane = l
            op.lane_val = self.lane_cnt[eng][l]
            self.lane_last[eng][l] = op
        for b in reads:
            b.r.append(op)
        for b in writes:
            b.w = op
            b.r = []
        self.ops[eng].append(op)
        return op

    def emit(self, nc, es):
        sems = {}
        for e in ("pe", "act", "dve", "pool"):
            sems[e] = es.enter_context(nc.semaphore("c_" + e))
        lsems = {}
        for e, n in self.NLANES.items():
            for l in range(n):
                lsems[(e, l)] = es.enter_context(nc.semaphore("l_%s%d" % (e, l)))
        for e in ("pe", "act", "dve", "pool"):
            k = 0
            for op in self.ops[e]:
                if (not op.dma) and op.signal:
                    k += 1
                    op.rank = k
        all_dma = [op for e in self.ENGS for op in self.ops[e] if op.dma]

        def run(engname, h):
            for op in self.ops[engname]:
                for d in op.waits:
                    if d.dma:
                        h.wait_ge(lsems[(d.eng, d.lane)], d.lane_val)
                    else:
                        h.wait_ge(sems[d.eng], d.rank)
                ins = op.fn(h)
                if op.dma:
                    ins.then_inc(lsems[(engname, op.lane)], 16)
                elif op.signal:
                    ins.then_inc(sems[engname], 1)
            if engname == "sp":
                for (e, l), s in lsems.items():
                    v = self.lane_cnt[e][l]
                    if v > 0:
                        h.wait_ge(s, v)

        with nc.Block() as block:
            @block.tensor
            def _(h):
                run("pe", h)

            @block.scalar
            def _(h):
                run("act", h)

            @block.vector
            def _(h):
                run("dve", h)

            @block.gpsimd
            def _(h):
                run("pool", h)

            @block.sync
            def _(h):
                run("sp", h)


class PsumAlloc:
    def __init__(self, banks):
        self.banks = banks
        self.busy = [False] * len(banks)
        self.rr = 0

    def get(self):
        n = len(self.banks)
        for k in range(n):
            i = (self.rr + k) % n
            if not self.busy[i]:
                self.busy[i] = True
                self.rr = (i + 1) % n
                return i
        raise RuntimeError("PSUM banks exhausted")

    def rel(self, i):
        self.busy[i] = False


def build_program(stage=3):
    nc = bass.Bass("TRN2", target_bir_lowering=False)
    S = Sched()
    es = ExitStack()

    def din(name, shape, dt=F32):
        return nc.dram_tensor(name, list(shape), dt, kind="ExternalInput").ap()

    def dout(name, shape, dt=F32):
        return nc.dram_tensor(name, list(shape), dt, kind="ExternalOutput").ap()

    xp = din("xp", [2 * SEQ, D])
    xs = din("xs", [16, D])
    npool = NPOOL if stage >= 3 else 2
    ck = din("ck", [npool, 128 * 128])
    cv = din("cv", [npool, 128 * 128])
    cik = din("cik", [npool, 128 * 64])
    cconv = din("cconv", [128, 4, 4, 2])
    ptT = din("ptT", [128, 4], I32)
    wf1 = din("wf1", [33, 128, 2048])
    wf2 = din("wf2", [33, 128, 2048])
    win = din("win", [12, 128, 2048])
    wout = din("wout", [4, 128, 2048])
    gains = din("gains", [128, 24])
    convp = din("convp", [128, 16])
    qkg = din("qkg", [128, 128])
    identb_d = din("identb", [128, 128], BF16)
    identf_d = din("identf", [128, 128])
    cm_d = din("cm", [128, 128])
    e_d = din("emat", [128, 1024], BF16)
    ropep = din("ropep", [128, 2, 16, 8])
    ropes = din("ropes", [16, 2, 8])
    cms_d = din("cms", [128, 4, 4])
    iota_d = din("iotap", [128, 1])
    cpow_d = din("cpow", [128, 32])

    yp = dout("yp", [2 * SEQ, D])
    ys = dout("ys", [16, D])
    kp = dout("kp", [2 * SEQ, 128])
    vp = dout("vp", [2 * SEQ, 128])
    ip = dout("ip", [2 * SEQ, 64])
    cpo = dout("cpo", [2, 128, 4, 2])
    kso = dout("kso", [16, 128])
    vso = dout("vso", [16, 128])
    iso = dout("iso", [16, 64])
    cso = dout("cso", [128, 4, 4, 2])

    wscr = {}
    for nm, npieces in (("wf1", 33), ("win", 12), ("wf2", 33)):
        t_ = nc.dram_tensor(nm + "_bf", [npieces, 128, 2048], BF16, kind="Internal").ap()
        wscr[nm] = (t_, [Buf("%s_bf%d" % (nm, i)) for i in range(npieces)])

    def sb(name, shape, dt=F32):
        t = es.enter_context(nc.sbuf_tensor(name, list(shape), dt))
        return t, Buf(name)

    X, bX = sb("X", [128, 4, 1024])
    bXs = [Buf("Xs%d" % i) for i in range(4)]
    XNT, bXNT = sb("XNT", [128, 8, 512], BF16)
    XSC, bXSC = sb("XSC", [128, 1024], BF16)
    SQJ, bSQJ = sb("SQJ", [128, 1024], BF16)
    SS, bSS = sb("SS", [128, 8])
    RS, bRS = sb("RS", [128, 8])
    NSTG = 2
    STG = [sb("STG%d" % i, [128, 2048]) for i in range(NSTG)]
    STGG = [Buf("STGG%d" % i) for i in range(NSTG)]
    STGH = [Buf("STGH%d" % i) for i in range(NSTG)]
    NWR = 4
    WR = [sb("WR%d" % i, [128, 2048], BF16) for i in range(NWR)]
    NWD = 4
    WD = [sb("WD%d" % i, [128, 2048], BF16) for i in range(NWD)]
    HT = [sb("HT%d" % i, [128, 512], BF16) for i in range(8)]
    ST = [sb("ST%d" % i, [128, 512]) for i in range(2)]
    WOUT, bWOUT = sb("WOUT", [128, 8, 1024], BF16)
    KT, bKT = sb("KT", [128, 2048], BF16)
    VE, bVE = sb("VE", [128, 16, 2, 65], BF16)
    KIT, bKIT = sb("KIT", [128, 2048], BF16)
    QT, bQT = sb("QT", [128, 4, 4, 128], BF16)
    QIT, bQIT = sb("QIT", [128, 4, 512], BF16)
    WI, bWI = sb("WI", [128, 4, 8])
    KVF, bKVF = sb("KVF", [128, 256])
    HC, bHC = sb("HC", [128, 512])
    UEXT, bUEXT = sb("UEXT", [128, 4, 520])
    ACC, bACC = sb("ACC", [128, 512])
    YCT, bYCT = sb("YCT", [128, 4, 512], BF16)
    GN, bGN = sb("GN", [128, 24])
    CVP, bCVP = sb("CVP", [128, 16])
    QKG, bQKG = sb("QKG", [128, 128])
    IDB, bIDB = sb("IDB", [128, 128], BF16)
    IDF, bIDF = sb("IDF", [128, 128])
    CM, bCM = sb("CM", [128, 128])
    EM, bEM = sb("EM", [128, 8, 128], BF16)
    ROPE, bROPE = sb("ROPE", [128, 2, 16, 8])
    ROPS, bROPS = sb("ROPS", [16, 2, 8])
    QIBD2 = [sb("QIBD%d" % i, [128, 8, 128], BF16) for i in range(2)]
    WIREP, bWIREP = sb("WIREP", [128, 128])
    WSEL2 = [sb("WSEL%d" % i, [128, 8, 128], BF16) for i in range(2)]
    DEN, bDEN = sb("DEN", [128, 8])
    WTS, bWTS = sb("WTS", [128, 128])
    MBIAS, bMB = sb("MBIAS", [128, 1])
    NEG1, bNEG1 = sb("NEG1", [128, 8])
    RR = [sb("RR%d" % i, [128, 512], BF16) for i in range(3)]
    ISB, bISB = sb("ISB", [128, 2048])
    JUNK, bJUNK = sb("JUNK", [128, 2048], BF16)
    MASKB, bMASKB = sb("MASKB", [128, 2048], BF16)
    MASKT, bMASKT = sb("MASKT", [128, 16, 128], BF16)
    PEXP = [sb("PEXP%d" % i, [128, 512], BF16) for i in range(2)]
    PTB = [sb("PTB%d" % i, [128, 512], BF16) for i in range(2)]
    OB, bOB = sb("OB", [128, 4, 2, 64], BF16)
    OT, bOT = sb("OT", [128, 4, 128], BF16)
    BS, bBS = sb("BS", [128, 16])
    HW, bHW = sb("HW", [128, 32])
    CPOW, bCPOW = sb("CPOW", [128, 32])
    MIDP, bMIDP = sb("MIDP", [128, 2])
    RDEN, bRDEN = sb("RDEN", [128, 8])

    banks = []
    for i in range(8):
        t = es.enter_context(nc.psum_tensor("PS%d" % i, [128, 512], F32))
        banks.append((t, Buf("PS%d" % i)))
    PA = PsumAlloc(banks)

    def pf(i):
        return banks[i][0]

    def pb(i):
        return banks[i][1]

    def pbf(i):
        return banks[i][0][:, :].bitcast(BF16)

    def dma(out, in_, reads, writes, eng="sp", **kw):
        return S.add(eng, lambda h: h.dma_start(out=out, in_=in_, **kw), reads, writes, dma=True)

    def act(out, in_, func, reads, writes, **kw):
        return S.add("act", lambda h: h.activation(out=out, in_=in_, func=func, **kw), reads, writes)

    def mm(out, lhsT, rhs, start, stop, reads, writes):
        return S.add("pe", lambda h: h.matmul(out, lhsT, rhs, start=start, stop=stop,
                                               skip_group_check=True), reads, writes)

    def tr(out, in_, ident, reads, writes):
        return S.add("pe", lambda h: h.transpose(out, in_, ident), reads, writes)

    def V(eng, name, reads, writes, *a, **kw):
        return S.add(eng, lambda h: getattr(h, name)(*a, **kw), reads, writes)

    dma(GN[:, :], gains, [], [bGN])
    dma(CVP[:, :], convp, [], [bCVP])
    dma(QKG[:, :], qkg, [], [bQKG])
    dma(IDB[:, :], identb_d, [], [bIDB])
    dma(IDF[:, :], identf_d, [], [bIDF])
    dma(CM[:, :], cm_d, [], [bCM])
    dma(EM[:, :, :], e_d.rearrange("p (g t) -> p g t", g=8), [], [bEM])
    dma(ROPE[:, :, :, :], ropep, [], [bROPE])
    dma(ROPS[:, :, :], ropes, [], [bROPS])
    dma(CPOW[:, :], cpow_d, [], [bCPOW])
    V("pool", "memset", [], [bVE], VE[:, :, :, :], 1.0)
    for i_ in range(2):
        V("pool", "memset", [], [QIBD2[i_][1]], QIBD2[i_][0][:, :, :], 0.0)
    V("pool", "memset", [], [bNEG1], NEG1[:, :], -1.0)
    V("pool", "memset", [], [bMB], MBIAS[:, :], -60000.0)
    V("pool", "memset", [], [bUEXT], UEXT[:, :, :], 0.0)

    stg_rr = [0]
    cast_rr = [0]

    cur_blk = [0]

    def load_piece(dram_piece, dst, dstbuf, gain_col=None, scr=None):
        if scr is not None and cur_blk[0] > 0:
            nm, pi = scr
            dma(dst, wscr[nm][0][pi], [wscr[nm][1][pi]], [dstbuf])
            return
        i = stg_rr[0]
        stg_rr[0] = (i + 1) % NSTG
        st, bst = STG[i]
        dma(st[:, :], dram_piece, [], [bst, STGG[i], STGH[i]])
        cast_rr[0] += 1
        if gain_col is None:
            V("dve", "tensor_copy", [bst, STGG[i], STGH[i]], [dstbuf], dst[:, :], st[:, :])
        else:
            g = GN[:, gain_col:gain_col + 8]
            V("dve" if cast_rr[0] % 3 != 0 else "pool", "tensor_tensor", [bst, STGG[i], STGH[i], bGN], [dstbuf],
              dst[:, :].rearrange("p (k n) -> p k n", k=8),
              st[:, :].rearrange("p (k n) -> p k n", k=8),
              g.unsqueeze(2).to_broadcast([128, 8, 256]), ALU.mult)
        if scr is not None:
            nm, pi = scr
            dma(wscr[nm][0][pi], dst, [dstbuf], [wscr[nm][1][pi]], eng="act")

    for i in range(4):
        load_piece(wout[i], WOUT[:, 2 * i:2 * i + 2, :].rearrange("p a n -> p (a n)"), bWOUT)

    wr_rr = [0]

    def next_wr():
        i = wr_rr[0]
        wr_rr[0] = (i + 1) % NWR
        return WR[i]

    SSN, bSSN = sb("SSN", [128, 4])
    RSN, bRSN = sb("RSN", [128, 4])

    def norm_block(nsub, pr):
        junk = ST[0][0][:, :].bitcast(BF16)
        bjunk = ST[0][1]
        for s in range(nsub):
            act(junk[:pr, :], X[:pr, s, :], AF.Square, [bXs[s]], [bjunk, bSSN], accum_out=SSN[:pr, s:s + 1])
        act(RSN[:pr, 0:nsub], SSN[:pr, 0:nsub], AF.Sqrt, [bSSN, bEPSB], [bRSN], scale=1.0 / D, bias=EPSB[:pr, 0:1])
        V("dve", "reciprocal", [bRSN], [bRSN], RSN[:pr, 0:nsub], RSN[:pr, 0:nsub])
        for s in range(nsub):
            xs_, bxs = (XSC, bXSC) if s % 2 == 0 else (SQJ, bSQJ)
            act(xs_[:pr, :], X[:pr, s, :], AF.Copy, [bXs[s], bRSN], [bxs], scale=RSN[:pr, s:s + 1])
            b = PA.get()
            for kc in range(8):
                tr(pbf(b)[:, kc * 128:kc * 128 + pr], xs_[:pr, kc * 128:(kc + 1) * 128], IDB[:pr, :pr],
                   [bxs, bIDB], [pb(b)])
            V("dve", "tensor_copy", [pb(b)], [bXNT], XNT[:, :, s * 128:s * 128 + pr],
              pbf(b).rearrange("p (k t) -> p k t", k=8)[:, :, 0:pr])
            PA.rel(b)

    EPSB, bEPSB = sb("EPSB", [128, 1])
    V("pool", "memset", [], [bEPSB], EPSB[:, :], EPS)

    THIRDS = [(0, 4), (4, 8), (8, 11)]

    def ffn_block(wf, gcol, nsub, pr, nb, wname, after_sub=None):
        for (g0, g1) in THIRDS:
            nch = (g1 - g0) * 2
            wds = []
            for gi in range(g0, g1):
                wg, bwg = next_wr()
                load_piece(wf[3 * gi], wg[:, :], bwg, gcol, (wname, 3 * gi))
                wu, bwu = next_wr()
                load_piece(wf[3 * gi + 1], wu[:, :], bwu, gcol, (wname, 3 * gi + 1))
                wd, bwd = WD[gi - g0]
                load_piece(wf[3 * gi + 2], wd[:, :], bwd, None, (wname, 3 * gi + 2))
                wds.append((wd, bwd))
                wg3 = wg[:, :].rearrange("p (k n) -> p k n", k=8)
                wu3 = wu[:, :].rearrange("p (k n) -> p k n", k=8)
                for c2 in range(2):
                    ci = (gi - g0) * 2 + c2
                    pg = PA.get()
                    pu = PA.get()
                    for kc in range(8):
                        mm(pf(pg)[:, :nb], wg3[:, kc, c2 * 128:(c2 + 1) * 128], XNT[:, kc, :nb],
                           kc == 0, kc == 7, [bwg, bXNT], [pb(pg)])
                    for kc in range(8):
                        mm(pf(pu)[:, :nb], wu3[:, kc, c2 * 128:(c2 + 1) * 128], XNT[:, kc, :nb],
                           kc == 0, kc == 7, [bwu, bXNT], [pb(pu)])
                    st, bst = ST[ci % 2]
                    act(st[:, :nb], pf(pg)[:, :nb], AF.Silu, [pb(pg)], [bst])
                    ht, bht = HT[ci]
                    V("dve", "tensor_tensor", [bst, pb(pu)], [bht], ht[:, :nb], st[:, :nb], pf(pu)[:, :nb],
                      ALU.mult)
                    PA.rel(pg)
                    PA.rel(pu)
            for s in range(nsub):
                for hh in range(2):
                    pd = PA.get()
                    for ci in range(nch):
                        wd, bwd = wds[ci // 2]
                        wd3 = wd[:, :].rearrange("p (c n) -> p c n", c=2)
                        ht, bht = HT[ci]
                        mm(pf(pd)[:pr, :], ht[:, s * 128:s * 128 + pr], wd3[:, ci % 2, hh * 512:(hh + 1) * 512],
                           ci == 0, ci == nch - 1, [bht, bwd], [pb(pd)])
                    V("dve", "scalar_tensor_tensor", [pb(pd), bXs[s]], [bXs[s]],
                      X[:pr, s, hh * 512:(hh + 1) * 512], pf(pd)[:pr, :], 0.5,
                      X[:pr, s, hh * 512:(hh + 1) * 512], ALU.mult, ALU.add)
                    PA.rel(pd)
                if after_sub is not None and (g0, g1) == THIRDS[-1]:
                    after_sub(s)

    class TS:
        pass

    TSETS = []
    for par in range(2):
        t_ = TS()
        t_.SSS16, t_.bSSS16 = sb("SSS16_%d" % par, [128, 16])
        t_.RSS16, t_.bRSS16 = sb("RSS16_%d" % par, [128, 16])
        TSETS.append(t_)

    def mix_in_block(blk, nsub, pr, nb, is_sample):
        q = blk // 4
        bi = blk % 4
        t0 = blk * 512
        s0 = bi * 512
        norm_block(nsub, pr)
        chunk_ps = {}
        slots = {}
        for j in range(6):
            w, bw = next_wr()
            load_piece(win[j], w[:, :], bw, 8, ("win", j))
            slots[j] = (w, bw)
            w3 = w[:, :].rearrange("p (k n) -> p k n", k=8)
            for c2 in range(2):
                cc = 2 * j + c2
                i, typ = cc // 3, cc % 3
                b = PA.get()
                for kc in range(8):
                    mm(pf(b)[:, :nb], w3[:, kc, c2 * 128:(c2 + 1) * 128], XNT[:, kc, :nb], kc == 0, kc == 7,
                       [bw, bXNT], [pb(b)])
                chunk_ps[(i, typ)] = b
                if typ == 2:
                    bgb, bgc, bhc = chunk_ps[(i, 0)], chunk_ps[(i, 1)], chunk_ps[(i, 2)]
                    act(HC[:, :nb], pf(bhc)[:, :nb], AF.Copy, [pb(bhc)], [bHC])
                    if not is_sample:
                        ue = UEXT[:, i, 2:2 + nb]
                        V("dve", "tensor_tensor", [pb(bgc), bHC], [bUEXT], ue, pf(bgc)[:, :nb], HC[:, :nb], ALU.mult)
                        u0 = UEXT[:, i, 0:nb]
                        u1 = UEXT[:, i, 1:1 + nb]
                        u2 = UEXT[:, i, 2:2 + nb]
                        acc = ACC[:, :nb]
                        gbp = pf(bgb)[:, :nb]
                        yo = YCT[:, i, :nb]
                    else:
                        ue4 = UEXT[:, i, 0:24].rearrange("p (b t) -> p b t", t=6)
                        V("dve", "tensor_tensor", [pb(bgc), bHC], [bUEXT], ue4[:, :, 2:6],
                          pf(bgc)[:, :nb].rearrange("p (b t) -> p b t", t=4),
                          HC[:, :nb].rearrange("p (b t) -> p b t", t=4), ALU.mult)
                        u0, u1, u2 = ue4[:, :, 0:4], ue4[:, :, 1:5], ue4[:, :, 2:6]
                        acc = ACC[:, :nb].rearrange("p (b t) -> p b t", t=4)
                        gbp = pf(bgb)[:, :nb].rearrange("p (b t) -> p b t", t=4)
                        yo = YCT[:, i, :nb].rearrange("p (b t) -> p b t", t=4)
                    V("dve", "tensor_scalar", [bUEXT, bCVP], [bACC], acc, u2, CVP[:, i * 4 + 2:i * 4 + 3],
                      CVP[:, i * 4 + 3:i * 4 + 4], ALU.mult, ALU.add)
                    V("dve", "scalar_tensor_tensor", [bUEXT, bACC, bCVP], [bACC], acc, u1,
                      CVP[:, i * 4 + 1:i * 4 + 2], acc, ALU.mult, ALU.add)
                    V("dve", "scalar_tensor_tensor", [bUEXT, bACC, bCVP], [bACC], acc, u0,
                      CVP[:, i * 4 + 0:i * 4 + 1], acc, ALU.mult, ALU.add)
                    V("dve", "tensor_tensor", [pb(bgb), bACC], [bYCT], yo, gbp, acc, ALU.mult)
                    for bb in (bgb, bgc, bhc):
                        PA.rel(bb)
                    if not is_sample:
                        if bi == 3:
                            dma(cpo[q, :, i, :], UEXT[:, i, 512:514], [bUEXT], [], eng="act")
                            V("pool", "memset", [bUEXT], [bUEXT], UEXT[:, i, 0:2], 0.0)
                        else:
                            V("pool", "tensor_copy", [bUEXT], [bUEXT], UEXT[:, i, 0:2], UEXT[:, i, 512:514])
                    else:
                        dma(cso[:, i, :, :], ue4[:, :, 4:6], [bUEXT], [])
        if not is_sample:
            cosb = lambda nh: ROPE[:pr, 0, bi * 4:bi * 4 + nsub, :].unsqueeze(2).to_broadcast([pr, nsub, nh, 8])
            sinb = lambda nh: ROPE[:pr, 1, bi * 4:bi * 4 + nsub, :].unsqueeze(2).to_broadcast([pr, nsub, nh, 8])
            brope = bROPE
        else:
            cosb = lambda nh: ROPS[:pr, 0, :].unsqueeze(1).unsqueeze(1).to_broadcast([pr, 1, nh, 8])
            sinb = lambda nh: ROPS[:pr, 1, :].unsqueeze(1).unsqueeze(1).to_broadcast([pr, 1, nh, 8])
            brope = bROPS
        nbank = (nsub + 1) // 2

        def rope_b(F, bF, G, bG, nh, width, off, bH):
            xv = F[:pr, 0:nsub * width].rearrange("p (s c) -> p s c", c=width)[:, :, off:off + nh * 64].rearrange(
                "p s (h d) -> p s h d", d=64)
            x1 = xv[:, :, :, 0:8]
            x2 = xv[:, :, :, 8:16]
            tt = [G[:pr, k * 128:k * 128 + nsub * nh * 8].rearrange("p (s h e) -> p s h e", s=nsub, e=8)
                  for k in range(4)]
            V("dve", "tensor_tensor", [bF, brope], [bG], tt[0], x1, cosb(nh), ALU.mult)
            V("dve", "tensor_tensor", [bF, brope], [bG], tt[1], x2, sinb(nh), ALU.mult)
            V("pool", "tensor_tensor", [bF, brope], [bH], tt[2], x2, cosb(nh), ALU.mult)
            V("pool", "tensor_tensor", [bF, brope], [bH], tt[3], x1, sinb(nh), ALU.mult)
            V("dve", "tensor_tensor", [bG, bH], [bF], x1, tt[0], tt[1], ALU.subtract)
            V("dve", "tensor_tensor", [bH, bG], [bF], x2, tt[2], tt[3], ALU.add)

        def tmA(j):
            w, bw = next_wr()
            load_piece(win[j], w[:, :], bw, 8, ("win", j))
            w3 = w[:, :].rearrange("p (k n) -> p k n", k=8)
            bk = [PA.get() for _ in range(nbank)]
            for s in range(nsub):
                b = bk[s // 2]
                off = (s % 2) * 256
                for kc in range(8):
                    mm(pf(b)[:pr, off:off + 256], XNT[:, kc, s * 128:s * 128 + pr], w3[:, kc, :], kc == 0, kc == 7,
                       [bw, bXNT], [pb(b)])
            return bk

        def tmB(j, bk):
            par = j % 2
            AR, bAR = STG[par]
            bGG = STGG[par]
            F, G = AR[:, 0:1024], AR[:, 1024:2048]
            BB, bBB = (XSC, bXSC) if par == 0 else (SQJ, bSQJ)
            SSB, bSSB = TSETS[par].SSS16, TSETS[par].bSSS16
            RSB, bRSB = TSETS[par].RSS16, TSETS[par].bRSS16
            ns2 = [min(2, nsub - 2 * q_) for q_ in range(nbank)]
            P3 = [pf(bk[q_])[:pr, 0:ns2[q_] * 256].rearrange("p (s c) -> p s c", c=256) for q_ in range(nbank)]
            F3 = F[:pr, 0:nsub * 256].rearrange("p (s c) -> p s c", c=256)
            F3b = [F[:pr, q_ * 512:q_ * 512 + ns2[q_] * 256].rearrange("p (s c) -> p s c", c=256) for q_ in range(nbank)]
            r0 = t0
            c0 = s0
            if j in (6, 7, 8):
                kw = 256 if j != 8 else 128
                nh = kw // 64
                gcol = 0 if j != 8 else 64
                for q_ in range(nbank):
                    act(G[:pr, q_ * 2 * kw:q_ * 2 * kw + ns2[q_] * kw].rearrange("p (s c) -> p s c", c=kw),
                        P3[q_][:, :, 0:kw], AF.Square, [pb(bk[q_])], [bGG])
                V("dve", "tensor_reduce", [bGG], [bSSB], SSB[:pr, 0:nsub * nh],
                  G[:pr, 0:nsub * kw].rearrange("p (h d) -> p h d", d=64), AX.X, ALU.add)
                act(RSB[:pr, 0:nsub * nh], SSB[:pr, 0:nsub * nh], AF.Sqrt, [bSSB, bEPSB], [bRSB], scale=1.0 / 64,
                    bias=EPSB[:pr, 0:1])
                V("dve", "reciprocal", [bRSB], [bRSB], RSB[:pr, 0:nsub * nh], RSB[:pr, 0:nsub * nh])
                for q_ in range(nbank):
                    V("dve", "tensor_tensor", [pb(bk[q_]), bRSB], [bAR],
                      F3b[q_][:, :, 0:kw].rearrange("p s (h d) -> p s h d", d=64),
                      P3[q_][:, :, 0:kw].rearrange("p s (h d) -> p s h d", d=64),
                      RSB[:pr, q_ * 2 * nh:q_ * 2 * nh + ns2[q_] * nh].rearrange("p (s h) -> p s h", h=nh).unsqueeze(
                          3).to_broadcast([pr, ns2[q_], nh, 64]), ALU.mult)
                fk = F3[:, :, 0:kw].rearrange("p s (h d) -> p s h d", d=64)
                V("dve", "tensor_tensor", [bAR, bQKG], [bAR], fk, fk,
                  QKG[:pr, gcol:gcol + 64].unsqueeze(1).unsqueeze(1).to_broadcast([pr, nsub, nh, 64]), ALU.mult)
                rope_b(F, bAR, G, bGG, nh, 256, 0, STGH[par])
            if j in (6, 7):
                act(BB[:pr, 0:nsub * 256], F[:pr, 0:nsub * 256], AF.Copy, [bAR], [bBB])
                b2 = PA.get()
                for s in range(nsub):
                    for rr in range(2):
                        tr(pbf(b2)[:, (s * 2 + rr) * 128:(s * 2 + rr) * 128 + pr],
                           BB[:pr, s * 256 + rr * 128:s * 256 + (rr + 1) * 128], IDB[:pr, :pr], [bBB, bIDB], [pb(b2)])
                r_ = (j - 6) * 2
                V("dve", "tensor_copy", [pb(b2)], [bQT], QT[:, 0:nsub, r_:r_ + 2, 0:pr],
                  pbf(b2)[:, 0:nsub * 256].rearrange("p (s r t) -> p s r t", r=2, t=128)[:, :, :, 0:pr])
                PA.rel(b2)
            elif j == 8:
                for q_ in range(nbank):
                    act(F3b[q_][:, :, 128:256], P3[q_][:, :, 128:256], AF.Copy, [pb(bk[q_])], [bAR])
                if not is_sample:
                    dma(kp[r0:r0 + nsub * 128, :].rearrange("(s p) c -> p s c", p=128), F3[:, :, 0:128], [bAR], [], eng="act")
                    dma(vp[r0:r0 + nsub * 128, :].rearrange("(s p) c -> p s c", p=128), F3[:, :, 128:256], [bAR], [], eng="act")
                    kt_dst, bkt = KT[:, c0:c0 + nsub * 128], bKT
                    V("dve", "tensor_copy", [bAR], [bVE], VE[:pr, c0 // 128:c0 // 128 + nsub, :, 0:64],
                      F3[:, :, 128:256].rearrange("p s (g d) -> p s g d", d=64))
                else:
                    dma(kso[0:pr, :], F[:pr, 0:128], [bAR], [], eng="act")
                    dma(vso[0:pr, :], F[:pr, 128:256], [bAR], [], eng="act")
                    kt_dst, bkt = KTS[:, 0:pr], bKTS
                    V("dve", "tensor_copy", [bAR], [bVES], VES[:pr, :, 0:64],
                      F[:pr, 128:256].rearrange("p (g d) -> p g d", d=64))
                    V("dve", "tensor_copy", [bAR], [bKVF], KVF[:pr, 128:256], F[:pr, 128:256])
                act(BB[:pr, 0:nsub * 128].rearrange("p (s c) -> p s c", c=128), F3[:, :, 0:128], AF.Copy, [bAR], [bBB])
                b2 = PA.get()
                for s in range(nsub):
                    tr(pbf(b2)[:, s * 128:s * 128 + pr], BB[:pr, s * 128:(s + 1) * 128], IDB[:pr, :pr],
                       [bBB, bIDB], [pb(b2)])
                if not is_sample:
                    V("dve", "tensor_copy", [pb(b2)], [bkt], kt_dst, pbf(b2)[:, 0:nsub * 128])
                else:
                    V("dve", "tensor_copy", [pb(b2)], [bkt], kt_dst, pbf(b2)[:, 0:pr])
                PA.rel(b2)
            elif j in (9, 10):
                for q_ in range(nbank):
                    act(F3b[q_][:, :, :], P3[q_][:, :, :], AF.Copy, [pb(bk[q_])], [bAR])
                rope_b(F, bAR, G, bGG, 4, 256, 0, STGH[par])
                act(BB[:pr, 0:nsub * 256], F[:pr, 0:nsub * 256], AF.Copy, [bAR], [bBB])
                if is_sample and stage >= 3:
                    for jl in range(2):
                        jj = (j - 9) * 2 + jl
                        for half in range(2):
                            hh_ = half * 4 + jj
                            src_ = F[:pr, jl * 128 + half * 64:jl * 128 + half * 64 + 64]
                            V("dve", "tensor_copy", [bAR], [bQID], QID[:pr, hh_, :, :],
                              src_.unsqueeze(1).to_broadcast([pr, 2, 64]))
                b2 = PA.get()
                for s in range(nsub):
                    for rr in range(2):
                        tr(pbf(b2)[:, (s * 2 + rr) * 128:(s * 2 + rr) * 128 + pr],
                           BB[:pr, s * 256 + rr * 128:s * 256 + (rr + 1) * 128], IDB[:pr, :pr], [bBB, bIDB], [pb(b2)])
                r_ = (j - 9) * 2
                V("dve", "tensor_copy", [pb(b2)], [bQIT],
                  QIT[:, r_:r_ + 2, 0:nsub * 128].rearrange("p r (s t) -> p s r t", t=128)[:, :, :, 0:pr],
                  pbf(b2)[:, 0:nsub * 256].rearrange("p (s r t) -> p s r t", r=2, t=128)[:, :, :, 0:pr])
                PA.rel(b2)
            else:
                for q_ in range(nbank):
                    act(F3b[q_][:, :, 0:64], P3[q_][:, :, 0:64], AF.Copy, [pb(bk[q_])], [bAR])
                    V("dve", "tensor_copy", [pb(bk[q_])], [bWI], WI[:pr, q_ * 2:q_ * 2 + ns2[q_], :], P3[q_][:, :, 64:72])
                rope_b(F, bAR, G, bGG, 1, 256, 0, STGH[par])
                if not is_sample:
                    dma(ip[r0:r0 + nsub * 128, :].rearrange("(s p) c -> p s c", p=128), F3[:, :, 0:64], [bAR], [], eng="act")
                    kit_dst, bkit = KIT[:, c0:c0 + nsub * 128], bKIT
                else:
                    dma(iso[0:pr, :], F[:pr, 0:64], [bAR], [], eng="act")
                    kit_dst, bkit = KITS[:, 0:pr], bKITS
                bb3 = BB[:pr, 0:nsub * 128].rearrange("p (s c) -> p s c", c=128)
                act(bb3[:, :, 0:64], F3[:, :, 0:64], AF.Copy, [bAR], [bBB])
                act(bb3[:, :, 64:128], F3[:, :, 0:64], AF.Copy, [bAR], [bBB])
                b2 = PA.get()
                for s in range(nsub):
                    tr(pbf(b2)[:, s * 128:s * 128 + pr], BB[:pr, s * 128:(s + 1) * 128], IDB[:pr, :pr],
                       [bBB, bIDB], [pb(b2)])
                if not is_sample:
                    V("dve", "tensor_copy", [pb(b2)], [bkit], kit_dst, pbf(b2)[:, 0:nsub * 128])
                else:
                    V("dve", "tensor_copy", [pb(b2)], [bkit], kit_dst, pbf(b2)[:, 0:pr])
                PA.rel(b2)
            for b in bk:
                PA.rel(b)

        ctx = tmA(6)
        for j in range(6, 12):
            nctx = tmA(j + 1) if j < 11 else None
            tmB(j, ctx)
            ctx = nctx


    KTS, bKTS = sb("KTS", [128, 16], BF16)
    KITS, bKITS = sb("KITS", [128, 16], BF16)
    VES, bVES = sb("VES", [16, 2, 65], BF16)
    V("pool", "memset", [], [bVES], VES[:, :, :], 1.0)

    def attn_prompt_block(blk):
        bi = blk % 4

        def geom(s):
            i = bi * 4 + s
            nk = i + 1
            return i, nk, nk * 128, (nk * 128 + 511) // 512

        def idx_prep(s):
            QIBD, bQIBD = QIBD2[s % 2]
            WSEL, bWSEL = WSEL2[s % 2]
            for half in range(2):
                srcv = QIT[half * 64:(half + 1) * 64, :, s * 128:(s + 1) * 128].rearrange(
                    "p j (g t) -> p g j t", t=16)
                dst = QIBD[half * 64:(half + 1) * 64, :, half * 64:(half + 1) * 64].rearrange(
                    "p g (j t) -> p g j t", t=16)
                V("pool", "tensor_copy", [bQIT], [bQIBD], dst, srcv)
            V("pool", "tensor_copy", [bWI], [bWIREP], WIREP[:, :].rearrange("p (h t) -> p h t", t=16),
              WI[:, s, :].unsqueeze(2).to_broadcast([128, 8, 16]))
            b = PA.get()
            tr(pf(b)[:, 0:128], WIREP[:, :], IDF[:, :], [bWIREP, bIDF], [pb(b)])
            act(WTS[:, :], pf(b)[:, 0:128], AF.Copy, [pb(b)], [bWTS])
            PA.rel(b)
            V("pool", "tensor_tensor", [bEM, bWTS], [bWSEL], WSEL[:, :, :], EM[:, :, :],
              WTS[:, :].unsqueeze(1).to_broadcast([128, 8, 128]), ALU.mult)

        def idx_compute(s):
            i, nk, Sk, npc = geom(s)
            QIBD, bQIBD = QIBD2[s % 2]
            WSEL, bWSEL = WSEL2[s % 2]
            held = []
            items = [(p, grp) for p in range(npc) for grp in range(8)]
            bIs = {}

            def score(k):
                p, grp = items[k]
                w = min(512, Sk - p * 512)
                bs = PA.get()
                mm(pf(bs)[:, :w], QIBD[:, grp, :], KIT[:, p * 512:p * 512 + w], True, True,
                   [bQIBD, bKIT], [pb(bs)])
                return bs

            nxt_bs = score(0)
            for k, (p, grp) in enumerate(items):
                w = min(512, Sk - p * 512)
                bs = nxt_bs
                if k + 1 < len(items):
                    nxt_bs = score(k + 1)
                if grp == 0:
                    bIs[p] = PA.get()
                bI = bIs[p]
                rr, brr = RR[k % 3]
                act(rr[:, :w], pf(bs)[:, :w], AF.Relu, [pb(bs)], [brr])
                PA.rel(bs)
                mm(pf(bI)[:, :w], WSEL[:, grp, :], rr[:, :w], grp == 0, grp == 7, [bWSEL, brr], [pb(bI)])
                if grp == 7:
                    held.append((bI, p, w))
            return held

        def idx_evac(s, held):
            i, nk, Sk, npc = geom(s)
            for (bI, p, w) in held:
                act(ISB[:, p * 512:p * 512 + w], pf(bI)[:, :w], AF.Copy, [pb(bI)], [bISB])
                PA.rel(bI)

        def bisect(s):
            i, nk, Sk, npc = geom(s)
            if i >= 2:
                V("dve", "tensor_reduce", [bISB], [bBS], BS[:, 0:1], ISB[:, :Sk], AX.X, ALU.max,
                  apply_absolute_value=True)
                V("dve", "tensor_scalar", [bBS], [bBS], BS[:, 1:2], BS[:, 0:1], 2.0, 2.0, ALU.mult, ALU.add)
                V("dve", "tensor_scalar", [bBS, bCPOW], [bHW], HW[:, :], CPOW[:, :], BS[:, 1:2], None, ALU.mult)
                V("dve", "memset", [], [bMIDP], MIDP[:, :], 0.0)
            V("dve", "tensor_tensor", [bISB, bCM], [bISB], ISB[:, i * 128:(i + 1) * 128],
              ISB[:, i * 128:(i + 1) * 128], CM[:, :], ALU.add)
            if i >= 2:
                for n in range(N_BISECT):
                    V("dve", "tensor_scalar", [bISB, bMIDP], [bJUNK, bBS], JUNK[:, :Sk], ISB[:, :Sk], MIDP[:, 0:1],
                      None, ALU.is_ge, ALU.add, accum_out=BS[:, 4:5])
                    V("dve", "scalar_tensor_tensor", [bBS, bHW], [bBS], BS[:, 5:6], BS[:, 4:5], K_TOP - 0.5,
                      HW[:, n:n + 1], ALU.is_ge, ALU.mult)
                    V("dve", "scalar_tensor_tensor", [bMIDP, bHW, bBS], [bMIDP], MIDP[:, 0:1], MIDP[:, 0:1],
                      HW[:, n + 1:n + 2], BS[:, 5:6], ALU.subtract, ALU.add)
                V("dve", "tensor_scalar", [bMIDP, bHW], [bBS], BS[:, 2:3], MIDP[:, 0:1], HW[:, N_BISECT:N_BISECT + 1],
                  None, ALU.subtract)
                V("dve", "tensor_scalar", [bISB, bBS], [bMASKB], MASKB[:, :Sk], ISB[:, :Sk], BS[:, 2:3], None,
                  ALU.is_ge)
            else:
                V("dve", "tensor_scalar", [bISB], [bMASKB], MASKB[:, :Sk], ISB[:, :Sk], -1.0e29, None, ALU.is_ge)

        def masktrans(s):
            i, nk, Sk, npc = geom(s)
            for c0 in range(0, nk, 8):
                n = min(8, nk - c0)
                b = PA.get()
                for c in range(c0, c0 + n):
                    tr(pbf(b)[:, (c - c0) * 128:(c - c0 + 1) * 128], MASKB[:, c * 128:(c + 1) * 128], IDB[:, :],
                       [bMASKB, bIDB], [pb(b)])
                act(MASKT[:, c0:c0 + n, :], pbf(b)[:, 0:n * 128].rearrange("p (c t) -> p c t", t=128), AF.Identity,
                    [pb(b), bMB], [bMASKT], scale=60000.0, bias=MBIAS[:, 0:1])
                PA.rel(b)

        def attend(s):
            i, nk, Sk, npc = geom(s)
            bO = [PA.get(), PA.get()]
            aitems = [(c, g) for c in range(nk) for g in range(2)]

            def qk(k):
                c, g = aitems[k]
                bs = PA.get()
                mm(pf(bs)[:, :], KT[g * 64:(g + 1) * 64, c * 128:(c + 1) * 128],
                   QT[g * 64:(g + 1) * 64, s, :, :].rearrange("p r t -> p (r t)"), True, False,
                   [bKT, bQT], [pb(bs)])
                for r in range(4):
                    mm(pf(bs)[:, r * 128:(r + 1) * 128], IDB[:, :], MASKT[:, c, :], False, r == 3,
                       [bIDB, bMASKT], [pb(bs)])
                return bs

            nxt_bs = qk(0)
            for k, (c, g) in enumerate(aitems):
                bs = nxt_bs
                if k + 1 < len(aitems):
                    nxt_bs = qk(k + 1)
                pt_, bpt = (PEXP + PTB)[k % 4]
                act(pt_[:, :], pf(bs)[:, :], AF.Exp, [pb(bs)], [bpt], scale=0.125)
                PA.rel(bs)
                for r in range(4):
                    mm(pf(bO[g])[:, r * 65:(r + 1) * 65], pt_[:, r * 128:(r + 1) * 128], VE[:, c, g, :],
                       c == 0 and r == 0, c == nk - 1 and r == 3, [bpt, bVE], [pb(bO[g])])
            for g in range(2):
                o3 = pf(bO[g])[:, 0:260].rearrange("p (r e) -> p r e", e=65)
                act(DEN[:, g * 4:(g + 1) * 4].unsqueeze(2), o3[:, :, 64:65], AF.Copy, [pb(bO[g])], [bDEN])
            V("pool", "tensor_tensor", [bDEN, bNEG1], [bRDEN], RDEN[:, :], DEN[:, :], NEG1[:, :], ALU.pow)
            for g in range(2):
                o3 = pf(bO[g])[:, 0:260].rearrange("p (r e) -> p r e", e=65)
                for r in range(4):
                    act(OB[:, r, g, :], o3[:, r, 0:64], AF.Copy, [pb(bO[g]), bRDEN], [bOB],
                        scale=RDEN[:, g * 4 + r:g * 4 + r + 1])
                PA.rel(bO[g])
            b = PA.get()
            for r in range(4):
                tr(pbf(b)[:, r * 128:(r + 1) * 128], OB[:, r, :, :].rearrange("p g d -> p (g d)"), IDB[:, :],
                   [bOB, bIDB], [pb(b)])
            act(OT[:, :, :], pbf(b)[:, 0:512].rearrange("p (r t) -> p r t", r=4), AF.Copy, [pb(b)], [bOT])
            PA.rel(b)
            w_out_add(s, 128, True)

        idx_prep(0)
        held = idx_compute(0)
        idx_evac(0, held)
        idx_prep(1)
        bisect(0)
        held = idx_compute(1)
        masktrans(0)
        idx_evac(1, held)
        idx_prep(2)
        for s in range(4):
            held2 = idx_compute(s + 2) if s + 2 <= 3 else None
            if s + 1 <= 3:
                bisect(s + 1)
            attend(s)
            if held2 is not None:
                idx_evac(s + 2, held2)
                if s + 3 <= 3:
                    idx_prep(s + 3)
            if s + 1 <= 3:
                masktrans(s + 1)

    def w_out_add(s, pr, with_attn, ot_cols=None, otbuf=None):
        for hh in range(2):
            b = PA.get()
            nk8 = 8 if with_attn else 4
            for k8 in range(nk8):
                if k8 < 4:
                    lhs, rb = YCT[:, k8, s * 128:s * 128 + pr], bYCT
                else:
                    lhs, rb = (OT[:, k8 - 4, 0:pr] if ot_cols is None else ot_cols(k8 - 4)), (bOT if otbuf is None else otbuf)
                mm(pf(b)[:pr, :], lhs, WOUT[:, k8, hh * 512:(hh + 1) * 512], k8 == 0, k8 == nk8 - 1,
                   [rb, bWOUT], [pb(b)])
            if with_attn and pr == 128:
                tx, btx = ST[1]
                act(tx[:pr, :], pf(b)[:pr, :], AF.Copy, [pb(b)], [btx])
                V("pool", "tensor_tensor", [btx, bXs[s]], [bXs[s]], X[:pr, s, hh * 512:(hh + 1) * 512], tx[:pr, :],
                  X[:pr, s, hh * 512:(hh + 1) * 512], ALU.add)
            else:
                V("dve", "tensor_tensor", [pb(b), bXs[s]], [bXs[s]], X[:pr, s, hh * 512:(hh + 1) * 512], pf(b)[:pr, :],
                  X[:pr, s, hh * 512:(hh + 1) * 512], ALU.add)
            PA.rel(b)

    N_BIS_S = 16
    if stage >= 3:
        IT = X[:, 1:4, :].rearrange("p a n -> p (a n)")[:, 0:2064].rearrange("p (b k t) -> p b k t", b=4, k=129)
        bIT = bX
        CMPB = XNT[:, :, :].rearrange("p k n -> p (k n)")[:, 0:2064].rearrange("p (b k t) -> p b k t", b=4, k=129)
        bCMPB = bXNT
        MASKS, bMASKS = sb("MASKS", [128, 4, 129, 4], BF16)
        QID, bQID = sb("QID", [16, 8, 2, 64], BF16)
        QITD, bQITD = sb("QITD", [128, 4, 4, 8], BF16)
        QTSB, bQTSB = sb("QTSB", [128, 4, 4, 4], BF16)
        WM, bWM = sb("WM", [16, 16, 8])
        WB, bWB = sb("WB", [128, 128])
        ONESF, bONESF = sb("ONESF", [128, 128])
        PTSB, bPTSB = sb("PTSB", [128, 4], I32)
        PTF, bPTF = sb("PTF", [128, 4])
        IDX, bIDX = sb("IDX", [128, 4, 8], I32)
        IDX4, bIDX4 = sb("IDX4", [128, 4, 4], I32)
        CMS, bCMS = sb("CMS", [128, 4, 4])
        SELM, bSELM = sb("SELM", [16, 8])
        VMK, bVMK = sb("VMK", [16, 4, 128])
        VESB, bVESB = sb("VESB", [4, 4, 2, 65], BF16)
        LO, bLO = sb("LO", [128, 16])
        MID, bMID = sb("MID", [128, 16])
        PC, bPC = sb("PC", [128, 16])
        T1, bT1 = sb("T1", [128, 16])
        AM, bAM = sb("AM", [128, 4])
        OBS, bOBS = sb("OBS", [16, 2, 64], BF16)
        OTS, bOTS = sb("OTS", [128, 4, 16], BF16)
        RDS, bRDS = sb("RDS", [16, 2])
        KTP2, bKTP2 = sb("KTP2", [128, 2048], BF16)
        VE2, bVE2 = sb("VE2", [128, 16, 2, 65], BF16)
        V("pool", "memset", [], [bVE2], VE2[:, :, :, :], 1.0)
        ck8 = ck.rearrange("n (e x) -> (n e) x", e=8)
        cv8 = cv.rearrange("n (e x) -> (n e) x", e=8)
        cik4 = cik.rearrange("n (e x) -> (n e) x", e=4)
        selm_d = din("selm", [16, 8])
        dma(PTSB[:, :], ptT, [], [bPTSB])
        dma(CMS[:, :, :], cms_d, [], [bCMS])
        dma(SELM[:, :], selm_d, [], [bSELM])
        V("pool", "memset", [], [bONESF], ONESF[:, :], 1.0)
        V("pool", "memset", [], [bVESB], VESB[:, :, :, :], 1.0)

    _bc = {}

    def bc_reg(h, val):
        if val not in _bc:
            _bc[val] = h.to_reg(val)
        return _bc[val]

    def attn_sample_block():
        V("pool", "memset", [], [bIT, bXs[1], bXs[2], bXs[3]], IT[:, :, :, :], NEG)
        V("dve", "tensor_copy", [bPTSB], [bPTF], PTF[:, :], PTSB[:, :])
        for e in range(8):
            V("dve", "tensor_scalar", [bPTF], [bIDX], IDX[:, :, e], PTF[:, :], 8.0, float(e), ALU.mult, ALU.add)
        for e in range(4):
            V("dve", "tensor_scalar", [bPTF], [bIDX4], IDX4[:, :, e], PTF[:, :], 4.0, float(e), ALU.mult, ALU.add)
        b0 = PA.get()
        for h in range(8):
            tr(pbf(b0)[:, h * 16:(h + 1) * 16], QID[:, h, :, :].rearrange("p a d -> p (a d)"), IDB[:16, :16],
               [bQID, bIDB], [pb(b0)])
        V("dve", "tensor_copy", [pb(b0)], [bQITD], QITD[:, :, :, :].rearrange("p b t h -> p h b t"),
          pbf(b0)[:, 0:128].rearrange("p (h b t) -> p h b t", h=8, b=4))
        PA.rel(b0)
        V("dve", "tensor_copy", [bQT], [bQTSB], QTSB[:, :, :, :].rearrange("p b r t -> p r b t"),
          QT[:, 0, :, 0:16].rearrange("p r (b t) -> p r b t", t=4))
        V("dve", "tensor_tensor", [bWI, bIDF], [bWM], WM[:, :, :], WI[:16, 0, :].unsqueeze(1).to_broadcast([16, 16, 8]),
          IDF[:16, :16].unsqueeze(2).to_broadcast([16, 16, 8]), ALU.mult)
        b0 = PA.get()
        mm(pf(b0)[:, 0:128], ONESF[:16, :], WM[:, :, :].rearrange("p a h -> p (a h)"), True, True, [bONESF, bWM], [pb(b0)])
        act(WB[:, :], pf(b0)[:, 0:128], AF.Copy, [pb(b0)], [bWB])
        PA.rel(b0)
        V("dve", "tensor_tensor", [bKVF, bSELM], [bVMK], VMK[:, :, :], KVF[:16, 128:256].unsqueeze(1).to_broadcast([16, 4, 128]),
          SELM[:, 4:8].unsqueeze(2).to_broadcast([16, 4, 128]), ALU.mult)
        b0 = PA.get()
        mm(pf(b0)[0:4, :], SELM[:, 0:4], VMK[:, :, :].rearrange("p b x -> p (b x)"), True, True, [bSELM, bVMK], [pb(b0)])
        V("dve", "tensor_copy", [pb(b0)], [bVESB], VESB[:, :, :, 0:64],
          pf(b0)[0:4, :].rearrange("p (b g d) -> p b g d", b=4, g=2))
        PA.rel(b0)
        if PARTS < 1:
            return w_out_add(0, 16, False)
        IKP, bIKP = ISB, bISB
        IKB, bIKB = JUNK, bJUNK
        KITP, bKITP = MASKB, bMASKB
        for b in range(4):
            for q4 in range(4):
                S.add("pool", lambda h, b=b, q4=q4: h.indirect_dma_start(
                    out=IKP[:, :], out_offset=None, in_=cik4,
                    in_offset=bass.IndirectOffsetOnAxis(ap=IDX4[:, b, q4:q4 + 1], axis=0),
                    bounds_check=bc_reg(h, npool * 4 - 1), oob_is_err=False),
                    [bIDX4], [bIKP], dma=True)
                act(IKB[:, :], IKP[:, :], AF.Copy, [bIKP], [bIKB])
                if SUB < 1:
                    continue
                for c0 in range(0, 16, 8):
                    bb = PA.get()
                    for pp in range(8):
                        tr(pbf(bb)[:, pp * 128:(pp + 1) * 128], IKB[:, (c0 + pp) * 128:(c0 + pp + 1) * 128], IDB[:, :],
                           [bIKB, bIDB], [pb(bb)])
                    V("dve", "tensor_copy", [pb(bb)], [bKITP], KITP[:, c0 * 128:(c0 + 8) * 128], pbf(bb)[:, :])
                    PA.rel(bb)
                if SUB < 2:
                    continue
                bsh = [PA.get(), PA.get()]
                for kk in range(32):
                    half, m = kk % 2, kk // 2
                    mm(pf(bsh[half])[:, m * 32:(m + 1) * 32], KITP[half * 64:(half + 1) * 64, m * 128:(m + 1) * 128],
                       QITD[half * 64:(half + 1) * 64, b, :, :].rearrange("p t h -> p (t h)"), True, True,
                       [bKITP, bQITD], [pb(bsh[half])])
                for half in range(2):
                    rr, brr = RR[half]
                    act(rr[:, :], pf(bsh[half])[:, :], AF.Relu, [pb(bsh[half])], [brr])
                    PA.rel(bsh[half])
                    if SUB < 3:
                        continue
                    st, bst = ST[half]
                    V("dve", "tensor_tensor", [brr, bWB], [bst], st[:, :].rearrange("p (k x) -> p k x", k=16),
                      rr[:, :].rearrange("p (k x) -> p k x", k=16),
                      WB[:, b * 32:(b + 1) * 32].unsqueeze(1).to_broadcast([128, 16, 32]), ALU.mult)
                    V("dve", "tensor_reduce", [bst], [bIT],
                      IT[:, b, q4 * 32:(q4 + 1) * 32, :].rearrange("p (m two) t -> p m two t", two=2)[:, :, half, :],
                      st[:, :].rearrange("p (k t h) -> p k t h", k=16, t=4), AX.X, ALU.add)
            if SUB < 4:
                continue
            bs = PA.get()
            mm(pf(bs)[0:4, 0:32], KITS[0:64, b * 4:(b + 1) * 4], QITD[0:64, b, :, :].rearrange("p t h -> p (t h)"),
               True, True, [bKITS, bQITD], [pb(bs)])
            rr, brr = RR[2]
            act(rr[0:4, 0:32], pf(bs)[0:4, 0:32], AF.Relu, [pb(bs)], [brr])
            PA.rel(bs)
            st, bst = ST[0]
            V("dve", "tensor_tensor", [brr, bWB], [bst], st[0:4, 0:32], rr[0:4, 0:32], WB[0:4, b * 32:(b + 1) * 32], ALU.mult)
            V("dve", "tensor_reduce", [bst], [bAM], AM[0:4, 0:4], st[0:4, 0:32].rearrange("p (t h) -> p t h", t=4),
              AX.X, ALU.add)
            V("dve", "tensor_tensor", [bAM, bCMS], [bIT], IT[0:4, b, 128, :], AM[0:4, 0:4], CMS[0:4, b, :], ALU.add)
        if PARTS < 2:
            return w_out_add(0, 16, False)
        V("dve", "tensor_reduce", [bIT], [bAM], AM[:, 0:1], IT[:, :, 0:128, :], AX.XYZ, ALU.max, apply_absolute_value=True)
        b0 = PA.get()
        tr(pf(b0)[0:1, 0:128], AM[:, 0:1], IDF[:, :], [bAM, bIDF], [pb(b0)])
        V("dve", "tensor_reduce", [pb(b0)], [bAM], AM[0:1, 1:2], pf(b0)[0:1, 0:128], AX.X, ALU.max)
        PA.rel(b0)
        b0 = PA.get()
        mm(pf(b0)[:, 0:2], ONESF[0:1, :], AM[0:1, 1:3], True, True, [bONESF, bAM], [pb(b0)])
        V("dve", "tensor_scalar", [pb(b0)], [bAM], AM[:, 2:3], pf(b0)[:, 0:1], 2.0, 2.0, ALU.mult, ALU.add)
        V("dve", "tensor_scalar", [pb(b0)], [bAM], AM[:, 3:4], pf(b0)[:, 0:1], -1.0, -1.0, ALU.mult, ALU.add)
        PA.rel(b0)
        V("dve", "tensor_copy", [bAM], [bLO], LO[:, :], AM[:, 3:4].to_broadcast([128, 16]))
        if PARTS < 3:
            return w_out_add(0, 16, False)
        itv = IT[:, :, :, :].rearrange("p b k t -> p b t k")
        cmv = CMPB[:, :, :, :].rearrange("p b k t -> p b t k")
        for n in range(N_BIS_S):
            c = 2.0 ** -(n + 1)
            V("dve", "scalar_tensor_tensor", [bAM, bLO], [bMID], MID[:, :], AM[:, 2:3].to_broadcast([128, 16]), c, LO[:, :],
              ALU.mult, ALU.add)
            V("dve", "tensor_tensor", [bIT, bMID], [bCMPB], cmv, itv,
              MID[:, :].rearrange("p (b t) -> p b t", t=4).unsqueeze(3).to_broadcast([128, 4, 4, 129]), ALU.is_ge)
            V("dve", "tensor_reduce", [bCMPB], [bPC], PC[:, :].rearrange("p (b t) -> p b t", t=4), cmv, AX.X, ALU.add)
            b0 = PA.get()
            mm(pf(b0)[:, 0:16], ONESF[:, :], PC[:, :], True, True, [bONESF, bPC], [pb(b0)])
            V("dve", "tensor_scalar", [pb(b0)], [bT1], T1[:, :], pf(b0)[:, 0:16], K_TOP - 0.5, c, ALU.is_ge, ALU.mult)
            PA.rel(b0)
            V("dve", "scalar_tensor_tensor", [bT1, bAM, bLO], [bLO], LO[:, :], T1[:, :], AM[:, 2:3], LO[:, :], ALU.mult, ALU.add)
        V("dve", "tensor_tensor", [bIT, bLO], [bMASKS], MASKS[:, :, :, :].rearrange("p b k t -> p b t k"), itv,
          LO[:, :].rearrange("p (b t) -> p b t", t=4).unsqueeze(3).to_broadcast([128, 4, 4, 129]), ALU.is_ge)
        if PARTS < 4:
            return w_out_add(0, 16, False)
        KPs = [(ISB, [bISB]), (STG[1][0], [STG[1][1], STGG[1], STGH[1]])]
        VPs = [(STG[0][0], [STG[0][1], STGG[0], STGH[0]]), (XNT[:, :, :].rearrange("p k n -> p (k n)").bitcast(F32), [bXNT])]
        KPBs = [(JUNK, bJUNK), (MASKT[:, :, :].rearrange("p c t -> p (c t)"), bMASKT)]
        KTPs = [(MASKB, bMASKB), (KTP2, bKTP2)]
        VEs = [(VE, bVE), (VE2, bVE2)]
        steps = [(b, e) for b in range(4) for e in range(8)]

        def gather(it):
            b, e = steps[it]
            KP, bKP = KPs[it % 2]
            VP, bVP = VPs[it % 2]
            S.add("pool", lambda h, b=b, e=e, KP=KP: h.indirect_dma_start(
                out=KP[:, :], out_offset=None, in_=ck8,
                in_offset=bass.IndirectOffsetOnAxis(ap=IDX[:, b, e:e + 1], axis=0),
                bounds_check=bc_reg(h, npool * 8 - 1), oob_is_err=False),
                [bIDX], bKP, dma=True)
            S.add("pool", lambda h, b=b, e=e, VP=VP: h.indirect_dma_start(
                out=VP[:, :], out_offset=None, in_=cv8,
                in_offset=bass.IndirectOffsetOnAxis(ap=IDX[:, b, e:e + 1], axis=0),
                bounds_check=bc_reg(h, npool * 8 - 1), oob_is_err=False),
                [bIDX], bVP, dma=True)

        def cast(it):
            KP, bKP = KPs[it % 2]
            VP, bVP = VPs[it % 2]
            KPB, bKPB = KPBs[it % 2]
            VEx, bVEx = VEs[it % 2]
            act(KPB[:, :], KP[:, :], AF.Copy, bKP, [bKPB])
            act(VEx[:, :, :, 0:64], VP[:, :].rearrange("p (k g d) -> p k g d", k=16, g=2), AF.Copy, bVP, [bVEx])

        state = {}

        def compute(it):
            b, e = steps[it]
            KPB, bKPB = KPBs[it % 2]
            KTP, bKTP = KTPs[it % 2]
            VEx, bVEx = VEs[it % 2]
            if e == 0:
                state["bO"] = PA.get()
                state["first"] = True
            bO = state["bO"]
            for c0 in range(0, 16, 8):
                bb = PA.get()
                for pp in range(8):
                    tr(pbf(bb)[:, pp * 128:(pp + 1) * 128], KPB[:, (c0 + pp) * 128:(c0 + pp + 1) * 128], IDB[:, :],
                       [bKPB, bIDB], [pb(bb)])
                V("dve", "tensor_copy", [pb(bb)], [bKTP], KTP[:, c0 * 128:(c0 + 8) * 128], pbf(bb)[:, :])
                PA.rel(bb)
            bsg = [PA.get(), PA.get()]
            for kl in range(16):
                for g in range(2):
                    mm(pf(bsg[g])[:, kl * 16:(kl + 1) * 16],
                       KTP[g * 64:(g + 1) * 64, kl * 128:(kl + 1) * 128],
                       QTSB[g * 64:(g + 1) * 64, b, :, :].rearrange("p r t -> p (r t)"), True, True,
                       [bKTP, bQTSB], [pb(bsg[g])])
            for g in range(2):
                pe_, bpe = PEXP[g]
                act(pe_[:, 0:256], pf(bsg[g])[:, 0:256], AF.Exp, [pb(bsg[g])], [bpe], scale=0.125)
                PA.rel(bsg[g])
                pt_, bpt = PTB[g]
                V("pool", "tensor_tensor", [bpe, bMASKS], [bpt],
                  pt_[:, 0:256].rearrange("p (k r t) -> p k r t", k=16, t=4),
                  pe_[:, 0:256].rearrange("p (k r t) -> p k r t", k=16, t=4),
                  MASKS[:, b, e * 16:(e + 1) * 16, :].unsqueeze(2).to_broadcast([128, 16, 4, 4]), ALU.mult)
            for kl in range(16):
                for g in range(2):
                    pt_, bpt = PTB[g]
                    mm(pf(bO)[0:16, g * 65:(g + 1) * 65], pt_[:, kl * 16:(kl + 1) * 16],
                       VEx[:, kl, g, :], state["first"], False, [bpt, bVEx], [pb(bO)])
                    state["first"] = False
            if e == 7:
                finish_seq(b, bO)

        def finish_seq(b, bO):
                for g in range(2):
                    bs = PA.get()
                    mm(pf(bs)[0:4, 0:16], KTS[g * 64:(g + 1) * 64, b * 4:(b + 1) * 4],
                       QTSB[g * 64:(g + 1) * 64, b, :, :].rearrange("p r t -> p (r t)"), True, True, [bKTS, bQTSB], [pb(bs)])
                    pe_, bpe = PEXP[g]
                    act(pe_[0:4, 0:16], pf(bs)[0:4, 0:16], AF.Exp, [pb(bs)], [bpe], scale=0.125)
                    PA.rel(bs)
                    pt_, bpt = PTB[g]
                    V("pool", "tensor_tensor", [bpe, bMASKS], [bpt], pt_[0:4, 0:16].rearrange("p (r t) -> p r t", t=4),
                      pe_[0:4, 0:16].rearrange("p (r t) -> p r t", t=4),
                      MASKS[0:4, b, 128, :].unsqueeze(1).to_broadcast([4, 4, 4]), ALU.mult)
                    mm(pf(bO)[0:16, g * 65:(g + 1) * 65], pt_[0:4, 0:16], VESB[0:4, b, g, :], False, g == 1,
                       [bpt, bVESB], [pb(bO)])
                o3 = pf(bO)[0:16, 0:130].rearrange("p (g e) -> p g e", e=65)
                V("dve", "reciprocal", [pb(bO)], [bRDS], RDS[:, :].unsqueeze(2), o3[:, :, 64:65])
                V("dve", "tensor_tensor", [pb(bO), bRDS], [bOBS], OBS[:, :, :], o3[:, :, 0:64],
                  RDS[:, :].unsqueeze(2).to_broadcast([16, 2, 64]), ALU.mult)
                PA.rel(bO)
                b0 = PA.get()
                tr(pbf(b0)[:, 0:16], OBS[:, :, :].rearrange("p g d -> p (g d)"), IDB[:16, :16], [bOBS, bIDB], [pb(b0)])
                V("dve", "tensor_copy", [pb(b0)], [bOTS], OTS[:, :, b * 4:(b + 1) * 4],
                  pbf(b0)[:, 0:16].rearrange("p (r t) -> p r t", t=4))
                PA.rel(b0)

        gather(0)
        cast(0)
        for it in range(len(steps)):
            if it + 1 < len(steps):
                gather(it + 1)
            compute(it)
            if it + 1 < len(steps):
                cast(it + 1)
        w_out_add(0, 16, True, ot_cols=lambda r: OTS[:, r, :], otbuf=bOTS)

    nblk_prompt = 8
    for blk in range(nblk_prompt + 1):
        is_sample = blk == nblk_prompt
        cur_blk[0] = blk
        if not is_sample:
            nsub, pr, nb = 4, 128, 512
            for s_ in range(4):
                dma(X[:, s_, :], xp[blk * 512 + s_ * 128:blk * 512 + (s_ + 1) * 128, :], [], [bXs[s_]], eng="act")
        else:
            nsub, pr, nb = 1, 16, 16
            dma(X[:16, 0, :], xs, [], [bXs[0]])
            for i4 in range(4):
                dma(UEXT[:, i4, 0:24].rearrange("p (b t) -> p b t", t=6)[:, :, 0:2], cconv[:, i4, :, :],
                    [bUEXT], [bUEXT])
        norm_block(nsub, pr)
        ffn_block(wf1, 0, nsub, pr, nb, "wf1")
        mix_in_block(blk, nsub, pr, nb, is_sample)
        if stage >= 2:
            if not is_sample:
                attn_prompt_block(blk)
            elif stage >= 3:
                attn_sample_block()
            else:
                w_out_add(0, 16, False)
            norm_block(nsub, pr)
            if not is_sample:
                def store_sub(s_, blk=blk):
                    dma(yp[blk * 512 + s_ * 128:blk * 512 + (s_ + 1) * 128, :], X[:, s_, :], [bXs[s_]], [], eng="act")
                ffn_block(wf2, 16, nsub, pr, nb, "wf2", after_sub=store_sub)
            else:
                ffn_block(wf2, 16, nsub, pr, nb, "wf2")
        if not is_sample:
            if stage < 2:
                for s_ in range(4):
                    dma(yp[blk * 512 + s_ * 128:blk * 512 + (s_ + 1) * 128, :], X[:, s_, :], [bXs[s_]], [], eng="act")
        else:
            dma(ys, X[:16, 0, :], [bXs[0]], [])

    S.emit(nc, es)
    es.close()
    return nc


def _pieces_kn(W, ncol_piece=256):
    K, N = W.shape
    nj = N // ncol_piece
    a = W.reshape(8, 128, nj, ncol_piece).transpose(2, 1, 0, 3)
    return np.ascontiguousarray(a.reshape(nj, 128, 8 * ncol_piece))


def _pieces_rows(W, rows_piece=256):
    R, N = W.shape
    ng = R // rows_piece
    a = W.reshape(ng, 2, 128, N).transpose(0, 2, 1, 3)
    return np.ascontiguousarray(a.reshape(ng, 128, 2 * N))


def _ffn_pieces(wg, wu, wd):
    pg = _pieces_kn(wg)
    pu = _pieces_kn(wu)
    pd = _pieces_rows(wd)
    out = np.empty((33, 128, 2048), np.float32)
    out[0::3] = pg
    out[1::3] = pu
    out[2::3] = pd
    return out


def _perm_heads():
    idx = np.empty(512, np.int64)
    for r in range(4):
        for g in range(2):
            for d in range(64):
                idx[r * 128 + g * 64 + d] = (g * 4 + r) * 64 + d
    return idx


def _prep_shared(inp):
    f = lambda k: np.asarray(inp[k], np.float32)[0]
    sh = {}
    sh["wf1"] = _ffn_pieces(f("ffn1_w_gate"), f("ffn1_w_up"), f("ffn1_w_down"))
    sh["wf2"] = _ffn_pieces(f("ffn2_w_gate"), f("ffn2_w_up"), f("ffn2_w_down"))
    w_in = f("w_in")
    ph = _perm_heads()
    gb, gc, hc = w_in[:, 0:512], w_in[:, 512:1024], w_in[:, 1024:1536]
    q = w_in[:, 1536:2048][:, ph]
    k = w_in[:, 2048:2176]
    v = w_in[:, 2176:2304]
    qi = w_in[:, 2304:2816][:, ph]
    ki = w_in[:, 2816:2880]
    wi = w_in[:, 2880:2888]
    conv_cols = []
    for i in range(4):
        for t in (gb, gc, hc):
            conv_cols.append(t[:, i * 128:(i + 1) * 128])
    pad = np.zeros((1024, 256 - 72), np.float32)
    wperm = np.concatenate(conv_cols + [q, k, v, qi, ki, wi, pad], axis=1)
    assert wperm.shape[1] == 3072
    sh["win"] = _pieces_kn(wperm)
    w_out = f("w_out")
    wo = np.empty((128, 8, 1024), np.float32)
    for i in range(4):
        wo[:, i, :] = w_out[i * 128:(i + 1) * 128, :]
    for r in range(4):
        for g in range(2):
            h = g * 4 + r
            wo[g * 64:(g + 1) * 64, 4 + r, :] = w_out[512 + h * 64:512 + (h + 1) * 64, :]
    sh["wout"] = np.ascontiguousarray(wo.reshape(128, 4, 2048).transpose(1, 0, 2))
    gains = np.empty((128, 24), np.float32)
    for n, key in enumerate(("ffn1_norm", "mix_norm", "ffn2_norm")):
        gains[:, n * 8:(n + 1) * 8] = f(key).reshape(8, 128).T
    sh["gains"] = gains
    cw = f("conv_w")
    cb = f("conv_b")
    convp = np.empty((128, 16), np.float32)
    for i in range(4):
        for j in range(3):
            convp[:, i * 4 + j] = cw[j, i * 128:(i + 1) * 128]
        convp[:, i * 4 + 3] = cb[i * 128:(i + 1) * 128]
    sh["convp"] = convp
    qkg = np.empty((128, 128), np.float32)
    qkg[:, 0:64] = f("q_norm")[None, :]
    qkg[:, 64:128] = f("k_norm")[None, :]
    sh["qkg"] = qkg
    sh["identb"] = np.eye(128, dtype=np.float32).astype(ml_dtypes.bfloat16)
    sh["identf"] = np.eye(128, dtype=np.float32)
    t = np.arange(128)
    sh["cm"] = np.where(t[None, :] <= t[:, None], 0.0, NEG).astype(np.float32)
    em = np.zeros((128, 8, 128), np.float32)
    for half in range(2):
        for j in range(4):
            for tt in range(16):
                for grp in range(8):
                    em[half * 64 + j * 16 + tt, grp, grp * 16 + tt] = 1.0
    sh["emat"] = em.reshape(128, 1024).astype(ml_dtypes.bfloat16)
    inv = np.power(np.float32(500000.0), -np.arange(8, dtype=np.float32) * np.float32(2.0 / 16)).astype(np.float32)
    pos = np.arange(SEQ, dtype=np.float32)
    ang = (pos[:, None] * inv[None, :]).astype(np.float32)
    rp = np.empty((128, 2, 16, 8), np.float32)
    rp[:, 0] = np.cos(ang).reshape(16, 128, 8).transpose(1, 0, 2)
    rp[:, 1] = np.sin(ang).reshape(16, 128, 8).transpose(1, 0, 2)
    sh["ropep"] = rp
    poss = (16384 + np.arange(4)).astype(np.float32)
    angs = (poss[:, None] * inv[None, :]).astype(np.float32)
    rs = np.empty((16, 2, 8), np.float32)
    rs[:, 0] = np.tile(np.cos(angs), (4, 1))
    rs[:, 1] = np.tile(np.sin(angs), (4, 1))
    sh["ropes"] = rs
    cms = np.full((128, 4, 4), NEG, np.float32)
    for tp in range(4):
        for tt in range(4):
            if tp <= tt:
                cms[tp, :, tt] = 0.0
    sh["cms"] = cms
    sh["iotap"] = np.arange(128, dtype=np.float32).reshape(128, 1)
    sh["cpow"] = np.tile((2.0 ** -(np.arange(32, dtype=np.float64) + 1)).astype(np.float32)[None, :], (128, 1))
    selm = np.zeros((16, 8), np.float32)
    for b in range(4):
        for t in range(4):
            selm[b * 4 + t, t] = 1.0
            selm[b * 4 + t, 4 + b] = 1.0
    sh["selm"] = selm
    sh["ck"] = np.ascontiguousarray(np.asarray(inp["cache_k"], np.float32)[0].reshape(NPOOL, 128 * 128))
    sh["cv"] = np.ascontiguousarray(np.asarray(inp["cache_v"], np.float32)[0].reshape(NPOOL, 128 * 128))
    sh["cik"] = np.ascontiguousarray(np.asarray(inp["cache_idx_k"], np.float32)[0].reshape(NPOOL, 128 * 64))
    return sh


_PROG = {}


def kernel(**inp):
    stage = inp.pop("_stage", 3)
    if stage not in _PROG:
        _PROG[stage] = build_program(stage)
    nc = _PROG[stage]
    sh = _prep_shared(inp)
    if stage < 3:
        for k_ in ("ck", "cv", "cik"):
            sh[k_] = np.ascontiguousarray(sh[k_][:2])
        sh.pop("selm")
    x_prompt = np.asarray(inp["x_prompt"], np.float32)
    x_sample = np.asarray(inp["x_sample"], np.float32)
    cache_conv = np.asarray(inp["cache_conv"], np.float32)[0]
    page_table = np.asarray(inp["page_table"], np.int32)
    in_maps = []
    for c in range(NCORES):
        m = dict(sh)
        m["xp"] = np.ascontiguousarray(x_prompt[2 * c:2 * c + 2].reshape(2 * SEQ, D))
        m["xs"] = np.ascontiguousarray(x_sample[4 * c:4 * c + 4].reshape(16, D))
        cc = cache_conv[4 * c:4 * c + 4]
        m["cconv"] = np.ascontiguousarray(cc.reshape(4, 2, 4, 128).transpose(3, 2, 0, 1))
        m["ptT"] = np.ascontiguousarray(page_table[4 * c:4 * c + 4].T)
        in_maps.append(m)
    res = run_bass_kernel_spmd(nc, in_maps, core_ids=list(range(NCORES)))
    R = res.results
    cat = lambda k: np.concatenate([np.asarray(r[k]) for r in R], axis=0)
    yp = cat("yp").reshape(16, SEQ, D)
    ys = cat("ys").reshape(32, 4, D)
    kp = cat("kp").reshape(1, 16, SEQ, 2, 64)
    vp = cat("vp").reshape(1, 16, SEQ, 2, 64)
    ip = cat("ip").reshape(1, 16, SEQ, 64)
    cpo = cat("cpo").reshape(16, 128, 4, 2)
    cp = np.ascontiguousarray(cpo.transpose(0, 3, 2, 1).reshape(1, 16, 2, 512))
    ks = cat("kso").reshape(1, 32, 4, 2, 64)
    vs = cat("vso").reshape(1, 32, 4, 2, 64)
    iss = cat("iso").reshape(1, 32, 4, 64)
    cso = np.stack([np.asarray(r["cso"]) for r in R], axis=0)
    cs = np.ascontiguousarray(cso.transpose(0, 3, 4, 2, 1).reshape(1, 32, 2, 512))
    f32 = lambda a: np.ascontiguousarray(a, dtype=np.float32)
    return tuple(f32(a) for a in (yp, ys, kp, vp, ip, cp, ks, vs, iss, cs))
```

```python
import os
import numpy as np
import ml_dtypes
from contextlib import ExitStack
import concourse.bass as bass
import concourse.mybir as mybir
from concourse.bass_utils import run_bass_kernel_spmd

F32 = mybir.dt.float32
BF16 = mybir.dt.bfloat16
I32 = mybir.dt.int32
AF = mybir.ActivationFunctionType
ALU = mybir.AluOpType
AX = mybir.AxisListType

NCORES = 8
D = 1024
DFF = 2816
SEQ = 2048
NPOOL = int(os.environ.get('K_NPOOL', '5120'))
PARTS = int(os.environ.get('K_PARTS', '99'))
SUB = int(os.environ.get('K_SUB', '99'))
EPS = 1e-6
NEG = -1.0e30
K_TOP = 256
N_BISECT = 15


class Buf:
    __slots__ = ("name", "w", "r")

    def __init__(self, name):
        self.name = name
        self.w = None
        self.r = []


class Op:
    __slots__ = ("eng", "idx", "fn", "dma", "waits", "lane", "lane_val", "signal", "rank")

    def __init__(self, eng, idx, fn, dma):
        self.eng = eng
        self.idx = idx
        self.fn = fn
        self.dma = dma
        self.waits = []
        self.lane = None
        self.lane_val = None
        self.signal = False
        self.rank = None


class Sched:
    ENGS = ("pe", "act", "dve", "pool", "sp")
    NLANES = {"sp": 12, "act": 8, "pool": 6}

    def __init__(self):
        self.ops = {e: [] for e in self.ENGS}
        self.maxw = {e: {f: -1 for f in self.ENGS} for e in self.ENGS}
        self.lane_last = {e: [None] * n for e, n in self.NLANES.items()}
        self.lane_cnt = {e: [0] * n for e, n in self.NLANES.items()}
        self.lane_rr = {e: 0 for e in self.NLANES}
        self.lane_waited = {e: {} for e in self.ENGS}

    def _add_wait(self, op, d):
        eng = op.eng
        if d.dma:
            key = (d.eng, d.lane)
            if self.lane_waited[eng].get(key, 0) >= d.lane_val:
                return
            self.lane_waited[eng][key] = d.lane_val
            op.waits.append(d)
        else:
            if self.maxw[eng][d.eng] >= d.idx:
                return
            self.maxw[eng][d.eng] = d.idx
            d.signal = True
            op.waits.append(d)

    def add(self, eng, fn, reads=(), writes=(), dma=False):
        op = Op(eng, len(self.ops[eng]), fn, dma)
        raw, other = [], []
        for b in reads:
            if b.w is not None:
                raw.append(b.w)
        for b in writes:
            if b.w is not None:
                other.append(b.w)
            other.extend(b.r)
        need = []
        for d in raw:
            if d.dma or dma or d.eng != eng or eng != "pe":
                need.append(d)
        for d in other:
            if d is op:
                continue
            if d.dma or dma or d.eng != eng:
                need.append(d)
        best = {}
        for d in need:
            key = (d.eng, d.lane) if d.dma else (d.eng, None)
            cur = best.get(key)
            val = d.lane_val if d.dma else d.idx
            if cur is None or val > (cur.lane_val if cur.dma else cur.idx):
                best[key] = d
        for d in best.values():
            self._add_wait(op, d)
        if dma:
            n = self.NLANES[eng]
            l = self.lane_rr[eng]
            self.lane_rr[eng] = (l + 1) % n
            prev = self.lane_last[eng][l]
            if prev is not None:
                self._add_wait(op, prev)
            self.lane_cnt[eng][l] += 16
            op.lane = l
            op.lane_val = self.lane_cnt[eng][l]
            self.lane_last[eng][l] = op
        for b in reads:
            b.r.append(op)
        for b in writes:
            b.w = op
            b.r = []
        self.ops[eng].append(op)
        return op

    def emit(self, nc, es):
        sems = {}
        for e in ("pe", "act", "dve", "pool"):
            sems[e] = es.enter_context(nc.semaphore("c_" + e))
        lsems = {}
        for e, n in self.NLANES.items():
            for l in range(n):
                lsems[(e, l)] = es.enter_context(nc.semaphore("l_%s%d" % (e, l)))
        for e in ("pe", "act", "dve", "pool"):
            k = 0
            for op in self.ops[e]:
                if (not op.dma) and op.signal:
                    k += 1
                    op.rank = k
        all_dma = [op for e in self.ENGS for op in self.ops[e] if op.dma]

        def run(engname, h):
            for op in self.ops[engname]:
                for d in op.waits:
                    if d.dma:
                        h.wait_ge(lsems[(d.eng, d.lane)], d.lane_val)
                    else:
                        h.wait_ge(sems[d.eng], d.rank)
                ins = op.fn(h)
                if op.dma:
                    ins.then_inc(lsems[(engname, op.lane)], 16)
                elif op.signal:
                    ins.then_inc(sems[engname], 1)
            if engname == "sp":
                for (e, l), s in lsems.items():
                    v = self.lane_cnt[e][l]
                    if v > 0:
                        h.wait_ge(s, v)

        with nc.Block() as block:
            @block.tensor
            def _(h):
                run("pe", h)

            @block.scalar
            def _(h):
                run("act", h)

            @block.vector
            def _(h):
                run("dve", h)

            @block.gpsimd
            def _(h):
                run("pool", h)

            @block.sync
            def _(h):
                run("sp", h)


class PsumAlloc:
    def __init__(self, banks):
        self.banks = banks
        self.busy = [False] * len(banks)
        self.rr = 0

    def get(self):
        n = len(self.banks)
        for k in range(n):
            i = (self.rr + k) % n
            if not self.busy[i]:
                self.busy[i] = True
                self.rr = (i + 1) % n
                return i
        raise RuntimeError("PSUM banks exhausted")

    def rel(self, i):
        self.busy[i] = False


def build_program(stage=3):
    nc = bass.Bass("TRN2", target_bir_lowering=False)
    S = Sched()
    es = ExitStack()

    def din(name, shape, dt=F32):
        return nc.dram_tensor(name, list(shape), dt, kind="ExternalInput").ap()

    def dout(name, shape, dt=F32):
        return nc.dram_tensor(name, list(shape), dt, kind="ExternalOutput").ap()

    xp = din("xp", [2 * SEQ, D])
    xs = din("xs", [16, D])
    npool = NPOOL if stage >= 3 else 2
    ck = din("ck", [npool, 128 * 128])
    cv = din("cv", [npool, 128 * 128])
    cik = din("cik", [npool, 128 * 64])
    cconv = din("cconv", [128, 4, 4, 2])
    ptT = din("ptT", [128, 4], I32)
    wf1 = din("wf1", [33, 128, 2048])
    wf2 = din("wf2", [33, 128, 2048])
    win = din("win", [12, 128, 2048])
    wout = din("wout", [4, 128, 2048])
    gains = din("gains", [128, 24])
    convp = din("convp", [128, 16])
    qkg = din("qkg", [128, 128])
    identb_d = din("identb", [128, 128], BF16)
    identf_d = din("identf", [128, 128])
    cm_d = din("cm", [128, 128])
    e_d = din("emat", [128, 1024], BF16)
    ropep = din("ropep", [128, 2, 16, 8])
    ropes = din("ropes", [16, 2, 8])
    cms_d = din("cms", [128, 4, 4])
    iota_d = din("iotap", [128, 1])
    cpow_d = din("cpow", [128, 32])

    yp = dout("yp", [2 * SEQ, D])
    ys = dout("ys", [16, D])
    kp = dout("kp", [2 * SEQ, 128])
    vp = dout("vp", [2 * SEQ, 128])
    ip = dout("ip", [2 * SEQ, 64])
    cpo = dout("cpo", [2, 128, 4, 2])
    kso = dout("kso", [16, 128])
    vso = dout("vso", [16, 128])
    iso = dout("iso", [16, 64])
    cso = dout("cso", [128, 4, 4, 2])

    wscr = {}
    for nm, npieces in (("wf1", 33), ("win", 12), ("wf2", 33)):
        t_ = nc.dram_tensor(nm + "_bf", [npieces, 128, 2048], BF16, kind="Internal").ap()
        wscr[nm] = (t_, [Buf("%s_bf%d" % (nm, i)) for i in range(npieces)])

    def sb(name, shape, dt=F32):
        t = es.enter_context(nc.sbuf_tensor(name, list(shape), dt))
        return t, Buf(name)

    X, bX = sb("X", [128, 4, 1024])
    bXs = [Buf("Xs%d" % i) for i in range(4)]
    XNT, bXNT = sb("XNT", [128, 8, 512], BF16)
    XSC, bXSC = sb("XSC", [128, 1024], BF16)
    SQJ, bSQJ = sb("SQJ", [128, 1024], BF16)
    SS, bSS = sb("SS", [128, 8])
    RS, bRS = sb("RS", [128, 8])
    NSTG = 2
    STG = [sb("STG%d" % i, [128, 2048]) for i in range(NSTG)]
    STGG = [Buf("STGG%d" % i) for i in range(NSTG)]
    STGH = [Buf("STGH%d" % i) for i in range(NSTG)]
    NWR = 4
    WR = [sb("WR%d" % i, [128, 2048], BF16) for i in range(NWR)]
    NWD = 4
    WD = [sb("WD%d" % i, [128, 2048], BF16) for i in range(NWD)]
    HT = [sb("HT%d" % i, [128, 512], BF16) for i in range(8)]
    ST = [sb("ST%d" % i, [128, 512]) for i in range(2)]
    WOUT, bWOUT = sb("WOUT", [128, 8, 1024], BF16)
    KT, bKT = sb("KT", [128, 2048], BF16)
    VE, bVE = sb("VE", [128, 16, 2, 65], BF16)
    KIT, bKIT = sb("KIT", [128, 2048], BF16)
    QT, bQT = sb("QT", [128, 4, 4, 128], BF16)
    QIT, bQIT = sb("QIT", [128, 4, 512], BF16)
    WI, bWI = sb("WI", [128, 4, 8])
    KVF, bKVF = sb("KVF", [128, 256])
    HC, bHC = sb("HC", [128, 512])
    UEXT, bUEXT = sb("UEXT", [128, 4, 520])
    ACC, bACC = sb("ACC", [128, 512])
    YCT, bYCT = sb("YCT", [128, 4, 512], BF16)
    GN, bGN = sb("GN", [128, 24])
    CVP, bCVP = sb("CVP", [128, 16])
    QKG, bQKG = sb("QKG", [128, 128])
    IDB, bIDB = sb("IDB", [128, 128], BF16)
    IDF, bIDF = sb("IDF", [128, 128])
    CM, bCM = sb("CM", [128, 128])
    EM, bEM = sb("EM", [128, 8, 128], BF16)
    ROPE, bROPE = sb("ROPE", [128, 2, 16, 8])
    ROPS, bROPS = sb("ROPS", [16, 2, 8])
    QIBD2 = [sb("QIBD%d" % i, [128, 8, 128], BF16) for i in range(2)]
    WIREP, bWIREP = sb("WIREP", [128, 128])
    WSEL2 = [sb("WSEL%d" % i, [128, 8, 128], BF16) for i in range(2)]
    DEN, bDEN = sb("DEN", [128, 8])
    MBIAS, bMB = sb("MBIAS", [128, 1])
    NEG1, bNEG1 = sb("NEG1", [128, 8])
    RR = [sb("RR%d" % i, [128, 512], BF16) for i in range(3)]
    ISB, bISB = sb("ISB", [128, 2048])
    JUNK, bJUNK = sb("JUNK", [128, 2048], BF16)
    MASKB, bMASKB = sb("MASKB", [128, 2048], BF16)
    MASKT, bMASKT = sb("MASKT", [128, 16, 128], BF16)
    PEXP = [sb("PEXP%d" % i, [128, 512], BF16) for i in range(2)]
    PTB = [sb("PTB%d" % i, [128, 512], BF16) for i in range(2)]
    OB, bOB = sb("OB", [128, 4, 2, 64], BF16)
    OT, bOT = sb("OT", [128, 4, 128], BF16)
    BS, bBS = sb("BS", [128, 16])
    HW, bHW = sb("HW", [128, 32])
    CPOW, bCPOW = sb("CPOW", [128, 32])
    MIDP, bMIDP = sb("MIDP", [128, 2])
    RDEN, bRDEN = sb("RDEN", [128, 8])

    banks = []
    for i in range(8):
        t = es.enter_context(nc.psum_tensor("PS%d" % i, [128, 512], F32))
        banks.append((t, Buf("PS%d" % i)))
    PA = PsumAlloc(banks)

    def pf(i):
        return banks[i][0]

    def pb(i):
        return banks[i][1]

    def pbf(i):
        return banks[i][0][:, :].bitcast(BF16)

    def dma(out, in_, reads, writes, eng="sp", **kw):
        return S.add(eng, lambda h: h.dma_start(out=out, in_=in_, **kw), reads, writes, dma=True)

    def act(out, in_, func, reads, writes, **kw):
        return S.add("act", lambda h: h.activation(out=out, in_=in_, func=func, **kw), reads, writes)

    def mm(out, lhsT, rhs, start, stop, reads, writes):
        return S.add("pe", lambda h: h.matmul(out, lhsT, rhs, start=start, stop=stop,
                                               skip_group_check=True), reads, writes)

    def tr(out, in_, ident, reads, writes):
        return S.add("pe", lambda h: h.transpose(out, in_, ident), reads, writes)

    def V(eng, name, reads, writes, *a, **kw):
        return S.add(eng, lambda h: getattr(h, name)(*a, **kw), reads, writes)

    dma(GN[:, :], gains, [], [bGN])
    dma(CVP[:, :], convp, [], [bCVP])
    dma(QKG[:, :], qkg, [], [bQKG])
    dma(IDB[:, :], identb_d, [], [bIDB])
    dma(IDF[:, :], identf_d, [], [bIDF])
    dma(CM[:, :], cm_d, [], [bCM])
    dma(EM[:, :, :], e_d.rearrange("p (g t) -> p g t", g=8), [], [bEM])
    dma(ROPE[:, :, :, :], ropep, [], [bROPE])
    dma(ROPS[:, :, :], ropes, [], [bROPS])
    dma(CPOW[:, :], cpow_d, [], [bCPOW])
    V("pool", "memset", [], [bVE], VE[:, :, :, :], 1.0)
    for i_ in range(2):
        V("pool", "memset", [], [QIBD2[i_][1]], QIBD2[i_][0][:, :, :], 0.0)
    V("pool", "memset", [], [bNEG1], NEG1[:, :], -1.0)
    V("pool", "memset", [], [bMB], MBIAS[:, :], -60000.0)
    V("pool", "memset", [], [bUEXT], UEXT[:, :, :], 0.0)

    stg_rr = [0]
    cast_rr = [0]

    cur_blk = [0]

    def load_piece(dram_piece, dst, dstbuf, gain_col=None, scr=None):
        if scr is not None and cur_blk[0] > 0:
            nm, pi = scr
            dma(dst, wscr[nm][0][pi], [wscr[nm][1][pi]], [dstbuf])
            return
        i = stg_rr[0]
        stg_rr[0] = (i + 1) % NSTG
        st, bst = STG[i]
        dma(st[:, :], dram_piece, [], [bst, STGG[i], STGH[i]])
        cast_rr[0] += 1
        if gain_col is None:
            V("dve", "tensor_copy", [bst, STGG[i], STGH[i]], [dstbuf], dst[:, :], st[:, :])
        else:
            g = GN[:, gain_col:gain_col + 8]
            V("dve" if cast_rr[0] % 3 != 0 else "pool", "tensor_tensor", [bst, STGG[i], STGH[i], bGN], [dstbuf],
              dst[:, :].rearrange("p (k n) -> p k n", k=8),
              st[:, :].rearrange("p (k n) -> p k n", k=8),
              g.unsqueeze(2).to_broadcast([128, 8, 256]), ALU.mult)
        if scr is not None:
            nm, pi = scr
            dma(wscr[nm][0][pi], dst, [dstbuf], [wscr[nm][1][pi]], eng="act")

    for i in range(4):
        load_piece(wout[i], WOUT[:, 2 * i:2 * i + 2, :].rearrange("p a n -> p (a n)"), bWOUT)

    wr_rr = [0]

    def next_wr():
        i = wr_rr[0]
        wr_rr[0] = (i + 1) % NWR
        return WR[i]

    SSN, bSSN = sb("SSN", [128, 4])
    RSN, bRSN = sb("RSN", [128, 4])

    def norm_block(nsub, pr):
        junk = ST[0][0][:, :].bitcast(BF16)
        bjunk = ST[0][1]
        for s in range(nsub):
            act(junk[:pr, :], X[:pr, s, :], AF.Square, [bXs[s]], [bjunk, bSSN], accum_out=SSN[:pr, s:s + 1])
        act(RSN[:pr, 0:nsub], SSN[:pr, 0:nsub], AF.Sqrt, [bSSN, bEPSB], [bRSN], scale=1.0 / D, bias=EPSB[:pr, 0:1])
        V("dve", "reciprocal", [bRSN], [bRSN], RSN[:pr, 0:nsub], RSN[:pr, 0:nsub])
        for s in range(nsub):
            xs_, bxs = (XSC, bXSC) if s % 2 == 0 else (SQJ, bSQJ)
            act(xs_[:pr, :], X[:pr, s, :], AF.Copy, [bXs[s], bRSN], [bxs], scale=RSN[:pr, s:s + 1])
            b = PA.get()
            for kc in range(8):
                tr(pbf(b)[:, kc * 128:kc * 128 + pr], xs_[:pr, kc * 128:(kc + 1) * 128], IDB[:pr, :pr],
                   [bxs, bIDB], [pb(b)])
            V("dve", "tensor_copy", [pb(b)], [bXNT], XNT[:, :, s * 128:s * 128 + pr],
              pbf(b).rearrange("p (k t) -> p k t", k=8)[:, :, 0:pr])
            PA.rel(b)

    EPSB, bEPSB = sb("EPSB", [128, 1])
    V("pool", "memset", [], [bEPSB], EPSB[:, :], EPS)

    THIRDS = [(0, 4), (4, 8), (8, 11)]

    def ffn_block(wf, gcol, nsub, pr, nb, wname, after_sub=None):
        for (g0, g1) in THIRDS:
            nch = (g1 - g0) * 2
            wds = []
            for gi in range(g0, g1):
                wg, bwg = next_wr()
                load_piece(wf[3 * gi], wg[:, :], bwg, gcol, (wname, 3 * gi))
                wu, bwu = next_wr()
                load_piece(wf[3 * gi + 1], wu[:, :], bwu, gcol, (wname, 3 * gi + 1))
                wd, bwd = WD[gi - g0]
                load_piece(wf[3 * gi + 2], wd[:, :], bwd, None, (wname, 3 * gi + 2))
                wds.append((wd, bwd))
                wg3 = wg[:, :].rearrange("p (k n) -> p k n", k=8)
                wu3 = wu[:, :].rearrange("p (k n) -> p k n", k=8)
                for c2 in range(2):
                    ci = (gi - g0) * 2 + c2
                    pg = PA.get()
                    pu = PA.get()
                    for kc in range(8):
                        mm(pf(pg)[:, :nb], wg3[:, kc, c2 * 128:(c2 + 1) * 128], XNT[:, kc, :nb],
                           kc == 0, kc == 7, [bwg, bXNT], [pb(pg)])
                    for kc in range(8):
                        mm(pf(pu)[:, :nb], wu3[:, kc, c2 * 128:(c2 + 1) * 128], XNT[:, kc, :nb],
                           kc == 0, kc == 7, [bwu, bXNT], [pb(pu)])
                    st, bst = ST[ci % 2]
                    act(st[:, :nb], pf(pg)[:, :nb], AF.Silu, [pb(pg)], [bst])
                    ht, bht = HT[ci]
                    V("dve", "tensor_tensor", [bst, pb(pu)], [bht], ht[:, :nb], st[:, :nb], pf(pu)[:, :nb],
                      ALU.mult)
                    PA.rel(pg)
                    PA.rel(pu)
            for s in range(nsub):
                for hh in range(2):
                    pd = PA.get()
                    for ci in range(nch):
                        wd, bwd = wds[ci // 2]
                        wd3 = wd[:, :].rearrange("p (c n) -> p c n", c=2)
                        ht, bht = HT[ci]
                        mm(pf(pd)[:pr, :], ht[:, s * 128:s * 128 + pr], wd3[:, ci % 2, hh * 512:(hh + 1) * 512],
                           ci == 0, ci == nch - 1, [bht, bwd], [pb(pd)])
                    V("dve", "scalar_tensor_tensor", [pb(pd), bXs[s]], [bXs[s]],
                      X[:pr, s, hh * 512:(hh + 1) * 512], pf(pd)[:pr, :], 0.5,
                      X[:pr, s, hh * 512:(hh + 1) * 512], ALU.mult, ALU.add)
                    PA.rel(pd)
                if after_sub is not None and (g0, g1) == THIRDS[-1]:
                    after_sub(s)

    class TS:
        pass

    TSETS = []
    for par in range(2):
        t_ = TS()
        t_.SSS16, t_.bSSS16 = sb("SSS16_%d" % par, [128, 16])
        t_.RSS16, t_.bRSS16 = sb("RSS16_%d" % par, [128, 16])
        TSETS.append(t_)

    def mix_in_block(blk, nsub, pr, nb, is_sample):
        q = blk // 4
        bi = blk % 4
        t0 = blk * 512
        s0 = bi * 512
        norm_block(nsub, pr)
        chunk_ps = {}
        slots = {}
        for j in range(6):
            w, bw = next_wr()
            load_piece(win[j], w[:, :], bw, 8, ("win", j))
            slots[j] = (w, bw)
            w3 = w[:, :].rearrange("p (k n) -> p k n", k=8)
            for c2 in range(2):
                cc = 2 * j + c2
                i, typ = cc // 3, cc % 3
                b = PA.get()
                for kc in range(8):
                    mm(pf(b)[:, :nb], w3[:, kc, c2 * 128:(c2 + 1) * 128], XNT[:, kc, :nb], kc == 0, kc == 7,
                       [bw, bXNT], [pb(b)])
                chunk_ps[(i, typ)] = b
                if typ == 2:
                    bgb, bgc, bhc = chunk_ps[(i, 0)], chunk_ps[(i, 1)], chunk_ps[(i, 2)]
                    act(HC[:, :nb], pf(bhc)[:, :nb], AF.Copy, [pb(bhc)], [bHC])
                    if not is_sample:
                        ue = UEXT[:, i, 2:2 + nb]
                        V("dve", "tensor_tensor", [pb(bgc), bHC], [bUEXT], ue, pf(bgc)[:, :nb], HC[:, :nb], ALU.mult)
                        u0 = UEXT[:, i, 0:nb]
                        u1 = UEXT[:, i, 1:1 + nb]
                        u2 = UEXT[:, i, 2:2 + nb]
                        acc = ACC[:, :nb]
                        gbp = pf(bgb)[:, :nb]
                        yo = YCT[:, i, :nb]
                    else:
                        ue4 = UEXT[:, i, 0:24].rearrange("p (b t) -> p b t", t=6)
                        V("dve", "tensor_tensor", [pb(bgc), bHC], [bUEXT], ue4[:, :, 2:6],
                          pf(bgc)[:, :nb].rearrange("p (b t) -> p b t", t=4),
                          HC[:, :nb].rearrange("p (b t) -> p b t", t=4), ALU.mult)
                        u0, u1, u2 = ue4[:, :, 0:4], ue4[:, :, 1:5], ue4[:, :, 2:6]
                        acc = ACC[:, :nb].rearrange("p (b t) -> p b t", t=4)
                        gbp = pf(bgb)[:, :nb].rearrange("p (b t) -> p b t", t=4)
                        yo = YCT[:, i, :nb].rearrange("p (b t) -> p b t", t=4)
                    V("dve", "tensor_scalar", [bUEXT, bCVP], [bACC], acc, u2, CVP[:, i * 4 + 2:i * 4 + 3],
                      CVP[:, i * 4 + 3:i * 4 + 4], ALU.mult, ALU.add)
                    V("dve", "scalar_tensor_tensor", [bUEXT, bACC, bCVP], [bACC], acc, u1,
                      CVP[:, i * 4 + 1:i * 4 + 2], acc, ALU.mult, ALU.add)
                    V("dve", "scalar_tensor_tensor", [bUEXT, bACC, bCVP], [bACC], acc, u0,
                      CVP[:, i * 4 + 0:i * 4 + 1], acc, ALU.mult, ALU.add)
                    V("dve", "tensor_tensor", [pb(bgb), bACC], [bYCT], yo, gbp, acc, ALU.mult)
                    for bb in (bgb, bgc, bhc):
                        PA.rel(bb)
                    if not is_sample:
                        if bi == 3:
                            dma(cpo[q, :, i, :], UEXT[:, i, 512:514], [bUEXT], [], eng="act")
                            V("pool", "memset", [bUEXT], [bUEXT], UEXT[:, i, 0:2], 0.0)
                        else:
                            V("pool", "tensor_copy", [bUEXT], [bUEXT], UEXT[:, i, 0:2], UEXT[:, i, 512:514])
                    else:
                        dma(cso[:, i, :, :], ue4[:, :, 4:6], [bUEXT], [])
        if not is_sample:
            cosb = lambda nh: ROPE[:pr, 0, bi * 4:bi * 4 + nsub, :].unsqueeze(2).to_broadcast([pr, nsub, nh, 8])
            sinb = lambda nh: ROPE[:pr, 1, bi * 4:bi * 4 + nsub, :].unsqueeze(2).to_broadcast([pr, nsub, nh, 8])
            brope = bROPE
        else:
            cosb = lambda nh: ROPS[:pr, 0, :].unsqueeze(1).unsqueeze(1).to_broadcast([pr, 1, nh, 8])
            sinb = lambda nh: ROPS[:pr, 1, :].unsqueeze(1).unsqueeze(1).to_broadcast([pr, 1, nh, 8])
            brope = bROPS
        nbank = (nsub + 1) // 2

        def rope_b(F, bF, G, bG, nh, width, off, bH):
            xv = F[:pr, 0:nsub * width].rearrange("p (s c) -> p s c", c=width)[:, :, off:off + nh * 64].rearrange(
                "p s (h d) -> p s h d", d=64)
            x1 = xv[:, :, :, 0:8]
            x2 = xv[:, :, :, 8:16]
            tt = [G[:pr, k * 128:k * 128 + nsub * nh * 8].rearrange("p (s h e) -> p s h e", s=nsub, e=8)
                  for k in range(4)]
            V("dve", "tensor_tensor", [bF, brope], [bG], tt[0], x1, cosb(nh), ALU.mult)
            V("dve", "tensor_tensor", [bF, brope], [bG], tt[1], x2, sinb(nh), ALU.mult)
            V("pool", "tensor_tensor", [bF, brope], [bH], tt[2], x2, cosb(nh), ALU.mult)
            V("pool", "tensor_tensor", [bF, brope], [bH], tt[3], x1, sinb(nh), ALU.mult)
            V("dve", "tensor_tensor", [bG, bH], [bF], x1, tt[0], tt[1], ALU.subtract)
            V("dve", "tensor_tensor", [bH, bG], [bF], x2, tt[2], tt[3], ALU.add)

        def tmA(j):
            w, bw = next_wr()
            load_piece(win[j], w[:, :], bw, 8, ("win", j))
            w3 = w[:, :].rearrange("p (k n) -> p k n", k=8)
            bk = [PA.get() for _ in range(nbank)]
            for s in range(nsub):
                b = bk[s // 2]
                off = (s % 2) * 256
                for kc in range(8):
                    mm(pf(b)[:pr, off:off + 256], XNT[:, kc, s * 128:s * 128 + pr], w3[:, kc, :], kc == 0, kc == 7,
                       [bw, bXNT], [pb(b)])
            return bk

        def tmB(j, bk):
            par = j % 2
            AR, bAR = STG[par]
            bGG = STGG[par]
            F, G = AR[:, 0:1024], AR[:, 1024:2048]
            BB, bBB = (XSC, bXSC) if par == 0 else (SQJ, bSQJ)
            SSB, bSSB = TSETS[par].SSS16, TSETS[par].bSSS16
            RSB, bRSB = TSETS[par].RSS16, TSETS[par].bRSS16
            ns2 = [min(2, nsub - 2 * q_) for q_ in range(nbank)]
            P3 = [pf(bk[q_])[:pr, 0:ns2[q_] * 256].rearrange("p (s c) -> p s c", c=256) for q_ in range(nbank)]
            F3 = F[:pr, 0:nsub * 256].rearrange("p (s c) -> p s c", c=256)
            F3b = [F[:pr, q_ * 512:q_ * 512 + ns2[q_] * 256].rearrange("p (s c) -> p s c", c=256) for q_ in range(nbank)]
            r0 = t0
            c0 = s0
            if j in (6, 7, 8):
                kw = 256 if j != 8 else 128
                nh = kw // 64
                gcol = 0 if j != 8 else 64
                for q_ in range(nbank):
                    act(G[:pr, q_ * 2 * kw:q_ * 2 * kw + ns2[q_] * kw].rearrange("p (s c) -> p s c", c=kw),
                        P3[q_][:, :, 0:kw], AF.Square, [pb(bk[q_])], [bGG])
                V("dve", "tensor_reduce", [bGG], [bSSB], SSB[:pr, 0:nsub * nh],
                  G[:pr, 0:nsub * kw].rearrange("p (h d) -> p h d", d=64), AX.X, ALU.add)
                act(RSB[:pr, 0:nsub * nh], SSB[:pr, 0:nsub * nh], AF.Sqrt, [bSSB, bEPSB], [bRSB], scale=1.0 / 64,
                    bias=EPSB[:pr, 0:1])
                V("dve", "reciprocal", [bRSB], [bRSB], RSB[:pr, 0:nsub * nh], RSB[:pr, 0:nsub * nh])
                for q_ in range(nbank):
                    V("dve", "tensor_tensor", [pb(bk[q_]), bRSB], [bAR],
                      F3b[q_][:, :, 0:kw].rearrange("p s (h d) -> p s h d", d=64),
                      P3[q_][:, :, 0:kw].rearrange("p s (h d) -> p s h d", d=64),
                      RSB[:pr, q_ * 2 * nh:q_ * 2 * nh + ns2[q_] * nh].rearrange("p (s h) -> p s h", h=nh).unsqueeze(
                          3).to_broadcast([pr, ns2[q_], nh, 64]), ALU.mult)
                fk = F3[:, :, 0:kw].rearrange("p s (h d) -> p s h d", d=64)
                V("dve", "tensor_tensor", [bAR, bQKG], [bAR], fk, fk,
                  QKG[:pr, gcol:gcol + 64].unsqueeze(1).unsqueeze(1).to_broadcast([pr, nsub, nh, 64]), ALU.mult)
                rope_b(F, bAR, G, bGG, nh, 256, 0, STGH[par])
            if j in (6, 7):
                act(BB[:pr, 0:nsub * 256], F[:pr, 0:nsub * 256], AF.Copy, [bAR], [bBB])
                b2 = PA.get()
                for s in range(nsub):
                    for rr in range(2):
                        tr(pbf(b2)[:, (s * 2 + rr) * 128:(s * 2 + rr) * 128 + pr],
                           BB[:pr, s * 256 + rr * 128:s * 256 + (rr + 1) * 128], IDB[:pr, :pr], [bBB, bIDB], [pb(b2)])
                r_ = (j - 6) * 2
                V("dve", "tensor_copy", [pb(b2)], [bQT], QT[:, 0:nsub, r_:r_ + 2, 0:pr],
                  pbf(b2)[:, 0:nsub * 256].rearrange("p (s r t) -> p s r t", r=2, t=128)[:, :, :, 0:pr])
                PA.rel(b2)
            elif j == 8:
                for q_ in range(nbank):
                    act(F3b[q_][:, :, 128:256], P3[q_][:, :, 128:256], AF.Copy, [pb(bk[q_])], [bAR])
                if not is_sample:
                    dma(kp[r0:r0 + nsub * 128, :].rearrange("(s p) c -> p s c", p=128), F3[:, :, 0:128], [bAR], [], eng="act")
                    dma(vp[r0:r0 + nsub * 128, :].rearrange("(s p) c -> p s c", p=128), F3[:, :, 128:256], [bAR], [], eng="act")
                    kt_dst, bkt = KT[:, c0:c0 + nsub * 128], bKT
                    V("dve", "tensor_copy", [bAR], [bVE], VE[:pr, c0 // 128:c0 // 128 + nsub, :, 0:64],
                      F3[:, :, 128:256].rearrange("p s (g d) -> p s g d", d=64))
                else:
                    dma(kso[0:pr, :], F[:pr, 0:128], [bAR], [], eng="act")
                    dma(vso[0:pr, :], F[:pr, 128:256], [bAR], [], eng="act")
                    kt_dst, bkt = KTS[:, 0:pr], bKTS
                    V("dve", "tensor_copy", [bAR], [bVES], VES[:pr, :, 0:64],
                      F[:pr, 128:256].rearrange("p (g d) -> p g d", d=64))
                    V("dve", "tensor_copy", [bAR], [bKVF], KVF[:pr, 128:256], F[:pr, 128:256])
                act(BB[:pr, 0:nsub * 128].rearrange("p (s c) -> p s c", c=128), F3[:, :, 0:128], AF.Copy, [bAR], [bBB])
                b2 = PA.get()
                for s in range(nsub):
                    tr(pbf(b2)[:, s * 128:s * 128 + pr], BB[:pr, s * 128:(s + 1) * 128], IDB[:pr, :pr],
                       [bBB, bIDB], [pb(b2)])
                if not is_sample:
                    V("dve", "tensor_copy", [pb(b2)], [bkt], kt_dst, pbf(b2)[:, 0:nsub * 128])
                else:
                    V("dve", "tensor_copy", [pb(b2)], [bkt], kt_dst, pbf(b2)[:, 0:pr])
                PA.rel(b2)
            elif j in (9, 10):
                for q_ in range(nbank):
                    act(F3b[q_][:, :, :], P3[q_][:, :, :], AF.Copy, [pb(bk[q_])], [bAR])
                rope_b(F, bAR, G, bGG, 4, 256, 0, STGH[par])
                act(BB[:pr, 0:nsub * 256], F[:pr, 0:nsub * 256], AF.Copy, [bAR], [bBB])
                if is_sample and stage >= 3:
                    for jl in range(2):
                        jj = (j - 9) * 2 + jl
                        for half in range(2):
                            hh_ = half * 4 + jj
                            src_ = F[:pr, jl * 128 + half * 64:jl * 128 + half * 64 + 64]
                            V("dve", "tensor_copy", [bAR], [bQID], QID[:pr, hh_, :, :],
                              src_.unsqueeze(1).to_broadcast([pr, 2, 64]))
                b2 = PA.get()
                for s in range(nsub):
                    for rr in range(2):
                        tr(pbf(b2)[:, (s * 2 + rr) * 128:(s * 2 + rr) * 128 + pr],
                           BB[:pr, s * 256 + rr * 128:s * 256 + (rr + 1) * 128], IDB[:pr, :pr], [bBB, bIDB], [pb(b2)])
                r_ = (j - 9) * 2
                V("dve", "tensor_copy", [pb(b2)], [bQIT],
                  QIT[:, r_:r_ + 2, 0:nsub * 128].rearrange("p r (s t) -> p s r t", t=128)[:, :, :, 0:pr],
                  pbf(b2)[:, 0:nsub * 256].rearrange("p (s r t) -> p s r t", r=2, t=128)[:, :, :, 0:pr])
                PA.rel(b2)
            else:
                for q_ in range(nbank):
                    act(F3b[q_][:, :, 0:64], P3[q_][:, :, 0:64], AF.Copy, [pb(bk[q_])], [bAR])
                    V("dve", "tensor_copy", [pb(bk[q_])], [bWI], WI[:pr, q_ * 2:q_ * 2 + ns2[q_], :], P3[q_][:, :, 64:72])
                rope_b(F, bAR, G, bGG, 1, 256, 0, STGH[par])
                if not is_sample:
                    dma(ip[r0:r0 + nsub * 128, :].rearrange("(s p) c -> p s c", p=128), F3[:, :, 0:64], [bAR], [], eng="act")
                    kit_dst, bkit = KIT[:, c0:c0 + nsub * 128], bKIT
                else:
                    dma(iso[0:pr, :], F[:pr, 0:64], [bAR], [], eng="act")
                    kit_dst, bkit = KITS[:, 0:pr], bKITS
                bb3 = BB[:pr, 0:nsub * 128].rearrange("p (s c) -> p s c", c=128)
                act(bb3[:, :, 0:64], F3[:, :, 0:64], AF.Copy, [bAR], [bBB])
                act(bb3[:, :, 64:128], F3[:, :, 0:64], AF.Copy, [bAR], [bBB])
                b2 = PA.get()
                for s in range(nsub):
                    tr(pbf(b2)[:, s * 128:s * 128 + pr], BB[:pr, s * 128:(s + 1) * 128], IDB[:pr, :pr],
                       [bBB, bIDB], [pb(b2)])
                if not is_sample:
                    V("dve", "tensor_copy", [pb(b2)], [bkit], kit_dst, pbf(b2)[:, 0:nsub * 128])
                else:
                    V("dve", "tensor_copy", [pb(b2)], [bkit], kit_dst, pbf(b2)[:, 0:pr])
                PA.rel(b2)
            for b in bk:
                PA.rel(b)

        ctx = tmA(6)
        for j in range(6, 12):
            nctx = tmA(j + 1) if j < 11 else None
            tmB(j, ctx)
            ctx = nctx


    KTS, bKTS = sb("KTS", [128, 16], BF16)
    KITS, bKITS = sb("KITS", [128, 16], BF16)
    VES, bVES = sb("VES", [16, 2, 65], BF16)
    V("pool", "memset", [], [bVES], VES[:, :, :], 1.0)

    def attn_prompt_block(blk):
        bi = blk % 4

        def geom(s):
            i = bi * 4 + s
            nk = i + 1
            return i, nk, nk * 128, (nk * 128 + 511) // 512

        def idx_prep(s):
            QIBD, bQIBD = QIBD2[s % 2]
            WSEL, bWSEL = WSEL2[s % 2]
            for half in range(2):
                srcv = QIT[half * 64:(half + 1) * 64, :, s * 128:(s + 1) * 128].rearrange(
                    "p j (g t) -> p g j t", t=16)
                dst = QIBD[half * 64:(half + 1) * 64, :, half * 64:(half + 1) * 64].rearrange(
                    "p g (j t) -> p g j t", t=16)
                V("pool", "tensor_copy", [bQIT], [bQIBD], dst, srcv)
            V("dve", "tensor_copy", [bWI], [bWIREP], WIREP[:, :].rearrange("p (h t) -> p h t", t=16),
              WI[:, s, :].unsqueeze(2).to_broadcast([128, 8, 16]))
            b = PA.get()
            tr(pf(b)[:, 0:128], WIREP[:, :], IDF[:, :], [bWIREP, bIDF], [pb(b)])
            V("dve", "tensor_tensor", [bEM, pb(b)], [bWSEL], WSEL[:, :, :], EM[:, :, :],
              pf(b)[:, 0:128].unsqueeze(1).to_broadcast([128, 8, 128]), ALU.mult)
            PA.rel(b)

        def idx_compute(s):
            i, nk, Sk, npc = geom(s)
            QIBD, bQIBD = QIBD2[s % 2]
            WSEL, bWSEL = WSEL2[s % 2]
            held = []
            items = [(p, grp) for p in range(npc) for grp in range(8)]
            bIs = {}

            def score(k):
                p, grp = items[k]
                w = min(512, Sk - p * 512)
                bs = PA.get()
                mm(pf(bs)[:, :w], QIBD[:, grp, :], KIT[:, p * 512:p * 512 + w], True, True,
                   [bQIBD, bKIT], [pb(bs)])
                return bs

            nxt_bs = score(0)
            for k, (p, grp) in enumerate(items):
                w = min(512, Sk - p * 512)
                bs = nxt_bs
                if k + 1 < len(items):
                    nxt_bs = score(k + 1)
                if grp == 0:
                    bIs[p] = PA.get()
                bI = bIs[p]
                rr, brr = RR[k % 3]
                act(rr[:, :w], pf(bs)[:, :w], AF.Relu, [pb(bs)], [brr])
                PA.rel(bs)
                mm(pf(bI)[:, :w], WSEL[:, grp, :], rr[:, :w], grp == 0, grp == 7, [bWSEL, brr], [pb(bI)])
                if grp == 7:
                    held.append((bI, p, w))
            return held

        def idx_evac(s, held):
            i, nk, Sk, npc = geom(s)
            for (bI, p, w) in held:
                act(ISB[:, p * 512:p * 512 + w], pf(bI)[:, :w], AF.Copy, [pb(bI)], [bISB])
                PA.rel(bI)

        def bisect(s):
            i, nk, Sk, npc = geom(s)
            if i >= 2:
                V("dve", "tensor_reduce", [bISB], [bBS], BS[:, 0:1], ISB[:, :Sk], AX.X, ALU.max,
                  apply_absolute_value=True)
                V("dve", "tensor_scalar", [bBS], [bBS], BS[:, 1:2], BS[:, 0:1], 2.0, 2.0, ALU.mult, ALU.add)
                V("dve", "tensor_scalar", [bBS, bCPOW], [bHW], HW[:, :], CPOW[:, :], BS[:, 1:2], None, ALU.mult)
                V("dve", "memset", [], [bMIDP], MIDP[:, :], 0.0)
            V("dve", "tensor_tensor", [bISB, bCM], [bISB], ISB[:, i * 128:(i + 1) * 128],
              ISB[:, i * 128:(i + 1) * 128], CM[:, :], ALU.add)
            if i >= 2:
                for n in range(N_BISECT):
                    V("dve", "tensor_scalar", [bISB, bMIDP], [bJUNK, bBS], JUNK[:, :Sk], ISB[:, :Sk], MIDP[:, 0:1],
                      None, ALU.is_ge, ALU.add, accum_out=BS[:, 4:5])
                    V("dve", "scalar_tensor_tensor", [bBS, bHW], [bBS], BS[:, 5:6], BS[:, 4:5], K_TOP - 0.5,
                      HW[:, n:n + 1], ALU.is_ge, ALU.mult)
                    V("dve", "scalar_tensor_tensor", [bMIDP, bHW, bBS], [bMIDP], MIDP[:, 0:1], MIDP[:, 0:1],
                      HW[:, n + 1:n + 2], BS[:, 5:6], ALU.subtract, ALU.add)
                V("dve", "tensor_scalar", [bMIDP, bHW], [bBS], BS[:, 2:3], MIDP[:, 0:1], HW[:, N_BISECT:N_BISECT + 1],
                  None, ALU.subtract)
                V("dve", "tensor_scalar", [bISB, bBS], [bMASKB], MASKB[:, :Sk], ISB[:, :Sk], BS[:, 2:3], None,
                  ALU.is_ge)
            else:
                V("dve", "tensor_scalar", [bISB], [bMASKB], MASKB[:, :Sk], ISB[:, :Sk], -1.0e29, None, ALU.is_ge)

        def masktrans(s):
            i, nk, Sk, npc = geom(s)
            for c0 in range(0, nk, 8):
                n = min(8, nk - c0)
                b = PA.get()
                for c in range(c0, c0 + n):
                    tr(pbf(b)[:, (c - c0) * 128:(c - c0 + 1) * 128], MASKB[:, c * 128:(c + 1) * 128], IDB[:, :],
                       [bMASKB, bIDB], [pb(b)])
                act(MASKT[:, c0:c0 + n, :], pbf(b)[:, 0:n * 128].rearrange("p (c t) -> p c t", t=128), AF.Identity,
                    [pb(b), bMB], [bMASKT], scale=60000.0, bias=MBIAS[:, 0:1])
                PA.rel(b)

        def attend(s):
            i, nk, Sk, npc = geom(s)
            bO = [PA.get(), PA.get()]
            aitems = [(c, g) for c in range(nk) for g in range(2)]

            def qk(k):
                c, g = aitems[k]
                bs = PA.get()
                mm(pf(bs)[:, :], KT[g * 64:(g + 1) * 64, c * 128:(c + 1) * 128],
                   QT[g * 64:(g + 1) * 64, s, :, :].rearrange("p r t -> p (r t)"), True, False,
                   [bKT, bQT], [pb(bs)])
                for r in range(4):
                    mm(pf(bs)[:, r * 128:(r + 1) * 128], IDB[:, :], MASKT[:, c, :], False, r == 3,
                       [bIDB, bMASKT], [pb(bs)])
                return bs

            nxt_bs = qk(0)
            for k, (c, g) in enumerate(aitems):
                bs = nxt_bs
                if k + 1 < len(aitems):
                    nxt_bs = qk(k + 1)
                pt_, bpt = (PEXP + PTB)[k % 4]
                act(pt_[:, :], pf(bs)[:, :], AF.Exp, [pb(bs)], [bpt], scale=0.125)
                PA.rel(bs)
                for r in range(4):
                    mm(pf(bO[g])[:, r * 65:(r + 1) * 65], pt_[:, r * 128:(r + 1) * 128], VE[:, c, g, :],
                       c == 0 and r == 0, c == nk - 1 and r == 3, [bpt, bVE], [pb(bO[g])])
            for g in range(2):
                o3 = pf(bO[g])[:, 0:260].rearrange("p (r e) -> p r e", e=65)
                act(DEN[:, g * 4:(g + 1) * 4].unsqueeze(2), o3[:, :, 64:65], AF.Copy, [pb(bO[g])], [bDEN])
            V("pool", "tensor_tensor", [bDEN, bNEG1], [bRDEN], RDEN[:, :], DEN[:, :], NEG1[:, :], ALU.pow)
            for g in range(2):
                o3 = pf(bO[g])[:, 0:260].rearrange("p (r e) -> p r e", e=65)
                for r in range(4):
                    act(OB[:, r, g, :], o3[:, r, 0:64], AF.Copy, [pb(bO[g]), bRDEN], [bOB],
                        scale=RDEN[:, g * 4 + r:g * 4 + r + 1])
                PA.rel(bO[g])
            b = PA.get()
            for r in range(4):
                tr(pbf(b)[:, r * 128:(r + 1) * 128], OB[:, r, :, :].rearrange("p g d -> p (g d)"), IDB[:, :],
                   [bOB, bIDB], [pb(b)])
            act(OT[:, :, :], pbf(b)[:, 0:512].rearrange("p (r t) -> p r t", r=4), AF.Copy, [pb(b)], [bOT])
            PA.rel(b)
            w_out_add(s, 128, True)

        idx_prep(0)
        held = idx_compute(0)
        idx_evac(0, held)
        idx_prep(1)
        bisect(0)
        held = idx_compute(1)
        masktrans(0)
        idx_evac(1, held)
        idx_prep(2)
        for s in range(4):
            held2 = idx_compute(s + 2) if s + 2 <= 3 else None
            if s + 1 <= 3:
                bisect(s + 1)
            attend(s)
            if held2 is not None:
                idx_evac(s + 2, held2)
                if s + 3 <= 3:
                    idx_prep(s + 3)
            if s + 1 <= 3:
                masktrans(s + 1)

    def w_out_add(s, pr, with_attn, ot_cols=None, otbuf=None):
        for hh in range(2):
            b = PA.get()
            nk8 = 8 if with_attn else 4
            for k8 in range(nk8):
                if k8 < 4:
                    lhs, rb = YCT[:, k8, s * 128:s * 128 + pr], bYCT
                else:
                    lhs, rb = (OT[:, k8 - 4, 0:pr] if ot_cols is None else ot_cols(k8 - 4)), (bOT if otbuf is None else otbuf)
                mm(pf(b)[:pr, :], lhs, WOUT[:, k8, hh * 512:(hh + 1) * 512], k8 == 0, k8 == nk8 - 1,
                   [rb, bWOUT], [pb(b)])
            if with_attn and pr == 128:
                tx, btx = ST[1]
                act(tx[:pr, :], pf(b)[:pr, :], AF.Copy, [pb(b)], [btx])
                V("pool", "tensor_tensor", [btx, bXs[s]], [bXs[s]], X[:pr, s, hh * 512:(hh + 1) * 512], tx[:pr, :],
                  X[:pr, s, hh * 512:(hh + 1) * 512], ALU.add)
            else:
                V("dve", "tensor_tensor", [pb(b), bXs[s]], [bXs[s]], X[:pr, s, hh * 512:(hh + 1) * 512], pf(b)[:pr, :],
                  X[:pr, s, hh * 512:(hh + 1) * 512], ALU.add)
            PA.rel(b)

    N_BIS_S = 16
    if stage >= 3:
        IT = X[:, 1:4, :].rearrange("p a n -> p (a n)")[:, 0:2064].rearrange("p (b k t) -> p b k t", b=4, k=129)
        bIT = bX
        CMPB = XNT[:, :, :].rearrange("p k n -> p (k n)")[:, 0:2064].rearrange("p (b k t) -> p b k t", b=4, k=129)
        bCMPB = bXNT
        MASKS, bMASKS = sb("MASKS", [128, 4, 129, 4], BF16)
        QID, bQID = sb("QID", [16, 8, 2, 64], BF16)
        QITD, bQITD = sb("QITD", [128, 4, 4, 8], BF16)
        QTSB, bQTSB = sb("QTSB", [128, 4, 4, 4], BF16)
        WM, bWM = sb("WM", [16, 16, 8])
        WB, bWB = sb("WB", [128, 128])
        ONESF, bONESF = sb("ONESF", [128, 128])
        PTSB, bPTSB = sb("PTSB", [128, 4], I32)
        PTF, bPTF = sb("PTF", [128, 4])
        IDX, bIDX = sb("IDX", [128, 4, 8], I32)
        IDX4, bIDX4 = sb("IDX4", [128, 4, 4], I32)
        CMS, bCMS = sb("CMS", [128, 4, 4])
        SELM, bSELM = sb("SELM", [16, 8])
        VMK, bVMK = sb("VMK", [16, 4, 128])
        VESB, bVESB = sb("VESB", [4, 4, 2, 65], BF16)
        LO, bLO = sb("LO", [128, 16])
        MID, bMID = sb("MID", [128, 16])
        PC, bPC = sb("PC", [128, 16])
        T1, bT1 = sb("T1", [128, 16])
        AM, bAM = sb("AM", [128, 4])
        OBS, bOBS = sb("OBS", [16, 2, 64], BF16)
        OTS, bOTS = sb("OTS", [128, 4, 16], BF16)
        RDS, bRDS = sb("RDS", [16, 2])
        KTP2, bKTP2 = sb("KTP2", [128, 2048], BF16)
        VE2, bVE2 = sb("VE2", [128, 16, 2, 65], BF16)
        V("pool", "memset", [], [bVE2], VE2[:, :, :, :], 1.0)
        ck8 = ck.rearrange("n (e x) -> (n e) x", e=8)
        cv8 = cv.rearrange("n (e x) -> (n e) x", e=8)
        cik4 = cik.rearrange("n (e x) -> (n e) x", e=4)
        selm_d = din("selm", [16, 8])
        dma(PTSB[:, :], ptT, [], [bPTSB])
        dma(CMS[:, :, :], cms_d, [], [bCMS])
        dma(SELM[:, :], selm_d, [], [bSELM])
        V("pool", "memset", [], [bONESF], ONESF[:, :], 1.0)
        V("pool", "memset", [], [bVESB], VESB[:, :, :, :], 1.0)

    _bc = {}

    def bc_reg(h, val):
        if val not in _bc:
            _bc[val] = h.to_reg(val)
        return _bc[val]

    def attn_sample_block():
        V("pool", "memset", [], [bIT, bXs[1], bXs[2], bXs[3]], IT[:, :, :, :], NEG)
        V("dve", "tensor_copy", [bPTSB], [bPTF], PTF[:, :], PTSB[:, :])
        for e in range(8):
            V("dve", "tensor_scalar", [bPTF], [bIDX], IDX[:, :, e], PTF[:, :], 8.0, float(e), ALU.mult, ALU.add)
        for e in range(4):
            V("dve", "tensor_scalar", [bPTF], [bIDX4], IDX4[:, :, e], PTF[:, :], 4.0, float(e), ALU.mult, ALU.add)
        b0 = PA.get()
        for h in range(8):
            tr(pbf(b0)[:, h * 16:(h + 1) * 16], QID[:, h, :, :].rearrange("p a d -> p (a d)"), IDB[:16, :16],
               [bQID, bIDB], [pb(b0)])
        V("dve", "tensor_copy", [pb(b0)], [bQITD], QITD[:, :, :, :].rearrange("p b t h -> p h b t"),
          pbf(b0)[:, 0:128].rearrange("p (h b t) -> p h b t", h=8, b=4))
        PA.rel(b0)
        V("dve", "tensor_copy", [bQT], [bQTSB], QTSB[:, :, :, :].rearrange("p b r t -> p r b t"),
          QT[:, 0, :, 0:16].rearrange("p r (b t) -> p r b t", t=4))
        V("dve", "tensor_tensor", [bWI, bIDF], [bWM], WM[:, :, :], WI[:16, 0, :].unsqueeze(1).to_broadcast([16, 16, 8]),
          IDF[:16, :16].unsqueeze(2).to_broadcast([16, 16, 8]), ALU.mult)
        b0 = PA.get()
        mm(pf(b0)[:, 0:128], ONESF[:16, :], WM[:, :, :].rearrange("p a h -> p (a h)"), True, True, [bONESF, bWM], [pb(b0)])
        act(WB[:, :], pf(b0)[:, 0:128], AF.Copy, [pb(b0)], [bWB])
        PA.rel(b0)
        V("dve", "tensor_tensor", [bKVF, bSELM], [bVMK], VMK[:, :, :], KVF[:16, 128:256].unsqueeze(1).to_broadcast([16, 4, 128]),
          SELM[:, 4:8].unsqueeze(2).to_broadcast([16, 4, 128]), ALU.mult)
        b0 = PA.get()
        mm(pf(b0)[0:4, :], SELM[:, 0:4], VMK[:, :, :].rearrange("p b x -> p (b x)"), True, True, [bSELM, bVMK], [pb(b0)])
        V("dve", "tensor_copy", [pb(b0)], [bVESB], VESB[:, :, :, 0:64],
          pf(b0)[0:4, :].rearrange("p (b g d) -> p b g d", b=4, g=2))
        PA.rel(b0)
        if PARTS < 1:
            return w_out_add(0, 16, False)
        IKP, bIKP = ISB, bISB
        IKB, bIKB = JUNK, bJUNK
        KITP, bKITP = MASKB, bMASKB
        for b in range(4):
            for q4 in range(4):
                S.add("pool", lambda h, b=b, q4=q4: h.indirect_dma_start(
                    out=IKP[:, :], out_offset=None, in_=cik4,
                    in_offset=bass.IndirectOffsetOnAxis(ap=IDX4[:, b, q4:q4 + 1], axis=0),
                    bounds_check=bc_reg(h, npool * 4 - 1), oob_is_err=False),
                    [bIDX4], [bIKP], dma=True)
                act(IKB[:, :], IKP[:, :], AF.Copy, [bIKP], [bIKB])
                if SUB < 1:
                    continue
                for c0 in range(0, 16, 8):
                    bb = PA.get()
                    for pp in range(8):
                        tr(pbf(bb)[:, pp * 128:(pp + 1) * 128], IKB[:, (c0 + pp) * 128:(c0 + pp + 1) * 128], IDB[:, :],
                           [bIKB, bIDB], [pb(bb)])
                    V("dve", "tensor_copy", [pb(bb)], [bKITP], KITP[:, c0 * 128:(c0 + 8) * 128], pbf(bb)[:, :])
                    PA.rel(bb)
                if SUB < 2:
                    continue
                bsh = [PA.get(), PA.get()]
                for kk in range(32):
                    half, m = kk % 2, kk // 2
                    mm(pf(bsh[half])[:, m * 32:(m + 1) * 32], KITP[half * 64:(half + 1) * 64, m * 128:(m + 1) * 128],
                       QITD[half * 64:(half + 1) * 64, b, :, :].rearrange("p t h -> p (t h)"), True, True,
                       [bKITP, bQITD], [pb(bsh[half])])
                for half in range(2):
                    rr, brr = RR[half]
                    act(rr[:, :], pf(bsh[half])[:, :], AF.Relu, [pb(bsh[half])], [brr])
                    PA.rel(bsh[half])
                    if SUB < 3:
                        continue
                    st, bst = ST[half]
                    V("dve", "tensor_tensor", [brr, bWB], [bst], st[:, :].rearrange("p (k x) -> p k x", k=16),
                      rr[:, :].rearrange("p (k x) -> p k x", k=16),
                      WB[:, b * 32:(b + 1) * 32].unsqueeze(1).to_broadcast([128, 16, 32]), ALU.mult)
                    V("dve", "tensor_reduce", [bst], [bIT],
                      IT[:, b, q4 * 32:(q4 + 1) * 32, :].rearrange("p (m two) t -> p m two t", two=2)[:, :, half, :],
                      st[:, :].rearrange("p (k t h) -> p k t h", k=16, t=4), AX.X, ALU.add)
            if SUB < 4:
                continue
            bs = PA.get()
            mm(pf(bs)[0:4, 0:32], KITS[0:64, b * 4:(b + 1) * 4], QITD[0:64, b, :, :].rearrange("p t h -> p (t h)"),
               True, True, [bKITS, bQITD], [pb(bs)])
            rr, brr = RR[2]
            act(rr[0:4, 0:32], pf(bs)[0:4, 0:32], AF.Relu, [pb(bs)], [brr])
            PA.rel(bs)
            st, bst = ST[0]
            V("dve", "tensor_tensor", [brr, bWB], [bst], st[0:4, 0:32], rr[0:4, 0:32], WB[0:4, b * 32:(b + 1) * 32], ALU.mult)
            V("dve", "tensor_reduce", [bst], [bAM], AM[0:4, 0:4], st[0:4, 0:32].rearrange("p (t h) -> p t h", t=4),
              AX.X, ALU.add)
            V("dve", "tensor_tensor", [bAM, bCMS], [bIT], IT[0:4, b, 128, :], AM[0:4, 0:4], CMS[0:4, b, :], ALU.add)
        if PARTS < 2:
            return w_out_add(0, 16, False)
        V("dve", "tensor_reduce", [bIT], [bAM], AM[:, 0:1], IT[:, :, 0:128, :], AX.XYZ, ALU.max, apply_absolute_value=True)
        b0 = PA.get()
        tr(pf(b0)[0:1, 0:128], AM[:, 0:1], IDF[:, :], [bAM, bIDF], [pb(b0)])
        V("dve", "tensor_reduce", [pb(b0)], [bAM], AM[0:1, 1:2], pf(b0)[0:1, 0:128], AX.X, ALU.max)
        PA.rel(b0)
        b0 = PA.get()
        mm(pf(b0)[:, 0:2], ONESF[0:1, :], AM[0:1, 1:3], True, True, [bONESF, bAM], [pb(b0)])
        V("dve", "tensor_scalar", [pb(b0)], [bAM], AM[:, 2:3], pf(b0)[:, 0:1], 2.0, 2.0, ALU.mult, ALU.add)
        V("dve", "tensor_scalar", [pb(b0)], [bAM], AM[:, 3:4], pf(b0)[:, 0:1], -1.0, -1.0, ALU.mult, ALU.add)
        PA.rel(b0)
        V("dve", "tensor_copy", [bAM], [bLO], LO[:, :], AM[:, 3:4].to_broadcast([128, 16]))
        if PARTS < 3:
            return w_out_add(0, 16, False)
        itv = IT[:, :, :, :].rearrange("p b k t -> p b t k")
        cmv = CMPB[:, :, :, :].rearrange("p b k t -> p b t k")
        for n in range(N_BIS_S):
            c = 2.0 ** -(n + 1)
            V("dve", "scalar_tensor_tensor", [bAM, bLO], [bMID], MID[:, :], AM[:, 2:3].to_broadcast([128, 16]), c, LO[:, :],
              ALU.mult, ALU.add)
            V("dve", "tensor_tensor", [bIT, bMID], [bCMPB], cmv, itv,
              MID[:, :].rearrange("p (b t) -> p b t", t=4).unsqueeze(3).to_broadcast([128, 4, 4, 129]), ALU.is_ge)
            V("dve", "tensor_reduce", [bCMPB], [bPC], PC[:, :].rearrange("p (b t) -> p b t", t=4), cmv, AX.X, ALU.add)
            b0 = PA.get()
            mm(pf(b0)[:, 0:16], ONESF[:, :], PC[:, :], True, True, [bONESF, bPC], [pb(b0)])
            V("dve", "tensor_scalar", [pb(b0)], [bT1], T1[:, :], pf(b0)[:, 0:16], K_TOP - 0.5, c, ALU.is_ge, ALU.mult)
            PA.rel(b0)
            V("dve", "scalar_tensor_tensor", [bT1, bAM, bLO], [bLO], LO[:, :], T1[:, :], AM[:, 2:3], LO[:, :], ALU.mult, ALU.add)
        V("dve", "tensor_tensor", [bIT, bLO], [bMASKS], MASKS[:, :, :, :].rearrange("p b k t -> p b t k"), itv,
          LO[:, :].rearrange("p (b t) -> p b t", t=4).unsqueeze(3).to_broadcast([128, 4, 4, 129]), ALU.is_ge)
        if PARTS < 4:
            return w_out_add(0, 16, False)
        KPs = [(ISB, [bISB]), (STG[1][0], [STG[1][1], STGG[1], STGH[1]])]
        VPs = [(STG[0][0], [STG[0][1], STGG[0], STGH[0]]), (XNT[:, :, :].rearrange("p k n -> p (k n)").bitcast(F32), [bXNT])]
        KPBs = [(JUNK, bJUNK), (MASKT[:, :, :].rearrange("p c t -> p (c t)"), bMASKT)]
        KTPs = [(MASKB, bMASKB), (KTP2, bKTP2)]
        VEs = [(VE, bVE), (VE2, bVE2)]
        steps = [(b, e) for b in range(4) for e in range(8)]

        def gather(it):
            b, e = steps[it]
            KP, bKP = KPs[it % 2]
            VP, bVP = VPs[it % 2]
            S.add("pool", lambda h, b=b, e=e, KP=KP: h.indirect_dma_start(
                out=KP[:, :], out_offset=None, in_=ck8,
                in_offset=bass.IndirectOffsetOnAxis(ap=IDX[:, b, e:e + 1], axis=0),
                bounds_check=bc_reg(h, npool * 8 - 1), oob_is_err=False),
                [bIDX], bKP, dma=True)
            S.add("pool", lambda h, b=b, e=e, VP=VP: h.indirect_dma_start(
                out=VP[:, :], out_offset=None, in_=cv8,
                in_offset=bass.IndirectOffsetOnAxis(ap=IDX[:, b, e:e + 1], axis=0),
                bounds_check=bc_reg(h, npool * 8 - 1), oob_is_err=False),
                [bIDX], bVP, dma=True)

        def cast(it):
            KP, bKP = KPs[it % 2]
            VP, bVP = VPs[it % 2]
            KPB, bKPB = KPBs[it % 2]
            VEx, bVEx = VEs[it % 2]
            act(KPB[:, :], KP[:, :], AF.Copy, bKP, [bKPB])
            act(VEx[:, :, :, 0:64], VP[:, :].rearrange("p (k g d) -> p k g d", k=16, g=2), AF.Copy, bVP, [bVEx])

        state = {}

        def compute(it):
            b, e = steps[it]
            KPB, bKPB = KPBs[it % 2]
            KTP, bKTP = KTPs[it % 2]
            VEx, bVEx = VEs[it % 2]
            if e == 0:
                state["bO"] = PA.get()
                state["first"] = True
            bO = state["bO"]
            for c0 in range(0, 16, 8):
                bb = PA.get()
                for pp in range(8):
                    tr(pbf(bb)[:, pp * 128:(pp + 1) * 128], KPB[:, (c0 + pp) * 128:(c0 + pp + 1) * 128], IDB[:, :],
                       [bKPB, bIDB], [pb(bb)])
                V("dve", "tensor_copy", [pb(bb)], [bKTP], KTP[:, c0 * 128:(c0 + 8) * 128], pbf(bb)[:, :])
                PA.rel(bb)
            bsg = [PA.get(), PA.get()]
            for kl in range(16):
                for g in range(2):
                    mm(pf(bsg[g])[:, kl * 16:(kl + 1) * 16],
                       KTP[g * 64:(g + 1) * 64, kl * 128:(kl + 1) * 128],
                       QTSB[g * 64:(g + 1) * 64, b, :, :].rearrange("p r t -> p (r t)"), True, True,
                       [bKTP, bQTSB], [pb(bsg[g])])
            for g in range(2):
                pe_, bpe = PEXP[g]
                act(pe_[:, 0:256], pf(bsg[g])[:, 0:256], AF.Exp, [pb(bsg[g])], [bpe], scale=0.125)
                PA.rel(bsg[g])
                pt_, bpt = PTB[g]
                V("pool", "tensor_tensor", [bpe, bMASKS], [bpt],
                  pt_[:, 0:256].rearrange("p (k r t) -> p k r t", k=16, t=4),
                  pe_[:, 0:256].rearrange("p (k r t) -> p k r t", k=16, t=4),
                  MASKS[:, b, e * 16:(e + 1) * 16, :].unsqueeze(2).to_broadcast([128, 16, 4, 4]), ALU.mult)
            for kl in range(16):
                for g in range(2):
                    pt_, bpt = PTB[g]
                    mm(pf(bO)[0:16, g * 65:(g + 1) * 65], pt_[:, kl * 16:(kl + 1) * 16],
                       VEx[:, kl, g, :], state["first"], False, [bpt, bVEx], [pb(bO)])
                    state["first"] = False
            if e == 7:
                finish_seq(b, bO)

        def finish_seq(b, bO):
                for g in range(2):
                    bs = PA.get()
                    mm(pf(bs)[0:4, 0:16], KTS[g * 64:(g + 1) * 64, b * 4:(b + 1) * 4],
                       QTSB[g * 64:(g + 1) * 64, b, :, :].rearrange("p r t -> p (r t)"), True, True, [bKTS, bQTSB], [pb(bs)])
                    pe_, bpe = PEXP[g]
                    act(pe_[0:4, 0:16], pf(bs)[0:4, 0:16], AF.Exp, [pb(bs)], [bpe], scale=0.125)
                    PA.rel(bs)
                    pt_, bpt = PTB[g]
                    V("pool", "tensor_tensor", [bpe, bMASKS], [bpt], pt_[0:4, 0:16].rearrange("p (r t) -> p r t", t=4),
                      pe_[0:4, 0:16].rearrange("p (r t) -> p r t", t=4),
                      MASKS[0:4, b, 128, :].unsqueeze(1).to_broadcast([4, 4, 4]), ALU.mult)
                    mm(pf(bO)[0:16, g * 65:(g + 1) * 65], pt_[0:4, 0:16], VESB[0:4, b, g, :], False, g == 1,
                       [bpt, bVESB], [pb(bO)])
                o3 = pf(bO)[0:16, 0:130].rearrange("p (g e) -> p g e", e=65)
                V("dve", "reciprocal", [pb(bO)], [bRDS], RDS[:, :].unsqueeze(2), o3[:, :, 64:65])
                V("dve", "tensor_tensor", [pb(bO), bRDS], [bOBS], OBS[:, :, :], o3[:, :, 0:64],
                  RDS[:, :].unsqueeze(2).to_broadcast([16, 2, 64]), ALU.mult)
                PA.rel(bO)
                b0 = PA.get()
                tr(pbf(b0)[:, 0:16], OBS[:, :, :].rearrange("p g d -> p (g d)"), IDB[:16, :16], [bOBS, bIDB], [pb(b0)])
                V("dve", "tensor_copy", [pb(b0)], [bOTS], OTS[:, :, b * 4:(b + 1) * 4],
                  pbf(b0)[:, 0:16].rearrange("p (r t) -> p r t", t=4))
                PA.rel(b0)

        gather(0)
        cast(0)
        for it in range(len(steps)):
            if it + 1 < len(steps):
                gather(it + 1)
            compute(it)
            if it + 1 < len(steps):
                cast(it + 1)
        w_out_add(0, 16, True, ot_cols=lambda r: OTS[:, r, :], otbuf=bOTS)

    nblk_prompt = 8
    for blk in range(nblk_prompt + 1):
        is_sample = blk == nblk_prompt
        cur_blk[0] = blk
        if not is_sample:
            nsub, pr, nb = 4, 128, 512
            for s_ in range(4):
                dma(X[:, s_, :], xp[blk * 512 + s_ * 128:blk * 512 + (s_ + 1) * 128, :], [], [bXs[s_]], eng="act")
        else:
            nsub, pr, nb = 1, 16, 16
            dma(X[:16, 0, :], xs, [], [bXs[0]])
            for i4 in range(4):
                dma(UEXT[:, i4, 0:24].rearrange("p (b t) -> p b t", t=6)[:, :, 0:2], cconv[:, i4, :, :],
                    [bUEXT], [bUEXT])
        norm_block(nsub, pr)
        ffn_block(wf1, 0, nsub, pr, nb, "wf1")
        mix_in_block(blk, nsub, pr, nb, is_sample)
        if stage >= 2:
            if not is_sample:
                attn_prompt_block(blk)
            elif stage >= 3:
                attn_sample_block()
            else:
                w_out_add(0, 16, False)
            norm_block(nsub, pr)
            if not is_sample:
                def store_sub(s_, blk=blk):
                    dma(yp[blk * 512 + s_ * 128:blk * 512 + (s_ + 1) * 128, :], X[:, s_, :], [bXs[s_]], [], eng="act")
                ffn_block(wf2, 16, nsub, pr, nb, "wf2", after_sub=store_sub)
            else:
                ffn_block(wf2, 16, nsub, pr, nb, "wf2")
        if not is_sample:
            if stage < 2:
                for s_ in range(4):
                    dma(yp[blk * 512 + s_ * 128:blk * 512 + (s_ + 1) * 128, :], X[:, s_, :], [bXs[s_]], [], eng="act")
        else:
            dma(ys, X[:16, 0, :], [bXs[0]], [])

    S.emit(nc, es)
    es.close()
    return nc


def _pieces_kn(W, ncol_piece=256):
    K, N = W.shape
    nj = N // ncol_piece
    a = W.reshape(8, 128, nj, ncol_piece).transpose(2, 1, 0, 3)
    return np.ascontiguousarray(a.reshape(nj, 128, 8 * ncol_piece))


def _pieces_rows(W, rows_piece=256):
    R, N = W.shape
    ng = R // rows_piece
    a = W.reshape(ng, 2, 128, N).transpose(0, 2, 1, 3)
    return np.ascontiguousarray(a.reshape(ng, 128, 2 * N))


def _ffn_pieces(wg, wu, wd):
    pg = _pieces_kn(wg)
    pu = _pieces_kn(wu)
    pd = _pieces_rows(wd)
    out = np.empty((33, 128, 2048), np.float32)
    out[0::3] = pg
    out[1::3] = pu
    out[2::3] = pd
    return out


def _perm_heads():
    idx = np.empty(512, np.int64)
    for r in range(4):
        for g in range(2):
            for d in range(64):
                idx[r * 128 + g * 64 + d] = (g * 4 + r) * 64 + d
    return idx


def _prep_shared(inp):
    f = lambda k: np.asarray(inp[k], np.float32)[0]
    sh = {}
    sh["wf1"] = _ffn_pieces(f("ffn1_w_gate"), f("ffn1_w_up"), f("ffn1_w_down"))
    sh["wf2"] = _ffn_pieces(f("ffn2_w_gate"), f("ffn2_w_up"), f("ffn2_w_down"))
    w_in = f("w_in")
    ph = _perm_heads()
    gb, gc, hc = w_in[:, 0:512], w_in[:, 512:1024], w_in[:, 1024:1536]
    q = w_in[:, 1536:2048][:, ph]
    k = w_in[:, 2048:2176]
    v = w_in[:, 2176:2304]
    qi = w_in[:, 2304:2816][:, ph]
    ki = w_in[:, 2816:2880]
    wi = w_in[:, 2880:2888]
    conv_cols = []
    for i in range(4):
        for t in (gb, gc, hc):
            conv_cols.append(t[:, i * 128:(i + 1) * 128])
    pad = np.zeros((1024, 256 - 72), np.float32)
    wperm = np.concatenate(conv_cols + [q, k, v, qi, ki, wi, pad], axis=1)
    assert wperm.shape[1] == 3072
    sh["win"] = _pieces_kn(wperm)
    w_out = f("w_out")
    wo = np.empty((128, 8, 1024), np.float32)
    for i in range(4):
        wo[:, i, :] = w_out[i * 128:(i + 1) * 128, :]
    for r in range(4):
        for g in range(2):
            h = g * 4 + r
            wo[g * 64:(g + 1) * 64, 4 + r, :] = w_out[512 + h * 64:512 + (h + 1) * 64, :]
    sh["wout"] = np.ascontiguousarray(wo.reshape(128, 4, 2048).transpose(1, 0, 2))
    gains = np.empty((128, 24), np.float32)
    for n, key in enumerate(("ffn1_norm", "mix_norm", "ffn2_norm")):
        gains[:, n * 8:(n + 1) * 8] = f(key).reshape(8, 128).T
    sh["gains"] = gains
    cw = f("conv_w")
    cb = f("conv_b")
    convp = np.empty((128, 16), np.float32)
    for i in range(4):
        for j in range(3):
            convp[:, i * 4 + j] = cw[j, i * 128:(i + 1) * 128]
        convp[:, i * 4 + 3] = cb[i * 128:(i + 1) * 128]
    sh["convp"] = convp
    qkg = np.empty((128, 128), np.float32)
    qkg[:, 0:64] = f("q_norm")[None, :]
    qkg[:, 64:128] = f("k_norm")[None, :]
    sh["qkg"] = qkg
    sh["identb"] = np.eye(128, dtype=np.float32).astype(ml_dtypes.bfloat16)
    sh["identf"] = np.eye(128, dtype=np.float32)
    t = np.arange(128)
    sh["cm"] = np.where(t[None, :] <= t[:, None], 0.0, NEG).astype(np.float32)
    em = np.zeros((128, 8, 128), np.float32)
    for half in range(2):
        for j in range(4):
            for tt in range(16):
                for grp in range(8):
                    em[half * 64 + j * 16 + tt, grp, grp * 16 + tt] = 1.0
    sh["emat"] = em.reshape(128, 1024).astype(ml_dtypes.bfloat16)
    inv = np.power(np.float32(500000.0), -np.arange(8, dtype=np.float32) * np.float32(2.0 / 16)).astype(np.float32)
    pos = np.arange(SEQ, dtype=np.float32)
    ang = (pos[:, None] * inv[None, :]).astype(np.float32)
    rp = np.empty((128, 2, 16, 8), np.float32)
    rp[:, 0] = np.cos(ang).reshape(16, 128, 8).transpose(1, 0, 2)
    rp[:, 1] = np.sin(ang).reshape(16, 128, 8).transpose(1, 0, 2)
    sh["ropep"] = rp
    poss = (16384 + np.arange(4)).astype(np.float32)
    angs = (poss[:, None] * inv[None, :]).astype(np.float32)
    rs = np.empty((16, 2, 8), np.float32)
    rs[:, 0] = np.tile(np.cos(angs), (4, 1))
    rs[:, 1] = np.tile(np.sin(angs), (4, 1))
    sh["ropes"] = rs
    cms = np.full((128, 4, 4), NEG, np.float32)
    for tp in range(4):
        for tt in range(4):
            if tp <= tt:
                cms[tp, :, tt] = 0.0
    sh["cms"] = cms
    sh["iotap"] = np.arange(128, dtype=np.float32).reshape(128, 1)
    sh["cpow"] = np.tile((2.0 ** -(np.arange(32, dtype=np.float64) + 1)).astype(np.float32)[None, :], (128, 1))
    selm = np.zeros((16, 8), np.float32)
    for b in range(4):
        for t in range(4):
            selm[b * 4 + t, t] = 1.0
            selm[b * 4 + t, 4 + b] = 1.0
    sh["selm"] = selm
    sh["ck"] = np.ascontiguousarray(np.asarray(inp["cache_k"], np.float32)[0].reshape(NPOOL, 128 * 128))
    sh["cv"] = np.ascontiguousarray(np.asarray(inp["cache_v"], np.float32)[0].reshape(NPOOL, 128 * 128))
    sh["cik"] = np.ascontiguousarray(np.asarray(inp["cache_idx_k"], np.float32)[0].reshape(NPOOL, 128 * 64))
    return sh


_PROG = {}


def kernel(**inp):
    stage = inp.pop("_stage", 3)
    if stage not in _PROG:
        _PROG[stage] = build_program(stage)
    nc = _PROG[stage]
    sh = _prep_shared(inp)
    if stage < 3:
        for k_ in ("ck", "cv", "cik"):
            sh[k_] = np.ascontiguousarray(sh[k_][:2])
        sh.pop("selm")
    x_prompt = np.asarray(inp["x_prompt"], np.float32)
    x_sample = np.asarray(inp["x_sample"], np.float32)
    cache_conv = np.asarray(inp["cache_conv"], np.float32)[0]
    page_table = np.asarray(inp["page_table"], np.int32)
    in_maps = []
    for c in range(NCORES):
        m = dict(sh)
        m["xp"] = np.ascontiguousarray(x_prompt[2 * c:2 * c + 2].reshape(2 * SEQ, D))
        m["xs"] = np.ascontiguousarray(x_sample[4 * c:4 * c + 4].reshape(16, D))
        cc = cache_conv[4 * c:4 * c + 4]
        m["cconv"] = np.ascontiguousarray(cc.reshape(4, 2, 4, 128).transpose(3, 2, 0, 1))
        m["ptT"] = np.ascontiguousarray(page_table[4 * c:4 * c + 4].T)
        in_maps.append(m)
    res = run_bass_kernel_spmd(nc, in_maps, core_ids=list(range(NCORES)))
    R = res.results
    cat = lambda k: np.concatenate([np.asarray(r[k]) for r in R], axis=0)
    yp = cat("yp").reshape(16, SEQ, D)
    ys = cat("ys").reshape(32, 4, D)
    kp = cat("kp").reshape(1, 16, SEQ, 2, 64)
    vp = cat("vp").reshape(1, 16, SEQ, 2, 64)
    ip = cat("ip").reshape(1, 16, SEQ, 64)
    cpo = cat("cpo").reshape(16, 128, 4, 2)
    cp = np.ascontiguousarray(cpo.transpose(0, 3, 2, 1).reshape(1, 16, 2, 512))
    ks = cat("kso").reshape(1, 32, 4, 2, 64)
    vs = cat("vso").reshape(1, 32, 4, 2, 64)
    iss = cat("iso").reshape(1, 32, 4, 64)
    cso = np.stack([np.asarray(r["cso"]) for r in R], axis=0)
    cs = np.ascontiguousarray(cso.transpose(0, 3, 4, 2, 1).reshape(1, 32, 2, 512))
    f32 = lambda a: np.ascontiguousarray(a, dtype=np.float32)
    return tuple(f32(a) for a in (yp, ys, kp, vp, ip, cp, ks, vs, iss, cs))
```

```python
import os
import numpy as np
import ml_dtypes
from contextlib import ExitStack
import concourse.bass as bass
import concourse.mybir as mybir
from concourse.bass_utils import run_bass_kernel_spmd

F32 = mybir.dt.float32
BF16 = mybir.dt.bfloat16
I32 = mybir.dt.int32
AF = mybir.ActivationFunctionType
ALU = mybir.AluOpType
AX = mybir.AxisListType

NCORES = 8
D = 1024
DFF = 2816
SEQ = 2048
NPOOL = int(os.environ.get('K_NPOOL', '5120'))
PARTS = int(os.environ.get('K_PARTS', '99'))
SUB = int(os.environ.get('K_SUB', '99'))
EPS = 1e-6
NEG = -1.0e30
K_TOP = 256
N_BISECT = 15


class Buf:
    __slots__ = ("name", "w", "r")

    def __init__(self, name):
        self.name = name
        self.w = None
        self.r = []


class Op:
    __slots__ = ("eng", "idx", "fn", "dma", "waits", "lane", "lane_val", "signal", "rank")

    def __init__(self, eng, idx, fn, dma):
        self.eng = eng
        self.idx = idx
        self.fn = fn
        self.dma = dma
        self.waits = []
        self.lane = None
        self.lane_val = None
        self.signal = False
        self.rank = None


class Sched:
    ENGS = ("pe", "act", "dve", "pool", "sp")
    NLANES = {"sp": 12, "act": 8, "pool": 6}

    def __init__(self):
        self.ops = {e: [] for e in self.ENGS}
        self.maxw = {e: {f: -1 for f in self.ENGS} for e in self.ENGS}
        self.lane_last = {e: [None] * n for e, n in self.NLANES.items()}
        self.lane_cnt = {e: [0] * n for e, n in self.NLANES.items()}
        self.lane_rr = {e: 0 for e in self.NLANES}
        self.lane_waited = {e: {} for e in self.ENGS}

    def _add_wait(self, op, d):
        eng = op.eng
        if d.dma:
            key = (d.eng, d.lane)
            if self.lane_waited[eng].get(key, 0) >= d.lane_val:
                return
            self.lane_waited[eng][key] = d.lane_val
            op.waits.append(d)
        else:
            if self.maxw[eng][d.eng] >= d.idx:
                return
            self.maxw[eng][d.eng] = d.idx
            d.signal = True
            op.waits.append(d)

    def add(self, eng, fn, reads=(), writes=(), dma=False):
        op = Op(eng, len(self.ops[eng]), fn, dma)
        raw, other = [], []
        for b in reads:
            if b.w is not None:
                raw.append(b.w)
        for b in writes:
            if b.w is not None:
                other.append(b.w)
            other.extend(b.r)
        need = []
        for d in raw:
            if d.dma or dma or d.eng != eng or eng != "pe":
                need.append(d)
        for d in other:
            if d is op:
                continue
            if d.dma or dma or d.eng != eng:
                need.append(d)
        best = {}
        for d in need:
            key = (d.eng, d.lane) if d.dma else (d.eng, None)
            cur = best.get(key)
            val = d.lane_val if d.dma else d.idx
            if cur is None or val > (cur.lane_val if cur.dma else cur.idx):
                best[key] = d
        for d in best.values():
            self._add_wait(op, d)
        if dma:
            n = self.NLANES[eng]
            l = self.lane_rr[eng]
            self.lane_rr[eng] = (l + 1) % n
            prev = self.lane_last[eng][l]
            if prev is not None:
                self._add_wait(op, prev)
            self.lane_cnt[eng][l] += 16
            op.lane = l
            op.lane_val = self.lane_cnt[eng][l]
            self.lane_last[eng][l] = op
        for b in reads:
            b.r.append(op)
        for b in writes:
            b.w = op
            b.r = []
        self.ops[eng].append(op)
        return op

    def emit(self, nc, es):
        sems = {}
        for e in ("pe", "act", "dve", "pool"):
            sems[e] = es.enter_context(nc.semaphore("c_" + e))
        lsems = {}
        for e, n in self.NLANES.items():
            for l in range(n):
                lsems[(e, l)] = es.enter_context(nc.semaphore("l_%s%d" % (e, l)))
        for e in ("pe", "act", "dve", "pool"):
            k = 0
            for op in self.ops[e]:
                if (not op.dma) and op.signal:
                    k += 1
                    op.rank = k
        all_dma = [op for e in self.ENGS for op in self.ops[e] if op.dma]

        def run(engname, h):
            for op in self.ops[engname]:
                for d in op.waits:
                    if d.dma:
                        h.wait_ge(lsems[(d.eng, d.lane)], d.lane_val)
                    else:
                        h.wait_ge(sems[d.eng], d.rank)
                ins = op.fn(h)
                if op.dma:
                    ins.then_inc(lsems[(engname, op.lane)], 16)
                elif op.signal:
                    ins.then_inc(sems[engname], 1)
            if engname == "sp":
                for (e, l), s in lsems.items():
                    v = self.lane_cnt[e][l]
                    if v > 0:
                        h.wait_ge(s, v)

        with nc.Block() as block:
            @block.tensor
            def _(h):
                run("pe", h)

            @block.scalar
            def _(h):
                run("act", h)

            @block.vector
            def _(h):
                run("dve", h)

            @block.gpsimd
            def _(h):
                run("pool", h)

            @block.sync
            def _(h):
                run("sp", h)


class PsumAlloc:
    def __init__(self, banks):
        self.banks = banks
        self.busy = [False] * len(banks)
        self.rr = 0

    def get(self):
        n = len(self.banks)
        for k in range(n):
            i = (self.rr + k) % n
            if not self.busy[i]:
                self.busy[i] = True
                self.rr = (i + 1) % n
                return i
        raise RuntimeError("PSUM banks exhausted")

    def rel(self, i):
        self.busy[i] = False


def build_program(stage=3):
    nc = bass.Bass("TRN2", target_bir_lowering=False)
    S = Sched()
    es = ExitStack()

    def din(name, shape, dt=F32):
        return nc.dram_tensor(name, list(shape), dt, kind="ExternalInput").ap()

    def dout(name, shape, dt=F32):
        return nc.dram_tensor(name, list(shape), dt, kind="ExternalOutput").ap()

    xp = din("xp", [2 * SEQ, D])
    xs = din("xs", [16, D])
    npool = NPOOL if stage >= 3 else 2
    ck = din("ck", [npool, 128 * 128])
    cv = din("cv", [npool, 128 * 128])
    cik = din("cik", [npool, 128 * 64])
    cconv = din("cconv", [128, 4, 4, 2])
    ptT = din("ptT", [128, 4], I32)
    wf1 = din("wf1", [33, 128, 2048])
    wf2 = din("wf2", [33, 128, 2048])
    win = din("win", [12, 128, 2048])
    wout = din("wout", [4, 128, 2048])
    gains = din("gains", [128, 24])
    convp = din("convp", [128, 16])
    qkg = din("qkg", [128, 128])
    identb_d = din("identb", [128, 128], BF16)
    identf_d = din("identf", [128, 128])
    cm_d = din("cm", [128, 128])
    e_d = din("emat", [128, 1024], BF16)
    ropep = din("ropep", [128, 2, 16, 8])
    ropes = din("ropes", [16, 2, 8])
    cms_d = din("cms", [128, 4, 4])
    iota_d = din("iotap", [128, 1])
    cpow_d = din("cpow", [128, 32])

    yp = dout("yp", [2 * SEQ, D])
    ys = dout("ys", [16, D])
    kp = dout("kp", [2 * SEQ, 128])
    vp = dout("vp", [2 * SEQ, 128])
    ip = dout("ip", [2 * SEQ, 64])
    cpo = dout("cpo", [2, 128, 4, 2])
    kso = dout("kso", [16, 128])
    vso = dout("vso", [16, 128])
    iso = dout("iso", [16, 64])
    cso = dout("cso", [128, 4, 4, 2])

    wscr = {}
    for nm, npieces in (("wf1", 33), ("win", 12), ("wf2", 33)):
        t_ = nc.dram_tensor(nm + "_bf", [npieces, 128, 2048], BF16, kind="Internal").ap()
        wscr[nm] = (t_, [Buf("%s_bf%d" % (nm, i)) for i in range(npieces)])

    def sb(name, shape, dt=F32):
        t = es.enter_context(nc.sbuf_tensor(name, list(shape), dt))
        return t, Buf(name)

    X, bX = sb("X", [128, 4, 1024])
    bXs = [Buf("Xs%d" % i) for i in range(4)]
    XNT, bXNT = sb("XNT", [128, 8, 512], BF16)
    XSC, bXSC = sb("XSC", [128, 1024], BF16)
    SQJ, bSQJ = sb("SQJ", [128, 1024], BF16)
    SS, bSS = sb("SS", [128, 8])
    RS, bRS = sb("RS", [128, 8])
    NSTG = 2
    STG = [sb("STG%d" % i, [128, 2048]) for i in range(NSTG)]
    STGG = [Buf("STGG%d" % i) for i in range(NSTG)]
    STGH = [Buf("STGH%d" % i) for i in range(NSTG)]
    NWR = 4
    WR = [sb("WR%d" % i, [128, 2048], BF16) for i in range(NWR)]
    NWD = 4
    WD = [sb("WD%d" % i, [128, 2048], BF16) for i in range(NWD)]
    HT = [sb("HT%d" % i, [128, 512], BF16) for i in range(8)]
    ST = [sb("ST%d" % i, [128, 512]) for i in range(2)]
    WOUT, bWOUT = sb("WOUT", [128, 8, 1024], BF16)
    KT, bKT = sb("KT", [128, 2048], BF16)
    VE, bVE = sb("VE", [128, 16, 2, 65], BF16)
    KIT, bKIT = sb("KIT", [128, 2048], BF16)
    QT, bQT = sb("QT", [128, 4, 4, 128], BF16)
    QIT, bQIT = sb("QIT", [128, 4, 512], BF16)
    WI, bWI = sb("WI", [128, 4, 8])
    KVF, bKVF = sb("KVF", [128, 256])
    HC, bHC = sb("HC", [128, 512])
    UEXT, bUEXT = sb("UEXT", [128, 4, 520])
    ACC, bACC = sb("ACC", [128, 512])
    YCT, bYCT = sb("YCT", [128, 4, 512], BF16)
    GN, bGN = sb("GN", [128, 24])
    CVP, bCVP = sb("CVP", [128, 16])
    QKG, bQKG = sb("QKG", [128, 128])
    IDB, bIDB = sb("IDB", [128, 128], BF16)
    IDF, bIDF = sb("IDF", [128, 128])
    CM, bCM = sb("CM", [128, 128])
    EM, bEM = sb("EM", [128, 8, 128], BF16)
    ROPE, bROPE = sb("ROPE", [128, 2, 16, 8])
    ROPS, bROPS = sb("ROPS", [16, 2, 8])
    QIBD2 = [sb("QIBD%d" % i, [128, 8, 128], BF16) for i in range(2)]
    WIREP, bWIREP = sb("WIREP", [128, 128])
    WSEL2 = [sb("WSEL%d" % i, [128, 8, 128], BF16) for i in range(2)]
    DEN, bDEN = sb("DEN", [128, 8])
    MBIAS, bMB = sb("MBIAS", [128, 1])
    NEG1, bNEG1 = sb("NEG1", [128, 8])
    RR = [sb("RR%d" % i, [128, 512], BF16) for i in range(3)]
    ISB, bISB = sb("ISB", [128, 2048])
    JUNK, bJUNK = sb("JUNK", [128, 2048], BF16)
    MASKB, bMASKB = sb("MASKB", [128, 2048], BF16)
    MASKT, bMASKT = sb("MASKT", [128, 16, 128], BF16)
    PEXP = [sb("PEXP%d" % i, [128, 512], BF16) for i in range(2)]
    PTB = [sb("PTB%d" % i, [128, 512], BF16) for i in range(2)]
    OB, bOB = sb("OB", [128, 4, 2, 64], BF16)
    OT, bOT = sb("OT", [128, 4, 128], BF16)
    BS, bBS = sb("BS", [128, 16])
    HW, bHW = sb("HW", [128, 32])
    CPOW, bCPOW = sb("CPOW", [128, 32])
    MIDP, bMIDP = sb("MIDP", [128, 2])
    RDEN, bRDEN = sb("RDEN", [128, 8])

    banks = []
    for i in range(8):
        t = es.enter_context(nc.psum_tensor("PS%d" % i, [128, 512], F32))
        banks.append((t, Buf("PS%d" % i)))
    PA = PsumAlloc(banks)

    def pf(i):
        return banks[i][0]

    def pb(i):
        return banks[i][1]

    def pbf(i):
        return banks[i][0][:, :].bitcast(BF16)

    def dma(out, in_, reads, writes, eng="sp", **kw):
        return S.add(eng, lambda h: h.dma_start(out=out, in_=in_, **kw), reads, writes, dma=True)

    def act(out, in_, func, reads, writes, **kw):
        return S.add("act", lambda h: h.activation(out=out, in_=in_, func=func, **kw), reads, writes)

    def mm(out, lhsT, rhs, start, stop, reads, writes):
        return S.add("pe", lambda h: h.matmul(out, lhsT, rhs, start=start, stop=stop,
                                               skip_group_check=True), reads, writes)

    def tr(out, in_, ident, reads, writes):
        return S.add("pe", lambda h: h.transpose(out, in_, ident), reads, writes)

    def V(eng, name, reads, writes, *a, **kw):
        return S.add(eng, lambda h: getattr(h, name)(*a, **kw), reads, writes)

    dma(GN[:, :], gains, [], [bGN])
    dma(CVP[:, :], convp, [], [bCVP])
    dma(QKG[:, :], qkg, [], [bQKG])
    dma(IDB[:, :], identb_d, [], [bIDB])
    dma(IDF[:, :], identf_d, [], [bIDF])
    dma(CM[:, :], cm_d, [], [bCM])
    dma(EM[:, :, :], e_d.rearrange("p (g t) -> p g t", g=8), [], [bEM])
    dma(ROPE[:, :, :, :], ropep, [], [bROPE])
    dma(ROPS[:, :, :], ropes, [], [bROPS])
    dma(CPOW[:, :], cpow_d, [], [bCPOW])
    V("pool", "memset", [], [bVE], VE[:, :, :, :], 1.0)
    for i_ in range(2):
        V("pool", "memset", [], [QIBD2[i_][1]], QIBD2[i_][0][:, :, :], 0.0)
    V("pool", "memset", [], [bNEG1], NEG1[:, :], -1.0)
    V("pool", "memset", [], [bMB], MBIAS[:, :], -60000.0)
    V("pool", "memset", [], [bUEXT], UEXT[:, :, :], 0.0)

    stg_rr = [0]
    cast_rr = [0]

    cur_blk = [0]

    def load_piece(dram_piece, dst, dstbuf, gain_col=None, scr=None):
        if scr is not None and cur_blk[0] > 0:
            nm, pi = scr
            dma(dst, wscr[nm][0][pi], [wscr[nm][1][pi]], [dstbuf])
            return
        i = stg_rr[0]
        stg_rr[0] = (i + 1) % NSTG
        st, bst = STG[i]
        dma(st[:, :], dram_piece, [], [bst, STGG[i], STGH[i]])
        cast_rr[0] += 1
        if gain_col is None:
            V("dve", "tensor_copy", [bst, STGG[i], STGH[i]], [dstbuf], dst[:, :], st[:, :])
        else:
            g = GN[:, gain_col:gain_col + 8]
            V("dve" if cast_rr[0] % 3 != 0 else "pool", "tensor_tensor", [bst, STGG[i], STGH[i], bGN], [dstbuf],
              dst[:, :].rearrange("p (k n) -> p k n", k=8),
              st[:, :].rearrange("p (k n) -> p k n", k=8),
              g.unsqueeze(2).to_broadcast([128, 8, 256]), ALU.mult)
        if scr is not None:
            nm, pi = scr
            dma(wscr[nm][0][pi], dst, [dstbuf], [wscr[nm][1][pi]], eng="act")

    for i in range(4):
        load_piece(wout[i], WOUT[:, 2 * i:2 * i + 2, :].rearrange("p a n -> p (a n)"), bWOUT)

    wr_rr = [0]

    def next_wr():
        i = wr_rr[0]
        wr_rr[0] = (i + 1) % NWR
        return WR[i]

    SSN, bSSN = sb("SSN", [128, 4])
    RSN, bRSN = sb("RSN", [128, 4])

    def norm_block(nsub, pr):
        junk = ST[0][0][:, :].bitcast(BF16)
        bjunk = ST[0][1]
        for s in range(nsub):
            act(junk[:pr, :], X[:pr, s, :], AF.Square, [bXs[s]], [bjunk, bSSN], accum_out=SSN[:pr, s:s + 1])
        act(RSN[:pr, 0:nsub], SSN[:pr, 0:nsub], AF.Sqrt, [bSSN, bEPSB], [bRSN], scale=1.0 / D, bias=EPSB[:pr, 0:1])
        V("dve", "reciprocal", [bRSN], [bRSN], RSN[:pr, 0:nsub], RSN[:pr, 0:nsub])
        for s in range(nsub):
            xs_, bxs = (XSC, bXSC) if s % 2 == 0 else (SQJ, bSQJ)
            act(xs_[:pr, :], X[:pr, s, :], AF.Copy, [bXs[s], bRSN], [bxs], scale=RSN[:pr, s:s + 1])
            b = PA.get()
            for kc in range(8):
                tr(pbf(b)[:, kc * 128:kc * 128 + pr], xs_[:pr, kc * 128:(kc + 1) * 128], IDB[:pr, :pr],
                   [bxs, bIDB], [pb(b)])
            V("dve", "tensor_copy", [pb(b)], [bXNT], XNT[:, :, s * 128:s * 128 + pr],
              pbf(b).rearrange("p (k t) -> p k t", k=8)[:, :, 0:pr])
            PA.rel(b)

    EPSB, bEPSB = sb("EPSB", [128, 1])
    V("pool", "memset", [], [bEPSB], EPSB[:, :], EPS)

    THIRDS = [(0, 4), (4, 8), (8, 11)]

    def ffn_block(wf, gcol, nsub, pr, nb, wname, after_sub=None):
        for (g0, g1) in THIRDS:
            nch = (g1 - g0) * 2
            wds = []
            for gi in range(g0, g1):
                wg, bwg = next_wr()
                load_piece(wf[3 * gi], wg[:, :], bwg, gcol, (wname, 3 * gi))
                wu, bwu = next_wr()
                load_piece(wf[3 * gi + 1], wu[:, :], bwu, gcol, (wname, 3 * gi + 1))
                wd, bwd = WD[gi - g0]
                load_piece(wf[3 * gi + 2], wd[:, :], bwd, None, (wname, 3 * gi + 2))
                wds.append((wd, bwd))
                wg3 = wg[:, :].rearrange("p (k n) -> p k n", k=8)
                wu3 = wu[:, :].rearrange("p (k n) -> p k n", k=8)
                for c2 in range(2):
                    ci = (gi - g0) * 2 + c2
                    pg = PA.get()
                    pu = PA.get()
                    for kc in range(8):
                        mm(pf(pg)[:, :nb], wg3[:, kc, c2 * 128:(c2 + 1) * 128], XNT[:, kc, :nb],
                           kc == 0, kc == 7, [bwg, bXNT], [pb(pg)])
                    for kc in range(8):
                        mm(pf(pu)[:, :nb], wu3[:, kc, c2 * 128:(c2 + 1) * 128], XNT[:, kc, :nb],
                           kc == 0, kc == 7, [bwu, bXNT], [pb(pu)])
                    st, bst = ST[ci % 2]
                    act(st[:, :nb], pf(pg)[:, :nb], AF.Silu, [pb(pg)], [bst])
                    ht, bht = HT[ci]
                    V("dve", "tensor_tensor", [bst, pb(pu)], [bht], ht[:, :nb], st[:, :nb], pf(pu)[:, :nb],
                      ALU.mult)
                    PA.rel(pg)
                    PA.rel(pu)
            for s in range(nsub):
                for hh in range(2):
                    pd = PA.get()
                    for ci in range(nch):
                        wd, bwd = wds[ci // 2]
                        wd3 = wd[:, :].rearrange("p (c n) -> p c n", c=2)
                        ht, bht = HT[ci]
                        mm(pf(pd)[:pr, :], ht[:, s * 128:s * 128 + pr], wd3[:, ci % 2, hh * 512:(hh + 1) * 512],
                           ci == 0, ci == nch - 1, [bht, bwd], [pb(pd)])
                    V("dve", "scalar_tensor_tensor", [pb(pd), bXs[s]], [bXs[s]],
                      X[:pr, s, hh * 512:(hh + 1) * 512], pf(pd)[:pr, :], 0.5,
                      X[:pr, s, hh * 512:(hh + 1) * 512], ALU.mult, ALU.add)
                    PA.rel(pd)
                if after_sub is not None and (g0, g1) == THIRDS[-1]:
                    after_sub(s)

    class TS:
        pass

    TSETS = []
    for par in range(2):
        t_ = TS()
        t_.SSS16, t_.bSSS16 = sb("SSS16_%d" % par, [128, 16])
        t_.RSS16, t_.bRSS16 = sb("RSS16_%d" % par, [128, 16])
        TSETS.append(t_)

    def mix_in_block(blk, nsub, pr, nb, is_sample):
        q = blk // 4
        bi = blk % 4
        t0 = blk * 512
        s0 = bi * 512
        norm_block(nsub, pr)
        chunk_ps = {}
        slots = {}
        for j in range(6):
            w, bw = next_wr()
            load_piece(win[j], w[:, :], bw, 8, ("win", j))
            slots[j] = (w, bw)
            w3 = w[:, :].rearrange("p (k n) -> p k n", k=8)
            for c2 in range(2):
                cc = 2 * j + c2
                i, typ = cc // 3, cc % 3
                b = PA.get()
                for kc in range(8):
                    mm(pf(b)[:, :nb], w3[:, kc, c2 * 128:(c2 + 1) * 128], XNT[:, kc, :nb], kc == 0, kc == 7,
                       [bw, bXNT], [pb(b)])
                chunk_ps[(i, typ)] = b
                if typ == 2:
                    bgb, bgc, bhc = chunk_ps[(i, 0)], chunk_ps[(i, 1)], chunk_ps[(i, 2)]
                    act(HC[:, :nb], pf(bhc)[:, :nb], AF.Copy, [pb(bhc)], [bHC])
                    if not is_sample:
                        ue = UEXT[:, i, 2:2 + nb]
                        V("dve", "tensor_tensor", [pb(bgc), bHC], [bUEXT], ue, pf(bgc)[:, :nb], HC[:, :nb], ALU.mult)
                        u0 = UEXT[:, i, 0:nb]
                        u1 = UEXT[:, i, 1:1 + nb]
                        u2 = UEXT[:, i, 2:2 + nb]
                        acc = ACC[:, :nb]
                        gbp = pf(bgb)[:, :nb]
                        yo = YCT[:, i, :nb]
                    else:
                        ue4 = UEXT[:, i, 0:24].rearrange("p (b t) -> p b t", t=6)
                        V("dve", "tensor_tensor", [pb(bgc), bHC], [bUEXT], ue4[:, :, 2:6],
                          pf(bgc)[:, :nb].rearrange("p (b t) -> p b t", t=4),
                          HC[:, :nb].rearrange("p (b t) -> p b t", t=4), ALU.mult)
                        u0, u1, u2 = ue4[:, :, 0:4], ue4[:, :, 1:5], ue4[:, :, 2:6]
                        acc = ACC[:, :nb].rearrange("p (b t) -> p b t", t=4)
                        gbp = pf(bgb)[:, :nb].rearrange("p (b t) -> p b t", t=4)
                        yo = YCT[:, i, :nb].rearrange("p (b t) -> p b t", t=4)
                    V("dve", "tensor_scalar", [bUEXT, bCVP], [bACC], acc, u2, CVP[:, i * 4 + 2:i * 4 + 3],
                      CVP[:, i * 4 + 3:i * 4 + 4], ALU.mult, ALU.add)
                    V("dve", "scalar_tensor_tensor", [bUEXT, bACC, bCVP], [bACC], acc, u1,
                      CVP[:, i * 4 + 1:i * 4 + 2], acc, ALU.mult, ALU.add)
                    V("dve", "scalar_tensor_tensor", [bUEXT, bACC, bCVP], [bACC], acc, u0,
                      CVP[:, i * 4 + 0:i * 4 + 1], acc, ALU.mult, ALU.add)
                    V("dve", "tensor_tensor", [pb(bgb), bACC], [bYCT], yo, gbp, acc, ALU.mult)
                    for bb in (bgb, bgc, bhc):
                        PA.rel(bb)
                    if not is_sample:
                        if bi == 3:
                            dma(cpo[q, :, i, :], UEXT[:, i, 512:514], [bUEXT], [], eng="act")
                            V("pool", "memset", [bUEXT], [bUEXT], UEXT[:, i, 0:2], 0.0)
                        else:
                            V("pool", "tensor_copy", [bUEXT], [bUEXT], UEXT[:, i, 0:2], UEXT[:, i, 512:514])
                    else:
                        dma(cso[:, i, :, :], ue4[:, :, 4:6], [bUEXT], [])
        if not is_sample:
            cosb = lambda nh: ROPE[:pr, 0, bi * 4:bi * 4 + nsub, :].unsqueeze(2).to_broadcast([pr, nsub, nh, 8])
            sinb = lambda nh: ROPE[:pr, 1, bi * 4:bi * 4 + nsub, :].unsqueeze(2).to_broadcast([pr, nsub, nh, 8])
            brope = bROPE
        else:
            cosb = lambda nh: ROPS[:pr, 0, :].unsqueeze(1).unsqueeze(1).to_broadcast([pr, 1, nh, 8])
            sinb = lambda nh: ROPS[:pr, 1, :].unsqueeze(1).unsqueeze(1).to_broadcast([pr, 1, nh, 8])
            brope = bROPS
        nbank = (nsub + 1) // 2

        def rope_b(F, bF, G, bG, nh, width, off, bH):
            xv = F[:pr, 0:nsub * width].rearrange("p (s c) -> p s c", c=width)[:, :, off:off + nh * 64].rearrange(
                "p s (h d) -> p s h d", d=64)
            x1 = xv[:, :, :, 0:8]
            x2 = xv[:, :, :, 8:16]
            tt = [G[:pr, k * 128:k * 128 + nsub * nh * 8].rearrange("p (s h e) -> p s h e", s=nsub, e=8)
                  for k in range(4)]
            V("dve", "tensor_tensor", [bF, brope], [bG], tt[0], x1, cosb(nh), ALU.mult)
            V("dve", "tensor_tensor", [bF, brope], [bG], tt[1], x2, sinb(nh), ALU.mult)
            V("pool", "tensor_tensor", [bF, brope], [bH], tt[2], x2, cosb(nh), ALU.mult)
            V("pool", "tensor_tensor", [bF, brope], [bH], tt[3], x1, sinb(nh), ALU.mult)
            V("dve", "tensor_tensor", [bG, bH], [bF], x1, tt[0], tt[1], ALU.subtract)
            V("dve", "tensor_tensor", [bH, bG], [bF], x2, tt[2], tt[3], ALU.add)

        def tmA(j):
            w, bw = next_wr()
            load_piece(win[j], w[:, :], bw, 8, ("win", j))
            w3 = w[:, :].rearrange("p (k n) -> p k n", k=8)
            bk = [PA.get() for _ in range(nbank)]
            for s in range(nsub):
                b = bk[s // 2]
                off = (s % 2) * 256
                for kc in range(8):
                    mm(pf(b)[:pr, off:off + 256], XNT[:, kc, s * 128:s * 128 + pr], w3[:, kc, :], kc == 0, kc == 7,
                       [bw, bXNT], [pb(b)])
            return bk

        def tmB(j, bk):
            par = j % 2
            AR, bAR = STG[par]
            bGG = STGG[par]
            F, G = AR[:, 0:1024], AR[:, 1024:2048]
            BB, bBB = (XSC, bXSC) if par == 0 else (SQJ, bSQJ)
            SSB, bSSB = TSETS[par].SSS16, TSETS[par].bSSS16
            RSB, bRSB = TSETS[par].RSS16, TSETS[par].bRSS16
            ns2 = [min(2, nsub - 2 * q_) for q_ in range(nbank)]
            P3 = [pf(bk[q_])[:pr, 0:ns2[q_] * 256].rearrange("p (s c) -> p s c", c=256) for q_ in range(nbank)]
            F3 = F[:pr, 0:nsub * 256].rearrange("p (s c) -> p s c", c=256)
            F3b = [F[:pr, q_ * 512:q_ * 512 + ns2[q_] * 256].rearrange("p (s c) -> p s c", c=256) for q_ in range(nbank)]
            r0 = t0
            c0 = s0
            if j in (6, 7, 8):
                kw = 256 if j != 8 else 128
                nh = kw // 64
                gcol = 0 if j != 8 else 64
                for q_ in range(nbank):
                    act(G[:pr, q_ * 2 * kw:q_ * 2 * kw + ns2[q_] * kw].rearrange("p (s c) -> p s c", c=kw),
                        P3[q_][:, :, 0:kw], AF.Square, [pb(bk[q_])], [bGG])
                V("dve", "tensor_reduce", [bGG], [bSSB], SSB[:pr, 0:nsub * nh],
                  G[:pr, 0:nsub * kw].rearrange("p (h d) -> p h d", d=64), AX.X, ALU.add)
                act(RSB[:pr, 0:nsub * nh], SSB[:pr, 0:nsub * nh], AF.Sqrt, [bSSB, bEPSB], [bRSB], scale=1.0 / 64,
                    bias=EPSB[:pr, 0:1])
                V("dve", "reciprocal", [bRSB], [bRSB], RSB[:pr, 0:nsub * nh], RSB[:pr, 0:nsub * nh])
                for q_ in range(nbank):
                    V("dve", "tensor_tensor", [pb(bk[q_]), bRSB], [bAR],
                      F3b[q_][:, :, 0:kw].rearrange("p s (h d) -> p s h d", d=64),
                      P3[q_][:, :, 0:kw].rearrange("p s (h d) -> p s h d", d=64),
                      RSB[:pr, q_ * 2 * nh:q_ * 2 * nh + ns2[q_] * nh].rearrange("p (s h) -> p s h", h=nh).unsqueeze(
                          3).to_broadcast([pr, ns2[q_], nh, 64]), ALU.mult)
                fk = F3[:, :, 0:kw].rearrange("p s (h d) -> p s h d", d=64)
                V("dve", "tensor_tensor", [bAR, bQKG], [bAR], fk, fk,
                  QKG[:pr, gcol:gcol + 64].unsqueeze(1).unsqueeze(1).to_broadcast([pr, nsub, nh, 64]), ALU.mult)
                rope_b(F, bAR, G, bGG, nh, 256, 0, STGH[par])
            if j in (6, 7):
                act(BB[:pr, 0:nsub * 256], F[:pr, 0:nsub * 256], AF.Copy, [bAR], [bBB])
                b2 = PA.get()
                for s in range(nsub):
                    for rr in range(2):
                        tr(pbf(b2)[:, (s * 2 + rr) * 128:(s * 2 + rr) * 128 + pr],
                           BB[:pr, s * 256 + rr * 128:s * 256 + (rr + 1) * 128], IDB[:pr, :pr], [bBB, bIDB], [pb(b2)])
                r_ = (j - 6) * 2
                V("dve", "tensor_copy", [pb(b2)], [bQT], QT[:, 0:nsub, r_:r_ + 2, 0:pr],
                  pbf(b2)[:, 0:nsub * 256].rearrange("p (s r t) -> p s r t", r=2, t=128)[:, :, :, 0:pr])
                PA.rel(b2)
            elif j == 8:
                for q_ in range(nbank):
                    act(F3b[q_][:, :, 128:256], P3[q_][:, :, 128:256], AF.Copy, [pb(bk[q_])], [bAR])
                if not is_sample:
                    dma(kp[r0:r0 + nsub * 128, :].rearrange("(s p) c -> p s c", p=128), F3[:, :, 0:128], [bAR], [], eng="act")
                    dma(vp[r0:r0 + nsub * 128, :].rearrange("(s p) c -> p s c", p=128), F3[:, :, 128:256], [bAR], [], eng="act")
                    kt_dst, bkt = KT[:, c0:c0 + nsub * 128], bKT
                    V("dve", "tensor_copy", [bAR], [bVE], VE[:pr, c0 // 128:c0 // 128 + nsub, :, 0:64],
                      F3[:, :, 128:256].rearrange("p s (g d) -> p s g d", d=64))
                else:
                    dma(kso[0:pr, :], F[:pr, 0:128], [bAR], [], eng="act")
                    dma(vso[0:pr, :], F[:pr, 128:256], [bAR], [], eng="act")
                    kt_dst, bkt = KTS[:, 0:pr], bKTS
                    V("dve", "tensor_copy", [bAR], [bVES], VES[:pr, :, 0:64],
                      F[:pr, 128:256].rearrange("p (g d) -> p g d", d=64))
                    V("dve", "tensor_copy", [bAR], [bKVF], KVF[:pr, 128:256], F[:pr, 128:256])
                act(BB[:pr, 0:nsub * 128].rearrange("p (s c) -> p s c", c=128), F3[:, :, 0:128], AF.Copy, [bAR], [bBB])
                b2 = PA.get()
                for s in range(nsub):
                    tr(pbf(b2)[:, s * 128:s * 128 + pr], BB[:pr, s * 128:(s + 1) * 128], IDB[:pr, :pr],
                       [bBB, bIDB], [pb(b2)])
                if not is_sample:
                    V("dve", "tensor_copy", [pb(b2)], [bkt], kt_dst, pbf(b2)[:, 0:nsub * 128])
                else:
                    V("dve", "tensor_copy", [pb(b2)], [bkt], kt_dst, pbf(b2)[:, 0:pr])
                PA.rel(b2)
            elif j in (9, 10):
                for q_ in range(nbank):
                    act(F3b[q_][:, :, :], P3[q_][:, :, :], AF.Copy, [pb(bk[q_])], [bAR])
                rope_b(F, bAR, G, bGG, 4, 256, 0, STGH[par])
                act(BB[:pr, 0:nsub * 256], F[:pr, 0:nsub * 256], AF.Copy, [bAR], [bBB])
                if is_sample and stage >= 3:
                    for jl in range(2):
                        jj = (j - 9) * 2 + jl
                        for half in range(2):
                            hh_ = half * 4 + jj
                            src_ = F[:pr, jl * 128 + half * 64:jl * 128 + half * 64 + 64]
                            V("dve", "tensor_copy", [bAR], [bQID], QID[:pr, hh_, :, :],
                              src_.unsqueeze(1).to_broadcast([pr, 2, 64]))
                b2 = PA.get()
                for s in range(nsub):
                    for rr in range(2):
                        tr(pbf(b2)[:, (s * 2 + rr) * 128:(s * 2 + rr) * 128 + pr],
                           BB[:pr, s * 256 + rr * 128:s * 256 + (rr + 1) * 128], IDB[:pr, :pr], [bBB, bIDB], [pb(b2)])
                r_ = (j - 9) * 2
                V("dve", "tensor_copy", [pb(b2)], [bQIT],
                  QIT[:, r_:r_ + 2, 0:nsub * 128].rearrange("p r (s t) -> p s r t", t=128)[:, :, :, 0:pr],
                  pbf(b2)[:, 0:nsub * 256].rearrange("p (s r t) -> p s r t", r=2, t=128)[:, :, :, 0:pr])
                PA.rel(b2)
            else:
                for q_ in range(nbank):
                    act(F3b[q_][:, :, 0:64], P3[q_][:, :, 0:64], AF.Copy, [pb(bk[q_])], [bAR])
                    V("dve", "tensor_copy", [pb(bk[q_])], [bWI], WI[:pr, q_ * 2:q_ * 2 + ns2[q_], :], P3[q_][:, :, 64:72])
                rope_b(F, bAR, G, bGG, 1, 256, 0, STGH[par])
                if not is_sample:
                    dma(ip[r0:r0 + nsub * 128, :].rearrange("(s p) c -> p s c", p=128), F3[:, :, 0:64], [bAR], [], eng="act")
                    kit_dst, bkit = KIT[:, c0:c0 + nsub * 128], bKIT
                else:
                    dma(iso[0:pr, :], F[:pr, 0:64], [bAR], [], eng="act")
                    kit_dst, bkit = KITS[:, 0:pr], bKITS
                bb3 = BB[:pr, 0:nsub * 128].rearrange("p (s c) -> p s c", c=128)
                act(bb3[:, :, 0:64], F3[:, :, 0:64], AF.Copy, [bAR], [bBB])
                act(bb3[:, :, 64:128], F3[:, :, 0:64], AF.Copy, [bAR], [bBB])
                b2 = PA.get()
                for s in range(nsub):
                    tr(pbf(b2)[:, s * 128:s * 128 + pr], BB[:pr, s * 128:(s + 1) * 128], IDB[:pr, :pr],
                       [bBB, bIDB], [pb(b2)])
                if not is_sample:
                    V("dve", "tensor_copy", [pb(b2)], [bkit], kit_dst, pbf(b2)[:, 0:nsub * 128])
                else:
                    V("dve", "tensor_copy", [pb(b2)], [bkit], kit_dst, pbf(b2)[:, 0:pr])
                PA.rel(b2)
            for b in bk:
                PA.rel(b)

        ctx = tmA(6)
        for j in range(6, 12):
            nctx = tmA(j + 1) if j < 11 else None
            tmB(j, ctx)
            ctx = nctx


    KTS, bKTS = sb("KTS", [128, 16], BF16)
    KITS, bKITS = sb("KITS", [128, 16], BF16)
    VES, bVES = sb("VES", [16, 2, 65], BF16)
    V("pool", "memset", [], [bVES], VES[:, :, :], 1.0)

    def attn_prompt_block(blk):
        bi = blk % 4

        def geom(s):
            i = bi * 4 + s
            nk = i + 1
            return i, nk, nk * 128, (nk * 128 + 511) // 512

        def idx_prep(s):
            QIBD, bQIBD = QIBD2[s % 2]
            WSEL, bWSEL = WSEL2[s % 2]
            for half in range(2):
                srcv = QIT[half * 64:(half + 1) * 64, :, s * 128:(s + 1) * 128].rearrange(
                    "p j (g t) -> p g j t", t=16)
                dst = QIBD[half * 64:(half + 1) * 64, :, half * 64:(half + 1) * 64].rearrange(
                    "p g (j t) -> p g j t", t=16)
                V("pool", "tensor_copy", [bQIT], [bQIBD], dst, srcv)
            V("dve", "tensor_copy", [bWI], [bWIREP], WIREP[:, :].rearrange("p (h t) -> p h t", t=16),
              WI[:, s, :].unsqueeze(2).to_broadcast([128, 8, 16]))
            b = PA.get()
            tr(pf(b)[:, 0:128], WIREP[:, :], IDF[:, :], [bWIREP, bIDF], [pb(b)])
            V("dve", "tensor_tensor", [bEM, pb(b)], [bWSEL], WSEL[:, :, :], EM[:, :, :],
              pf(b)[:, 0:128].unsqueeze(1).to_broadcast([128, 8, 128]), ALU.mult)
            PA.rel(b)

        def idx_compute(s):
            i, nk, Sk, npc = geom(s)
            QIBD, bQIBD = QIBD2[s % 2]
            WSEL, bWSEL = WSEL2[s % 2]
            held = []
            items = [(p, grp) for p in range(npc) for grp in range(8)]
            bIs = {}

            def score(k):
                p, grp = items[k]
                w = min(512, Sk - p * 512)
                bs = PA.get()
                mm(pf(bs)[:, :w], QIBD[:, grp, :], KIT[:, p * 512:p * 512 + w], True, True,
                   [bQIBD, bKIT], [pb(bs)])
                return bs

            nxt_bs = score(0)
            for k, (p, grp) in enumerate(items):
                w = min(512, Sk - p * 512)
                bs = nxt_bs
                if k + 1 < len(items):
                    nxt_bs = score(k + 1)
                if grp == 0:
                    bIs[p] = PA.get()
                bI = bIs[p]
                rr, brr = RR[k % 3]
                act(rr[:, :w], pf(bs)[:, :w], AF.Relu, [pb(bs)], [brr])
                PA.rel(bs)
                mm(pf(bI)[:, :w], WSEL[:, grp, :], rr[:, :w], grp == 0, grp == 7, [bWSEL, brr], [pb(bI)])
                if grp == 7:
                    held.append((bI, p, w))
            return held

        def idx_evac(s, held):
            i, nk, Sk, npc = geom(s)
            for (bI, p, w) in held:
                act(ISB[:, p * 512:p * 512 + w], pf(bI)[:, :w], AF.Copy, [pb(bI)], [bISB])
                PA.rel(bI)

        def bisect(s):
            i, nk, Sk, npc = geom(s)
            if i >= 2:
                V("dve", "tensor_reduce", [bISB], [bBS], BS[:, 0:1], ISB[:, :Sk], AX.X, ALU.max,
                  apply_absolute_value=True)
                V("dve", "tensor_scalar", [bBS], [bBS], BS[:, 1:2], BS[:, 0:1], 2.0, 2.0, ALU.mult, ALU.add)
                V("dve", "tensor_scalar", [bBS, bCPOW], [bHW], HW[:, :], CPOW[:, :], BS[:, 1:2], None, ALU.mult)
                V("dve", "memset", [], [bMIDP], MIDP[:, :], 0.0)
            V("dve", "tensor_tensor", [bISB, bCM], [bISB], ISB[:, i * 128:(i + 1) * 128],
              ISB[:, i * 128:(i + 1) * 128], CM[:, :], ALU.add)
            if i >= 2:
                for n in range(N_BISECT):
                    V("dve", "tensor_scalar", [bISB, bMIDP], [bJUNK, bBS], JUNK[:, :Sk], ISB[:, :Sk], MIDP[:, 0:1],
                      None, ALU.is_ge, ALU.add, accum_out=BS[:, 4:5])
                    V("dve", "scalar_tensor_tensor", [bBS, bHW], [bBS], BS[:, 5:6], BS[:, 4:5], K_TOP - 0.5,
                      HW[:, n:n + 1], ALU.is_ge, ALU.mult)
                    V("dve", "scalar_tensor_tensor", [bMIDP, bHW, bBS], [bMIDP], MIDP[:, 0:1], MIDP[:, 0:1],
                      HW[:, n + 1:n + 2], BS[:, 5:6], ALU.subtract, ALU.add)
                V("dve", "tensor_scalar", [bMIDP, bHW], [bBS], BS[:, 2:3], MIDP[:, 0:1], HW[:, N_BISECT:N_BISECT + 1],
                  None, ALU.subtract)
                V("dve", "tensor_scalar", [bISB, bBS], [bMASKB], MASKB[:, :Sk], ISB[:, :Sk], BS[:, 2:3], None,
                  ALU.is_ge)
            else:
                V("dve", "tensor_scalar", [bISB], [bMASKB], MASKB[:, :Sk], ISB[:, :Sk], -1.0e29, None, ALU.is_ge)

        def masktrans(s):
            i, nk, Sk, npc = geom(s)
            for c0 in range(0, nk, 8):
                n = min(8, nk - c0)
                b = PA.get()
                for c in range(c0, c0 + n):
                    tr(pbf(b)[:, (c - c0) * 128:(c - c0 + 1) * 128], MASKB[:, c * 128:(c + 1) * 128], IDB[:, :],
                       [bMASKB, bIDB], [pb(b)])
                act(MASKT[:, c0:c0 + n, :], pbf(b)[:, 0:n * 128].rearrange("p (c t) -> p c t", t=128), AF.Identity,
                    [pb(b), bMB], [bMASKT], scale=60000.0, bias=MBIAS[:, 0:1])
                PA.rel(b)

        def attend(s):
            i, nk, Sk, npc = geom(s)
            bO = [PA.get(), PA.get()]
            aitems = [(c, g) for c in range(nk) for g in range(2)]

            def qk(k):
                c, g = aitems[k]
                bs = PA.get()
                mm(pf(bs)[:, :], KT[g * 64:(g + 1) * 64, c * 128:(c + 1) * 128],
                   QT[g * 64:(g + 1) * 64, s, :, :].rearrange("p r t -> p (r t)"), True, False,
                   [bKT, bQT], [pb(bs)])
                for r in range(4):
                    mm(pf(bs)[:, r * 128:(r + 1) * 128], IDB[:, :], MASKT[:, c, :], False, r == 3,
                       [bIDB, bMASKT], [pb(bs)])
                return bs

            nxt_bs = qk(0)
            for k, (c, g) in enumerate(aitems):
                bs = nxt_bs
                if k + 1 < len(aitems):
                    nxt_bs = qk(k + 1)
                pt_, bpt = (PEXP + PTB)[k % 4]
                act(pt_[:, :], pf(bs)[:, :], AF.Exp, [pb(bs)], [bpt], scale=0.125)
                PA.rel(bs)
                for r in range(4):
                    mm(pf(bO[g])[:, r * 65:(r + 1) * 65], pt_[:, r * 128:(r + 1) * 128], VE[:, c, g, :],
                       c == 0 and r == 0, c == nk - 1 and r == 3, [bpt, bVE], [pb(bO[g])])
            for g in range(2):
                o3 = pf(bO[g])[:, 0:260].rearrange("p (r e) -> p r e", e=65)
                act(DEN[:, g * 4:(g + 1) * 4].unsqueeze(2), o3[:, :, 64:65], AF.Copy, [pb(bO[g])], [bDEN])
            V("pool", "tensor_tensor", [bDEN, bNEG1], [bRDEN], RDEN[:, :], DEN[:, :], NEG1[:, :], ALU.pow)
            for g in range(2):
                o3 = pf(bO[g])[:, 0:260].rearrange("p (r e) -> p r e", e=65)
                for r in range(4):
                    act(OB[:, r, g, :], o3[:, r, 0:64], AF.Copy, [pb(bO[g]), bRDEN], [bOB],
                        scale=RDEN[:, g * 4 + r:g * 4 + r + 1])
                PA.rel(bO[g])
            b = PA.get()
            for r in range(4):
                tr(pbf(b)[:, r * 128:(r + 1) * 128], OB[:, r, :, :].rearrange("p g d -> p (g d)"), IDB[:, :],
                   [bOB, bIDB], [pb(b)])
            act(OT[:, :, :], pbf(b)[:, 0:512].rearrange("p (r t) -> p r t", r=4), AF.Copy, [pb(b)], [bOT])
            PA.rel(b)
            w_out_add(s, 128, True)

        idx_prep(0)
        held = idx_compute(0)
        idx_evac(0, held)
        idx_prep(1)
        idx_prep(2)
        bisect(0)
        held = idx_compute(1)
        early2 = idx_compute(2) if bi <= 1 else None
        masktrans(0)
        idx_evac(1, held)
        for s in range(4):
            if s == 0 and early2 is not None:
                held2 = early2
            else:
                held2 = idx_compute(s + 2) if s + 2 <= 3 else None
            if s + 1 <= 3:
                bisect(s + 1)
            attend(s)
            if held2 is not None:
                idx_evac(s + 2, held2)
                if s + 3 <= 3:
                    idx_prep(s + 3)
            if s + 1 <= 3:
                masktrans(s + 1)

    def w_out_add(s, pr, with_attn, ot_cols=None, otbuf=None):
        for hh in range(2):
            b = PA.get()
            nk8 = 8 if with_attn else 4
            for k8 in range(nk8):
                if k8 < 4:
                    lhs, rb = YCT[:, k8, s * 128:s * 128 + pr], bYCT
                else:
                    lhs, rb = (OT[:, k8 - 4, 0:pr] if ot_cols is None else ot_cols(k8 - 4)), (bOT if otbuf is None else otbuf)
                mm(pf(b)[:pr, :], lhs, WOUT[:, k8, hh * 512:(hh + 1) * 512], k8 == 0, k8 == nk8 - 1,
                   [rb, bWOUT], [pb(b)])
            if with_attn and pr == 128:
                tx, btx = ST[1]
                act(tx[:pr, :], pf(b)[:pr, :], AF.Copy, [pb(b)], [btx])
                V("pool", "tensor_tensor", [btx, bXs[s]], [bXs[s]], X[:pr, s, hh * 512:(hh + 1) * 512], tx[:pr, :],
                  X[:pr, s, hh * 512:(hh + 1) * 512], ALU.add)
            else:
                V("dve", "tensor_tensor", [pb(b), bXs[s]], [bXs[s]], X[:pr, s, hh * 512:(hh + 1) * 512], pf(b)[:pr, :],
                  X[:pr, s, hh * 512:(hh + 1) * 512], ALU.add)
            PA.rel(b)

    N_BIS_S = 16
    if stage >= 3:
        IT = X[:, 1:4, :].rearrange("p a n -> p (a n)")[:, 0:2064].rearrange("p (b k t) -> p b k t", b=4, k=129)
        bIT = bX
        CMPB = XNT[:, :, :].rearrange("p k n -> p (k n)")[:, 0:2064].rearrange("p (b k t) -> p b k t", b=4, k=129)
        bCMPB = bXNT
        MASKS, bMASKS = sb("MASKS", [128, 4, 129, 4], BF16)
        QID, bQID = sb("QID", [16, 8, 2, 64], BF16)
        QITD, bQITD = sb("QITD", [128, 4, 4, 8], BF16)
        QTSB, bQTSB = sb("QTSB", [128, 4, 4, 4], BF16)
        WM, bWM = sb("WM", [16, 16, 8])
        WB, bWB = sb("WB", [128, 128])
        ONESF, bONESF = sb("ONESF", [128, 128])
        PTSB, bPTSB = sb("PTSB", [128, 4], I32)
        PTF, bPTF = sb("PTF", [128, 4])
        IDX, bIDX = sb("IDX", [128, 4, 8], I32)
        IDX4, bIDX4 = sb("IDX4", [128, 4, 4], I32)
        CMS, bCMS = sb("CMS", [128, 4, 4])
        SELM, bSELM = sb("SELM", [16, 8])
        VMK, bVMK = sb("VMK", [16, 4, 128])
        VESB, bVESB = sb("VESB", [4, 4, 2, 65], BF16)
        LO, bLO = sb("LO", [128, 16])
        MID, bMID = sb("MID", [128, 16])
        PC, bPC = sb("PC", [128, 16])
        T1, bT1 = sb("T1", [128, 16])
        AM, bAM = sb("AM", [128, 4])
        OBS, bOBS = sb("OBS", [16, 2, 64], BF16)
        OTS, bOTS = sb("OTS", [128, 4, 16], BF16)
        RDS, bRDS = sb("RDS", [16, 2])
        KTP2, bKTP2 = sb("KTP2", [128, 2048], BF16)
        VE2, bVE2 = sb("VE2", [128, 16, 2, 65], BF16)
        V("pool", "memset", [], [bVE2], VE2[:, :, :, :], 1.0)
        ck8 = ck.rearrange("n (e x) -> (n e) x", e=8)
        cv8 = cv.rearrange("n (e x) -> (n e) x", e=8)
        cik4 = cik.rearrange("n (e x) -> (n e) x", e=4)
        selm_d = din("selm", [16, 8])
        dma(PTSB[:, :], ptT, [], [bPTSB])
        dma(CMS[:, :, :], cms_d, [], [bCMS])
        dma(SELM[:, :], selm_d, [], [bSELM])
        V("pool", "memset", [], [bONESF], ONESF[:, :], 1.0)
        V("pool", "memset", [], [bVESB], VESB[:, :, :, :], 1.0)

    _bc = {}

    def bc_reg(h, val):
        if val not in _bc:
            _bc[val] = h.to_reg(val)
        return _bc[val]

    def attn_sample_block():
        V("pool", "memset", [], [bIT, bXs[1], bXs[2], bXs[3]], IT[:, :, :, :], NEG)
        V("dve", "tensor_copy", [bPTSB], [bPTF], PTF[:, :], PTSB[:, :])
        for e in range(8):
            V("dve", "tensor_scalar", [bPTF], [bIDX], IDX[:, :, e], PTF[:, :], 8.0, float(e), ALU.mult, ALU.add)
        for e in range(4):
            V("dve", "tensor_scalar", [bPTF], [bIDX4], IDX4[:, :, e], PTF[:, :], 4.0, float(e), ALU.mult, ALU.add)
        b0 = PA.get()
        for h in range(8):
            tr(pbf(b0)[:, h * 16:(h + 1) * 16], QID[:, h, :, :].rearrange("p a d -> p (a d)"), IDB[:16, :16],
               [bQID, bIDB], [pb(b0)])
        V("dve", "tensor_copy", [pb(b0)], [bQITD], QITD[:, :, :, :].rearrange("p b t h -> p h b t"),
          pbf(b0)[:, 0:128].rearrange("p (h b t) -> p h b t", h=8, b=4))
        PA.rel(b0)
        V("dve", "tensor_copy", [bQT], [bQTSB], QTSB[:, :, :, :].rearrange("p b r t -> p r b t"),
          QT[:, 0, :, 0:16].rearrange("p r (b t) -> p r b t", t=4))
        V("dve", "tensor_tensor", [bWI, bIDF], [bWM], WM[:, :, :], WI[:16, 0, :].unsqueeze(1).to_broadcast([16, 16, 8]),
          IDF[:16, :16].unsqueeze(2).to_broadcast([16, 16, 8]), ALU.mult)
        b0 = PA.get()
        mm(pf(b0)[:, 0:128], ONESF[:16, :], WM[:, :, :].rearrange("p a h -> p (a h)"), True, True, [bONESF, bWM], [pb(b0)])
        act(WB[:, :], pf(b0)[:, 0:128], AF.Copy, [pb(b0)], [bWB])
        PA.rel(b0)
        V("dve", "tensor_tensor", [bKVF, bSELM], [bVMK], VMK[:, :, :], KVF[:16, 128:256].unsqueeze(1).to_broadcast([16, 4, 128]),
          SELM[:, 4:8].unsqueeze(2).to_broadcast([16, 4, 128]), ALU.mult)
        b0 = PA.get()
        mm(pf(b0)[0:4, :], SELM[:, 0:4], VMK[:, :, :].rearrange("p b x -> p (b x)"), True, True, [bSELM, bVMK], [pb(b0)])
        V("dve", "tensor_copy", [pb(b0)], [bVESB], VESB[:, :, :, 0:64],
          pf(b0)[0:4, :].rearrange("p (b g d) -> p b g d", b=4, g=2))
        PA.rel(b0)
        if PARTS < 1:
            return w_out_add(0, 16, False)
        IKP, bIKP = ISB, bISB
        IKB, bIKB = JUNK, bJUNK
        KITP, bKITP = MASKB, bMASKB
        for b in range(4):
            for q4 in range(4):
                S.add("pool", lambda h, b=b, q4=q4: h.indirect_dma_start(
                    out=IKP[:, :], out_offset=None, in_=cik4,
                    in_offset=bass.IndirectOffsetOnAxis(ap=IDX4[:, b, q4:q4 + 1], axis=0),
                    bounds_check=bc_reg(h, npool * 4 - 1), oob_is_err=False),
                    [bIDX4], [bIKP], dma=True)
                act(IKB[:, :], IKP[:, :], AF.Copy, [bIKP], [bIKB])
                if SUB < 1:
                    continue
                for c0 in range(0, 16, 8):
                    bb = PA.get()
                    for pp in range(8):
                        tr(pbf(bb)[:, pp * 128:(pp + 1) * 128], IKB[:, (c0 + pp) * 128:(c0 + pp + 1) * 128], IDB[:, :],
                           [bIKB, bIDB], [pb(bb)])
                    V("dve", "tensor_copy", [pb(bb)], [bKITP], KITP[:, c0 * 128:(c0 + 8) * 128], pbf(bb)[:, :])
                    PA.rel(bb)
                if SUB < 2:
                    continue
                bsh = [PA.get(), PA.get()]
                for kk in range(32):
                    half, m = kk % 2, kk // 2
                    mm(pf(bsh[half])[:, m * 32:(m + 1) * 32], KITP[half * 64:(half + 1) * 64, m * 128:(m + 1) * 128],
                       QITD[half * 64:(half + 1) * 64, b, :, :].rearrange("p t h -> p (t h)"), True, True,
                       [bKITP, bQITD], [pb(bsh[half])])
                for half in range(2):
                    rr, brr = RR[half]
                    act(rr[:, :], pf(bsh[half])[:, :], AF.Relu, [pb(bsh[half])], [brr])
                    PA.rel(bsh[half])
                    if SUB < 3:
                        continue
                    st, bst = ST[half]
                    V("dve", "tensor_tensor", [brr, bWB], [bst], st[:, :].rearrange("p (k x) -> p k x", k=16),
                      rr[:, :].rearrange("p (k x) -> p k x", k=16),
                      WB[:, b * 32:(b + 1) * 32].unsqueeze(1).to_broadcast([128, 16, 32]), ALU.mult)
                    V("dve", "tensor_reduce", [bst], [bIT],
                      IT[:, b, q4 * 32:(q4 + 1) * 32, :].rearrange("p (m two) t -> p m two t", two=2)[:, :, half, :],
                      st[:, :].rearrange("p (k t h) -> p k t h", k=16, t=4), AX.X, ALU.add)
            if SUB < 4:
                continue
            bs = PA.get()
            mm(pf(bs)[0:4, 0:32], KITS[0:64, b * 4:(b + 1) * 4], QITD[0:64, b, :, :].rearrange("p t h -> p (t h)"),
               True, True, [bKITS, bQITD], [pb(bs)])
            rr, brr = RR[2]
            act(rr[0:4, 0:32], pf(bs)[0:4, 0:32], AF.Relu, [pb(bs)], [brr])
            PA.rel(bs)
            st, bst = ST[0]
            V("dve", "tensor_tensor", [brr, bWB], [bst], st[0:4, 0:32], rr[0:4, 0:32], WB[0:4, b * 32:(b + 1) * 32], ALU.mult)
            V("dve", "tensor_reduce", [bst], [bAM], AM[0:4, 0:4], st[0:4, 0:32].rearrange("p (t h) -> p t h", t=4),
              AX.X, ALU.add)
            V("dve", "tensor_tensor", [bAM, bCMS], [bIT], IT[0:4, b, 128, :], AM[0:4, 0:4], CMS[0:4, b, :], ALU.add)
        if PARTS < 2:
            return w_out_add(0, 16, False)
        V("dve", "tensor_reduce", [bIT], [bAM], AM[:, 0:1], IT[:, :, 0:128, :], AX.XYZ, ALU.max, apply_absolute_value=True)
        b0 = PA.get()
        tr(pf(b0)[0:1, 0:128], AM[:, 0:1], IDF[:, :], [bAM, bIDF], [pb(b0)])
        V("dve", "tensor_reduce", [pb(b0)], [bAM], AM[0:1, 1:2], pf(b0)[0:1, 0:128], AX.X, ALU.max)
        PA.rel(b0)
        b0 = PA.get()
        mm(pf(b0)[:, 0:2], ONESF[0:1, :], AM[0:1, 1:3], True, True, [bONESF, bAM], [pb(b0)])
        V("dve", "tensor_scalar", [pb(b0)], [bAM], AM[:, 2:3], pf(b0)[:, 0:1], 2.0, 2.0, ALU.mult, ALU.add)
        V("dve", "tensor_scalar", [pb(b0)], [bAM], AM[:, 3:4], pf(b0)[:, 0:1], -1.0, -1.0, ALU.mult, ALU.add)
        PA.rel(b0)
        V("dve", "tensor_copy", [bAM], [bLO], LO[:, :], AM[:, 3:4].to_broadcast([128, 16]))
        if PARTS < 3:
            return w_out_add(0, 16, False)
        itv = IT[:, :, :, :].rearrange("p b k t -> p b t k")
        cmv = CMPB[:, :, :, :].rearrange("p b k t -> p b t k")
        for n in range(N_BIS_S):
            c = 2.0 ** -(n + 1)
            V("dve", "scalar_tensor_tensor", [bAM, bLO], [bMID], MID[:, :], AM[:, 2:3].to_broadcast([128, 16]), c, LO[:, :],
              ALU.mult, ALU.add)
            V("dve", "tensor_tensor", [bIT, bMID], [bCMPB], cmv, itv,
              MID[:, :].rearrange("p (b t) -> p b t", t=4).unsqueeze(3).to_broadcast([128, 4, 4, 129]), ALU.is_ge)
            V("dve", "tensor_reduce", [bCMPB], [bPC], PC[:, :].rearrange("p (b t) -> p b t", t=4), cmv, AX.X, ALU.add)
            b0 = PA.get()
            mm(pf(b0)[:, 0:16], ONESF[:, :], PC[:, :], True, True, [bONESF, bPC], [pb(b0)])
            V("dve", "tensor_scalar", [pb(b0)], [bT1], T1[:, :], pf(b0)[:, 0:16], K_TOP - 0.5, c, ALU.is_ge, ALU.mult)
            PA.rel(b0)
            V("dve", "scalar_tensor_tensor", [bT1, bAM, bLO], [bLO], LO[:, :], T1[:, :], AM[:, 2:3], LO[:, :], ALU.mult, ALU.add)
        V("dve", "tensor_tensor", [bIT, bLO], [bMASKS], MASKS[:, :, :, :].rearrange("p b k t -> p b t k"), itv,
          LO[:, :].rearrange("p (b t) -> p b t", t=4).unsqueeze(3).to_broadcast([128, 4, 4, 129]), ALU.is_ge)
        if PARTS < 4:
            return w_out_add(0, 16, False)
        KPs = [(ISB, [bISB]), (STG[1][0], [STG[1][1], STGG[1], STGH[1]])]
        VPs = [(STG[0][0], [STG[0][1], STGG[0], STGH[0]]), (XNT[:, :, :].rearrange("p k n -> p (k n)").bitcast(F32), [bXNT])]
        KPBs = [(JUNK, bJUNK), (MASKT[:, :, :].rearrange("p c t -> p (c t)"), bMASKT)]
        KTPs = [(MASKB, bMASKB), (KTP2, bKTP2)]
        VEs = [(VE, bVE), (VE2, bVE2)]
        steps = [(b, e) for b in range(4) for e in range(8)]

        def gather(it):
            b, e = steps[it]
            KP, bKP = KPs[it % 2]
            VP, bVP = VPs[it % 2]
            S.add("pool", lambda h, b=b, e=e, KP=KP: h.indirect_dma_start(
                out=KP[:, :], out_offset=None, in_=ck8,
                in_offset=bass.IndirectOffsetOnAxis(ap=IDX[:, b, e:e + 1], axis=0),
                bounds_check=bc_reg(h, npool * 8 - 1), oob_is_err=False),
                [bIDX], bKP, dma=True)
            S.add("pool", lambda h, b=b, e=e, VP=VP: h.indirect_dma_start(
                out=VP[:, :], out_offset=None, in_=cv8,
                in_offset=bass.IndirectOffsetOnAxis(ap=IDX[:, b, e:e + 1], axis=0),
                bounds_check=bc_reg(h, npool * 8 - 1), oob_is_err=False),
                [bIDX], bVP, dma=True)

        def cast(it):
            KP, bKP = KPs[it % 2]
            VP, bVP = VPs[it % 2]
            KPB, bKPB = KPBs[it % 2]
            VEx, bVEx = VEs[it % 2]
            act(KPB[:, :], KP[:, :], AF.Copy, bKP, [bKPB])
            act(VEx[:, :, :, 0:64], VP[:, :].rearrange("p (k g d) -> p k g d", k=16, g=2), AF.Copy, bVP, [bVEx])

        state = {}

        def compute(it):
            b, e = steps[it]
            KPB, bKPB = KPBs[it % 2]
            KTP, bKTP = KTPs[it % 2]
            VEx, bVEx = VEs[it % 2]
            if e == 0:
                state["bO"] = PA.get()
                state["first"] = True
            bO = state["bO"]
            for c0 in range(0, 16, 8):
                bb = PA.get()
                for pp in range(8):
                    tr(pbf(bb)[:, pp * 128:(pp + 1) * 128], KPB[:, (c0 + pp) * 128:(c0 + pp + 1) * 128], IDB[:, :],
                       [bKPB, bIDB], [pb(bb)])
                V("dve", "tensor_copy", [pb(bb)], [bKTP], KTP[:, c0 * 128:(c0 + 8) * 128], pbf(bb)[:, :])
                PA.rel(bb)
            bsg = [PA.get(), PA.get()]
            for kl in range(16):
                for g in range(2):
                    mm(pf(bsg[g])[:, kl * 16:(kl + 1) * 16],
                       KTP[g * 64:(g + 1) * 64, kl * 128:(kl + 1) * 128],
                       QTSB[g * 64:(g + 1) * 64, b, :, :].rearrange("p r t -> p (r t)"), True, True,
                       [bKTP, bQTSB], [pb(bsg[g])])
            for g in range(2):
                pe_, bpe = PEXP[g]
                act(pe_[:, 0:256], pf(bsg[g])[:, 0:256], AF.Exp, [pb(bsg[g])], [bpe], scale=0.125)
                PA.rel(bsg[g])
                pt_, bpt = PTB[g]
                V("pool", "tensor_tensor", [bpe, bMASKS], [bpt],
                  pt_[:, 0:256].rearrange("p (k r t) -> p k r t", k=16, t=4),
                  pe_[:, 0:256].rearrange("p (k r t) -> p k r t", k=16, t=4),
                  MASKS[:, b, e * 16:(e + 1) * 16, :].unsqueeze(2).to_broadcast([128, 16, 4, 4]), ALU.mult)
            for kl in range(16):
                for g in range(2):
                    pt_, bpt = PTB[g]
                    mm(pf(bO)[0:16, g * 65:(g + 1) * 65], pt_[:, kl * 16:(kl + 1) * 16],
                       VEx[:, kl, g, :], state["first"], False, [bpt, bVEx], [pb(bO)])
                    state["first"] = False
            if e == 7:
                finish_seq(b, bO)

        def finish_seq(b, bO):
                for g in range(2):
                    bs = PA.get()
                    mm(pf(bs)[0:4, 0:16], KTS[g * 64:(g + 1) * 64, b * 4:(b + 1) * 4],
                       QTSB[g * 64:(g + 1) * 64, b, :, :].rearrange("p r t -> p (r t)"), True, True, [bKTS, bQTSB], [pb(bs)])
                    pe_, bpe = PEXP[g]
                    act(pe_[0:4, 0:16], pf(bs)[0:4, 0:16], AF.Exp, [pb(bs)], [bpe], scale=0.125)
                    PA.rel(bs)
                    pt_, bpt = PTB[g]
                    V("pool", "tensor_tensor", [bpe, bMASKS], [bpt], pt_[0:4, 0:16].rearrange("p (r t) -> p r t", t=4),
                      pe_[0:4, 0:16].rearrange("p (r t) -> p r t", t=4),
                      MASKS[0:4, b, 128, :].unsqueeze(1).to_broadcast([4, 4, 4]), ALU.mult)
                    mm(pf(bO)[0:16, g * 65:(g + 1) * 65], pt_[0:4, 0:16], VESB[0:4, b, g, :], False, g == 1,
                       [bpt, bVESB], [pb(bO)])
                o3 = pf(bO)[0:16, 0:130].rearrange("p (g e) -> p g e", e=65)
                V("dve", "reciprocal", [pb(bO)], [bRDS], RDS[:, :].unsqueeze(2), o3[:, :, 64:65])
                V("dve", "tensor_tensor", [pb(bO), bRDS], [bOBS], OBS[:, :, :], o3[:, :, 0:64],
                  RDS[:, :].unsqueeze(2).to_broadcast([16, 2, 64]), ALU.mult)
                PA.rel(bO)
                b0 = PA.get()
                tr(pbf(b0)[:, 0:16], OBS[:, :, :].rearrange("p g d -> p (g d)"), IDB[:16, :16], [bOBS, bIDB], [pb(b0)])
                V("dve", "tensor_copy", [pb(b0)], [bOTS], OTS[:, :, b * 4:(b + 1) * 4],
                  pbf(b0)[:, 0:16].rearrange("p (r t) -> p r t", t=4))
                PA.rel(b0)

        gather(0)
        cast(0)
        for it in range(len(steps)):
            if it + 1 < len(steps):
                gather(it + 1)
            compute(it)
            if it + 1 < len(steps):
                cast(it + 1)
        w_out_add(0, 16, True, ot_cols=lambda r: OTS[:, r, :], otbuf=bOTS)

    nblk_prompt = 8
    for blk in range(nblk_prompt + 1):
        is_sample = blk == nblk_prompt
        cur_blk[0] = blk
        if not is_sample:
            nsub, pr, nb = 4, 128, 512
            for s_ in range(4):
                dma(X[:, s_, :], xp[blk * 512 + s_ * 128:blk * 512 + (s_ + 1) * 128, :], [], [bXs[s_]], eng="act")
        else:
            nsub, pr, nb = 1, 16, 16
            dma(X[:16, 0, :], xs, [], [bXs[0]])
            for i4 in range(4):
                dma(UEXT[:, i4, 0:24].rearrange("p (b t) -> p b t", t=6)[:, :, 0:2], cconv[:, i4, :, :],
                    [bUEXT], [bUEXT])
        norm_block(nsub, pr)
        ffn_block(wf1, 0, nsub, pr, nb, "wf1")
        mix_in_block(blk, nsub, pr, nb, is_sample)
        if stage >= 2:
            if not is_sample:
                attn_prompt_block(blk)
            elif stage >= 3:
                attn_sample_block()
            else:
                w_out_add(0, 16, False)
            norm_block(nsub, pr)
            if not is_sample:
                def store_sub(s_, blk=blk):
                    dma(yp[blk * 512 + s_ * 128:blk * 512 + (s_ + 1) * 128, :], X[:, s_, :], [bXs[s_]], [], eng="act")
                ffn_block(wf2, 16, nsub, pr, nb, "wf2", after_sub=store_sub)
            else:
                ffn_block(wf2, 16, nsub, pr, nb, "wf2")
        if not is_sample:
            if stage < 2:
                for s_ in range(4):
                    dma(yp[blk * 512 + s_ * 128:blk * 512 + (s_ + 1) * 128, :], X[:, s_, :], [bXs[s_]], [], eng="act")
        else:
            dma(ys, X[:16, 0, :], [bXs[0]], [])

    S.emit(nc, es)
    es.close()
    return nc


def _pieces_kn(W, ncol_piece=256):
    K, N = W.shape
    nj = N // ncol_piece
    a = W.reshape(8, 128, nj, ncol_piece).transpose(2, 1, 0, 3)
    return np.ascontiguousarray(a.reshape(nj, 128, 8 * ncol_piece))


def _pieces_rows(W, rows_piece=256):
    R, N = W.shape
    ng = R // rows_piece
    a = W.reshape(ng, 2, 128, N).transpose(0, 2, 1, 3)
    return np.ascontiguousarray(a.reshape(ng, 128, 2 * N))


def _ffn_pieces(wg, wu, wd):
    pg = _pieces_kn(wg)
    pu = _pieces_kn(wu)
    pd = _pieces_rows(wd)
    out = np.empty((33, 128, 2048), np.float32)
    out[0::3] = pg
    out[1::3] = pu
    out[2::3] = pd
    return out


def _perm_heads():
    idx = np.empty(512, np.int64)
    for r in range(4):
        for g in range(2):
            for d in range(64):
                idx[r * 128 + g * 64 + d] = (g * 4 + r) * 64 + d
    return idx


def _prep_shared(inp):
    f = lambda k: np.asarray(inp[k], np.float32)[0]
    sh = {}
    sh["wf1"] = _ffn_pieces(f("ffn1_w_gate"), f("ffn1_w_up"), f("ffn1_w_down"))
    sh["wf2"] = _ffn_pieces(f("ffn2_w_gate"), f("ffn2_w_up"), f("ffn2_w_down"))
    w_in = f("w_in")
    ph = _perm_heads()
    gb, gc, hc = w_in[:, 0:512], w_in[:, 512:1024], w_in[:, 1024:1536]
    q = w_in[:, 1536:2048][:, ph]
    k = w_in[:, 2048:2176]
    v = w_in[:, 2176:2304]
    qi = w_in[:, 2304:2816][:, ph]
    ki = w_in[:, 2816:2880]
    wi = w_in[:, 2880:2888]
    conv_cols = []
    for i in range(4):
        for t in (gb, gc, hc):
            conv_cols.append(t[:, i * 128:(i + 1) * 128])
    pad = np.zeros((1024, 256 - 72), np.float32)
    wperm = np.concatenate(conv_cols + [q, k, v, qi, ki, wi, pad], axis=1)
    assert wperm.shape[1] == 3072
    sh["win"] = _pieces_kn(wperm)
    w_out = f("w_out")
    wo = np.empty((128, 8, 1024), np.float32)
    for i in range(4):
        wo[:, i, :] = w_out[i * 128:(i + 1) * 128, :]
    for r in range(4):
        for g in range(2):
            h = g * 4 + r
            wo[g * 64:(g + 1) * 64, 4 + r, :] = w_out[512 + h * 64:512 + (h + 1) * 64, :]
    sh["wout"] = np.ascontiguousarray(wo.reshape(128, 4, 2048).transpose(1, 0, 2))
    gains = np.empty((128, 24), np.float32)
    for n, key in enumerate(("ffn1_norm", "mix_norm", "ffn2_norm")):
        gains[:, n * 8:(n + 1) * 8] = f(key).reshape(8, 128).T
    sh["gains"] = gains
    cw = f("conv_w")
    cb = f("conv_b")
    convp = np.empty((128, 16), np.float32)
    for i in range(4):
        for j in range(3):
            convp[:, i * 4 + j] = cw[j, i * 128:(i + 1) * 128]
        convp[:, i * 4 + 3] = cb[i * 128:(i + 1) * 128]
    sh["convp"] = convp
    qkg = np.empty((128, 128), np.float32)
    qkg[:, 0:64] = f("q_norm")[None, :]
    qkg[:, 64:128] = f("k_norm")[None, :]
    sh["qkg"] = qkg
    sh["identb"] = np.eye(128, dtype=np.float32).astype(ml_dtypes.bfloat16)
    sh["identf"] = np.eye(128, dtype=np.float32)
    t = np.arange(128)
    sh["cm"] = np.where(t[None, :] <= t[:, None], 0.0, NEG).astype(np.float32)
    em = np.zeros((128, 8, 128), np.float32)
    for half in range(2):
        for j in range(4):
            for tt in range(16):
                for grp in range(8):
                    em[half * 64 + j * 16 + tt, grp, grp * 16 + tt] = 1.0
    sh["emat"] = em.reshape(128, 1024).astype(ml_dtypes.bfloat16)
    inv = np.power(np.float32(500000.0), -np.arange(8, dtype=np.float32) * np.float32(2.0 / 16)).astype(np.float32)
    pos = np.arange(SEQ, dtype=np.float32)
    ang = (pos[:, None] * inv[None, :]).astype(np.float32)
    rp = np.empty((128, 2, 16, 8), np.float32)
    rp[:, 0] = np.cos(ang).reshape(16, 128, 8).transpose(1, 0, 2)
    rp[:, 1] = np.sin(ang).reshape(16, 128, 8).transpose(1, 0, 2)
    sh["ropep"] = rp
    poss = (16384 + np.arange(4)).astype(np.float32)
    angs = (poss[:, None] * inv[None, :]).astype(np.float32)
    rs = np.empty((16, 2, 8), np.float32)
    rs[:, 0] = np.tile(np.cos(angs), (4, 1))
    rs[:, 1] = np.tile(np.sin(angs), (4, 1))
    sh["ropes"] = rs
    cms = np.full((128, 4, 4), NEG, np.float32)
    for tp in range(4):
        for tt in range(4):
            if tp <= tt:
                cms[tp, :, tt] = 0.0
    sh["cms"] = cms
    sh["iotap"] = np.arange(128, dtype=np.float32).reshape(128, 1)
    sh["cpow"] = np.tile((2.0 ** -(np.arange(32, dtype=np.float64) + 1)).astype(np.float32)[None, :], (128, 1))
    selm = np.zeros((16, 8), np.float32)
    for b in range(4):
        for t in range(4):
            selm[b * 4 + t, t] = 1.0
            selm[b * 4 + t, 4 + b] = 1.0
    sh["selm"] = selm
    sh["ck"] = np.ascontiguousarray(np.asarray(inp["cache_k"], np.float32)[0].reshape(NPOOL, 128 * 128))
    sh["cv"] = np.ascontiguousarray(np.asarray(inp["cache_v"], np.float32)[0].reshape(NPOOL, 128 * 128))
    sh["cik"] = np.ascontiguousarray(np.asarray(inp["cache_idx_k"], np.float32)[0].reshape(NPOOL, 128 * 64))
    return sh


_PROG = {}


def kernel(**inp):
    stage = inp.pop("_stage", 3)
    if stage not in _PROG:
        _PROG[stage] = build_program(stage)
    nc = _PROG[stage]
    sh = _prep_shared(inp)
    if stage < 3:
        for k_ in ("ck", "cv", "cik"):
            sh[k_] = np.ascontiguousarray(sh[k_][:2])
        sh.pop("selm")
    x_prompt = np.asarray(inp["x_prompt"], np.float32)
    x_sample = np.asarray(inp["x_sample"], np.float32)
    cache_conv = np.asarray(inp["cache_conv"], np.float32)[0]
    page_table = np.asarray(inp["page_table"], np.int32)
    in_maps = []
    for c in range(NCORES):
        m = dict(sh)
        m["xp"] = np.ascontiguousarray(x_prompt[2 * c:2 * c + 2].reshape(2 * SEQ, D))
        m["xs"] = np.ascontiguousarray(x_sample[4 * c:4 * c + 4].reshape(16, D))
        cc = cache_conv[4 * c:4 * c + 4]
        m["cconv"] = np.ascontiguousarray(cc.reshape(4, 2, 4, 128).transpose(3, 2, 0, 1))
        m["ptT"] = np.ascontiguousarray(page_table[4 * c:4 * c + 4].T)
        in_maps.append(m)
    res = run_bass_kernel_spmd(nc, in_maps, core_ids=list(range(NCORES)))
    R = res.results
    cat = lambda k: np.concatenate([np.asarray(r[k]) for r in R], axis=0)
    yp = cat("yp").reshape(16, SEQ, D)
    ys = cat("ys").reshape(32, 4, D)
    kp = cat("kp").reshape(1, 16, SEQ, 2, 64)
    vp = cat("vp").reshape(1, 16, SEQ, 2, 64)
    ip = cat("ip").reshape(1, 16, SEQ, 64)
    cpo = cat("cpo").reshape(16, 128, 4, 2)
    cp = np.ascontiguousarray(cpo.transpose(0, 3, 2, 1).reshape(1, 16, 2, 512))
    ks = cat("kso").reshape(1, 32, 4, 2, 64)
    vs = cat("vso").reshape(1, 32, 4, 2, 64)
    iss = cat("iso").reshape(1, 32, 4, 64)
    cso = np.stack([np.asarray(r["cso"]) for r in R], axis=0)
    cs = np.ascontiguousarray(cso.transpose(0, 3, 4, 2, 1).reshape(1, 32, 2, 512))
    f32 = lambda a: np.ascontiguousarray(a, dtype=np.float32)
    return tuple(f32(a) for a in (yp, ys, kp, vp, ip, cp, ks, vs, iss, cs))
```
